# Optimizing a Trainium2 kernel written in Bass

```python
import math
import jax, jax.numpy as jnp
from jax import lax
import numpy as np

D_MODEL = 2048
BATCH = 8
SEQ = 4096
DEPTH = 2
DEC_BATCH = 8
DEC_SEQ = 64
PAST_LEN = 1024

CHUNK = 64
N_MIXERS = 2
N_A_LAYERS = (DEPTH + 1) // 2
N_B_LAYERS = DEPTH // 2
EPS = 1e-6
NEG_INF = -1e30

A_HEADS = 16
A_HEAD_DIM = D_MODEL // A_HEADS
A_WIDTH = A_HEADS * A_HEAD_DIM
A_LEFT_CHUNKS = 8
A_WINDOW = A_LEFT_CHUNKS * CHUNK
A_BAND = A_WINDOW + CHUNK
REL_CLIP = 128
A_SCALE = A_HEAD_DIM ** -0.5

B_HEADS = 16
B_NOPE = 128
B_ROPE = 64
B_V = 128
B_QK = B_NOPE + B_ROPE
B_Q_RANK = D_MODEL // 4
B_KV_RANK = 512
B_WIDTH = B_HEADS * B_V
B_IN = B_Q_RANK + B_KV_RANK + B_ROPE + B_WIDTH
B_SCALE = B_QK ** -0.5
ROPE_THETA = 10000.0
Q_BLOCK = 128

kernel_name = 'hybrid_chunkband_mla_streaming_step'


def rmsnorm(x, g):
    xf = x.astype(jnp.float32)
    y = xf * lax.rsqrt(jnp.mean(xf * xf, axis=-1, keepdims=True) + EPS)
    return (y * g.astype(jnp.float32)).astype(x.dtype)


def rel_bias(table, q_pos, k_pos):
    idx = jnp.clip(q_pos[:, None] - k_pos[None, :], -REL_CLIP, REL_CLIP) + REL_CLIP
    return jnp.take(table, idx, axis=1).astype(jnp.float32)


def rope(x, pos):
    half = x.shape[-1] // 2
    inv = ROPE_THETA ** (-jnp.arange(half, dtype=jnp.float32) / half)
    ang = pos.astype(jnp.float32)[:, None] * inv[None, :]
    shape = (pos.shape[0],) + (1,) * (x.ndim - 3) + (half,)
    cos = jnp.cos(ang).reshape(shape)
    sin = jnp.sin(ang).reshape(shape)
    xf = x.astype(jnp.float32)
    x1, x2 = xf[..., :half], xf[..., half:]
    return jnp.concatenate([x1 * cos - x2 * sin, x1 * sin + x2 * cos], axis=-1).astype(x.dtype)


def a_project(h, w_in, g_q, g_k):
    b, s, _ = h.shape
    q, k, v, g = jnp.split(h @ w_in, 4, axis=-1)
    hd = (b, s, A_HEADS, A_HEAD_DIM)
    return rmsnorm(q.reshape(hd), g_q), rmsnorm(k.reshape(hd), g_k), v.reshape(hd), g


def a_prompt(x, ln, w_in, g_q, g_k, table, w_out):
    b, s, _ = x.shape
    nc = s // CHUNK
    q, k, v, g = a_project(rmsnorm(x, ln), w_in, g_q, g_k)
    pad = ((0, 0), (A_WINDOW, 0), (0, 0), (0, 0))
    kp = jnp.pad(k, pad)
    vp = jnp.pad(v, pad)
    qc = jnp.moveaxis(q.reshape(b, nc, CHUNK, A_HEADS, A_HEAD_DIM), 1, 0)
    band = jnp.arange(A_BAND)
    bias = rel_bias(table, jnp.arange(CHUNK) + A_WINDOW, band)

    def one_chunk(args):
        c, qb = args
        start = c * CHUNK
        kb = lax.dynamic_slice_in_dim(kp, start, A_BAND, axis=1)
        vb = lax.dynamic_slice_in_dim(vp, start, A_BAND, axis=1)
        sc = jnp.einsum('bqhd,bkhd->bhqk', qb, kb, preferred_element_type=jnp.float32) * A_SCALE + bias
        sc = jnp.where(start + band >= A_WINDOW, sc, NEG_INF)
        p = jax.nn.softmax(sc, axis=-1).astype(vb.dtype)
        return jnp.einsum('bhqk,bkhd->bqhd', p, vb)

    o = lax.map(one_chunk, (jnp.arange(nc), qc))
    o = jnp.moveaxis(o, 0, 1).reshape(b, s, A_WIDTH)
    y = x + (jax.nn.silu(g) * o) @ w_out
    keep = min(A_WINDOW, s)
    return y, k[:, s - keep:], v[:, s - keep:]


def a_sample(x, k_cache, v_cache, ln, w_in, g_q, g_k, table, w_out):
    b, s, _ = x.shape
    n_cache = k_cache.shape[1]
    q, k, v, g = a_project(rmsnorm(x, ln), w_in, g_q, g_k)
    kk = jnp.concatenate([k_cache.astype(k.dtype), k], axis=1)
    vv = jnp.concatenate([v_cache.astype(v.dtype), v], axis=1)
    q_pos = PAST_LEN + jnp.arange(s)
    k_pos = jnp.concatenate([PAST_LEN - n_cache + jnp.arange(n_cache), q_pos])
    bias = rel_bias(table, q_pos, k_pos)
    sc = jnp.einsum('bqhd,bkhd->bhqk', q, kk, preferred_element_type=jnp.float32) * A_SCALE + bias
    p = jax.nn.softmax(sc, axis=-1).astype(vv.dtype)
    o = jnp.einsum('bhqk,bkhd->bqhd', p, vv).reshape(b, s, A_WIDTH)
    y = x + (jax.nn.silu(g) * o) @ w_out
    return y, k, v


def b_project(h, pos, w_in, g_qa, w_uq, g_kva, g_qn, g_qr, g_kr):
    b, s, _ = h.shape
    cq, ckv, kr, g = jnp.split(h @ w_in, [B_Q_RANK, B_Q_RANK + B_KV_RANK, B_Q_RANK + B_KV_RANK + B_ROPE], axis=-1)
    q = (rmsnorm(cq, g_qa) @ w_uq).reshape(b, s, B_HEADS, B_QK)
    q_nope = rmsnorm(q[..., :B_NOPE], g_qn)
    q_rope = rope(rmsnorm(q[..., B_NOPE:], g_qr), pos)
    ckv = rmsnorm(ckv, g_kva)
    k_rope = rope(rmsnorm(kr, g_kr), pos)
    return q_nope, q_rope, ckv, k_rope, g


def b_expand(ckv, w_uk, w_uv, g_kn):
    b, s, _ = ckv.shape
    k_nope = rmsnorm((ckv @ w_uk).reshape(b, s, B_HEADS, B_NOPE), g_kn)
    v = (ckv @ w_uv).reshape(b, s, B_HEADS, B_V)
    return k_nope, v


def b_scores(q_nope, q_rope, k_nope, k_rope):
    sn = jnp.einsum('bqhd,bkhd->bhqk', q_nope, k_nope, preferred_element_type=jnp.float32)
    sr = jnp.einsum('bqhr,bkr->bhqk', q_rope, k_rope, preferred_element_type=jnp.float32)
    return (sn + sr) * B_SCALE


def b_prompt(x, ln, w_in, g_qa, w_uq, g_kva, w_uk, w_uv, g_qn, g_kn, g_qr, g_kr, w_out):
    b, s, _ = x.shape
    pos = jnp.arange(s)
    q_nope, q_rope, ckv, k_rope, g = b_project(rmsnorm(x, ln), pos, w_in, g_qa, w_uq, g_kva, g_qn, g_qr, g_kr)
    k_nope, v = b_expand(ckv, w_uk, w_uv, g_kn)
    nb = s // Q_BLOCK
    qn = jnp.moveaxis(q_nope.reshape(b, nb, Q_BLOCK, B_HEADS, B_NOPE), 1, 0)
    qr = jnp.moveaxis(q_rope.reshape(b, nb, Q_BLOCK, B_HEADS, B_ROPE), 1, 0)
    k_chunk = pos // CHUNK

    def one_block(args):
        blk, qnb, qrb = args
        q_chunk = (blk * Q_BLOCK + jnp.arange(Q_BLOCK)) // CHUNK
        sc = b_scores(qnb, qrb, k_nope, k_rope)
        sc = jnp.where(k_chunk[None, :] <= q_chunk[:, None], sc, NEG_INF)
        p = jax.nn.softmax(sc, axis=-1).astype(v.dtype)
        return jnp.einsum('bhqk,bkhd->bqhd', p, v)

    o = lax.map(one_block, (jnp.arange(nb), qn, qr))
    o = jnp.moveaxis(o, 0, 1).reshape(b, s, B_WIDTH)
    y = x + (jax.nn.silu(g) * o) @ w_out
    return y, ckv, k_rope


def b_sample(x, ckv_cache, kr_cache, ln, w_in, g_qa, w_uq, g_kva, w_uk, w_uv, g_qn, g_kn, g_qr, g_kr, w_out):
    b, s, _ = x.shape
    pos = PAST_LEN + jnp.arange(s)
    q_nope, q_rope, ckv, k_rope, g = b_project(rmsnorm(x, ln), pos, w_in, g_qa, w_uq, g_kva, g_qn, g_qr, g_kr)
    ckv_all = jnp.concatenate([ckv_cache.astype(ckv.dtype), ckv], axis=1)
    kr_all = jnp.concatenate([kr_cache.astype(k_rope.dtype), k_rope], axis=1)
    k_nope, v = b_expand(ckv_all, w_uk, w_uv, g_kn)
    sc = b_scores(q_nope, q_rope, k_nope, kr_all)
    p = jax.nn.softmax(sc, axis=-1).astype(v.dtype)
    o = jnp.einsum('bhqk,bkhd->bqhd', p, v).reshape(b, s, B_WIDTH)
    y = x + (jax.nn.silu(g) * o) @ w_out
    return y, ckv, k_rope


def setup_inputs(seed: int = 0) -> dict:
    key = jax.random.key(seed)
    ks = jax.random.split(key, 26)
    f32 = jnp.float32

    def nrm(k, shape, scale):
        return jax.random.normal(k, shape, f32) * scale

    def gain(k, shape):
        return 1.0 + 0.01 * jax.random.normal(k, shape, f32)

    a_cache = min(A_WINDOW, PAST_LEN)
    na, nb = N_A_LAYERS, N_B_LAYERS
    return {
        'x_prompt': nrm(ks[0], (BATCH, SEQ, D_MODEL), 1.0),
        'x_sample': nrm(ks[1], (DEC_BATCH, DEC_SEQ, D_MODEL), 1.0),
        'cache_a_k': nrm(ks[2], (na, DEC_BATCH, a_cache, A_HEADS, A_HEAD_DIM), 1.0),
        'cache_a_v': nrm(ks[3], (na, DEC_BATCH, a_cache, A_HEADS, A_HEAD_DIM), 1.0),
        'cache_b_ckv': nrm(ks[4], (nb, DEC_BATCH, PAST_LEN, B_KV_RANK), 1.0),
        'cache_b_krope': nrm(ks[5], (nb, DEC_BATCH, PAST_LEN, B_ROPE), 1.0),
        'a_ln': gain(ks[6], (na, D_MODEL)),
        'w_a_in': nrm(ks[7], (na, D_MODEL, 4 * A_WIDTH), D_MODEL ** -0.5),
        'a_q_norm': gain(ks[8], (na, A_HEAD_DIM)),
        'a_k_norm': gain(ks[9], (na, A_HEAD_DIM)),
        'a_rel_bias': nrm(ks[10], (na, A_HEADS, 2 * REL_CLIP + 1), 0.2),
        'w_a_out': nrm(ks[11], (na, A_WIDTH, D_MODEL), A_WIDTH ** -0.5),
        'b_ln': gain(ks[12], (nb, D_MODEL)),
        'w_b_in': nrm(ks[13], (nb, D_MODEL, B_IN), D_MODEL ** -0.5),
        'b_q_a_norm': gain(ks[14], (nb, B_Q_RANK)),
        'w_b_uq': nrm(ks[15], (nb, B_Q_RANK, B_HEADS * B_QK), B_Q_RANK ** -0.5),
        'b_kv_a_norm': gain(ks[16], (nb, B_KV_RANK)),
        'w_b_uk': nrm(ks[17], (nb, B_KV_RANK, B_HEADS * B_NOPE), B_KV_RANK ** -0.5),
        'w_b_uv': nrm(ks[18], (nb, B_KV_RANK, B_HEADS * B_V), B_KV_RANK ** -0.5),
        'b_q_nope_norm': gain(ks[19], (nb, B_NOPE)),
        'b_k_nope_norm': gain(ks[20], (nb, B_NOPE)),
        'b_q_rope_norm': gain(ks[21], (nb, B_ROPE)),
        'b_k_rope_norm': gain(ks[22], (nb, B_ROPE)),
        'w_b_out': nrm(ks[23], (nb, B_WIDTH, D_MODEL), B_WIDTH ** -0.5),
    }


def reference(x_prompt, x_sample, cache_a_k, cache_a_v, cache_b_ckv, cache_b_krope,
              a_ln, w_a_in, a_q_norm, a_k_norm, a_rel_bias, w_a_out,
              b_ln, w_b_in, b_q_a_norm, w_b_uq, b_kv_a_norm, w_b_uk, w_b_uv,
              b_q_nope_norm, b_k_nope_norm, b_q_rope_norm, b_k_rope_norm, w_b_out):
    yp, ys = x_prompt, x_sample
    akp, avp, aks, avs = [], [], [], []
    bcp, brp, bcs, brs = [], [], [], []
    for layer in range(DEPTH):
        i = layer // N_MIXERS
        if layer % N_MIXERS == 0:
            yp, k_p, v_p = a_prompt(yp, a_ln[i], w_a_in[i], a_q_norm[i], a_k_norm[i], a_rel_bias[i], w_a_out[i])
            ys, k_s, v_s = a_sample(ys, cache_a_k[i], cache_a_v[i], a_ln[i], w_a_in[i], a_q_norm[i],
                                    a_k_norm[i], a_rel_bias[i], w_a_out[i])
            akp.append(k_p); avp.append(v_p); aks.append(k_s); avs.append(v_s)
        else:
            yp, c_p, r_p = b_prompt(yp, b_ln[i], w_b_in[i], b_q_a_norm[i], w_b_uq[i], b_kv_a_norm[i],
                                    w_b_uk[i], w_b_uv[i], b_q_nope_norm[i], b_k_nope_norm[i],
                                    b_q_rope_norm[i], b_k_rope_norm[i], w_b_out[i])
            ys, c_s, r_s = b_sample(ys, cache_b_ckv[i], cache_b_krope[i], b_ln[i], w_b_in[i], b_q_a_norm[i],
                                    w_b_uq[i], b_kv_a_norm[i], w_b_uk[i], w_b_uv[i], b_q_nope_norm[i],
                                    b_k_nope_norm[i], b_q_rope_norm[i], b_k_rope_norm[i], w_b_out[i])
            bcp.append(c_p); brp.append(r_p); bcs.append(c_s); brs.append(r_s)
    return (yp, ys,
            jnp.stack(akp), jnp.stack(avp), jnp.stack(bcp), jnp.stack(brp),
            jnp.stack(aks), jnp.stack(avs), jnp.stack(bcs), jnp.stack(brs))
```

```python
import numpy as np
from contextlib import ExitStack
import concourse.bass as bass
import concourse.mybir as mybir
from concourse.bass_utils import run_bass_kernel_spmd

F32 = mybir.dt.float32
BF16 = mybir.dt.bfloat16
I32 = mybir.dt.int32
AF = mybir.ActivationFunctionType
ALU = mybir.AluOpType

N_CORES = 8
T = 4096
D = 2048
NS = 64
PAST = 1024
EPS = 1e-6
A_SCALE = 128 ** -0.5
B_SCALE = 192 ** -0.5
BIN = 3136

ENGS = ("pe", "act", "dve", "pool", "sp")
CH = 30000
N_DMA_SEMS = 40
import os as _os0
ROPE_ENG = _os0.environ.get("DEV_ROPE_ENG", "dve")
EB_ENG = _os0.environ.get("DEV_EB_ENG", "pool")


class Res:
    __slots__ = ("w", "r", "excl")

    def __init__(self, excl=False):
        self.w = None
        self.r = []
        self.excl = excl


class Op:
    __slots__ = ("eng", "fn", "reads", "writes", "dma", "seq", "deps", "needs_inc", "inc_idx", "dsem", "dval")

    def __init__(self, eng, fn, reads, writes, dma):
        self.eng = eng
        self.fn = fn
        self.reads = reads
        self.writes = writes
        self.dma = dma
        self.deps = []
        self.needs_inc = False


class Sched:
    def __init__(self, nc, stack):
        self.nc = nc
        self.stack = stack
        self.e = {"pe": nc.tensor, "act": nc.scalar, "dve": nc.vector, "pool": nc.gpsimd, "sp": nc.sync}
        self.ops = []
        self.seqc = {e: 0 for e in ENGS}
        self.incc = {e: 0 for e in ENGS}
        self.waited = {e: {e2: -1 for e2 in ENGS} for e in ENGS}
        self.esems = {e: [] for e in ENGS}
        self.dsems = [stack.enter_context(nc.semaphore(f"sdma{i}")) for i in range(N_DMA_SEMS)]
        self.dcount = [0] * N_DMA_SEMS
        self.n_dma = 0
        self.tot_ops = 0
        self.tot_waits = 0

    def op(self, eng, fn, reads=(), writes=()):
        self.ops.append(Op(eng, fn, tuple(reads), tuple(writes), False))

    def dma(self, eng, fn, reads=(), writes=()):
        self.ops.append(Op(eng, fn, tuple(reads), tuple(writes), True))

    def _esem(self, e, idx):
        k = idx // CH
        while len(self.esems[e]) <= k:
            self.esems[e].append(self.stack.enter_context(self.nc.semaphore(f"s{e}{len(self.esems[e])}")))
        return self.esems[e][k], idx % CH + 1

    def flush(self):
        ops = self.ops
        self.ops = []
        waited = self.waited
        waited_dma = {e: set() for e in ENGS}
        ring = []
        allres = set()
        last_on = {}
        for o in ops:
            o.seq = self.seqc[o.eng]
            self.seqc[o.eng] += 1
            cand = []
            for r in o.reads:
                allres.add(r)
                if r.w is not None:
                    cand.append(r.w)
                if r.excl:
                    for d in r.r:
                        if d.eng != o.eng:
                            cand.append(d)
            for r in o.writes:
                allres.add(r)
                if r.w is not None:
                    cand.append(r.w)
                cand.extend(r.r)
            if o.dma:
                o.dsem = self.n_dma % N_DMA_SEMS
                self.n_dma += 1
                if len(ring) >= N_DMA_SEMS:
                    cand.append(ring[len(ring) - N_DMA_SEMS])
                ring.append(o)
            else:
                last_on[o.eng] = o
            best = {}
            for d in cand:
                if d is o:
                    continue
                if d.dma:
                    if id(d) not in waited_dma[o.eng]:
                        waited_dma[o.eng].add(id(d))
                        o.deps.append(d)
                else:
                    if d.eng == "pe" and o.eng == "pe" and not o.dma:
                        continue
                    if d.seq > waited[o.eng][d.eng]:
                        if d.eng not in best or d.seq > best[d.eng].seq:
                            best[d.eng] = d
            for e2, d in best.items():
                waited[o.eng][e2] = d.seq
                d.needs_inc = True
                o.deps.append(d)
            for r in o.reads:
                r.r.append(o)
            for r in o.writes:
                r.w = o
                r.r = []
        for o in last_on.values():
            o.needs_inc = True
        for o in ops:
            if o.dma:
                self.dcount[o.dsem] += 16
                o.dval = self.dcount[o.dsem]
            elif o.needs_inc:
                o.inc_idx = self.incc[o.eng]
                self.incc[o.eng] += 1
        for o in ops:
            eng = self.e[o.eng]
            for d in o.deps:
                if d.dma:
                    eng.wait_ge(self.dsems[d.dsem], d.dval)
                else:
                    s, v = self._esem(d.eng, d.inc_idx)
                    eng.wait_ge(s, v)
                self.tot_waits += 1
            ins = o.fn(eng)
            if o.dma:
                ins.then_inc(self.dsems[o.dsem], 16)
            elif o.needs_inc:
                s, v = self._esem(o.eng, o.inc_idx)
                ins.then_inc(s, 1)
        self.tot_ops += len(ops)
        for e in ENGS:
            eng = self.e[e]
            for e2 in ENGS:
                if e2 != e and self.incc[e2] > 0:
                    s, v = self._esem(e2, self.incc[e2] - 1)
                    eng.wait_ge(s, v)
            for i in range(N_DMA_SEMS):
                if self.dcount[i] > 0:
                    eng.wait_ge(self.dsems[i], self.dcount[i])
        for e in ENGS:
            for e2 in ENGS:
                waited[e][e2] = self.seqc[e2] - 1
        for r in allres:
            r.w = None
            r.r = []


class StopBuild(Exception):
    pass


import os as _os
_STOP = _os.environ.get("DEV_STOP", "")


class Builder:
    def stopat(self, tag):
        if _STOP == tag:
            raise StopBuild()

    def __init__(self, phases="all"):
        self.phases = phases
        self.nc = bass.Bass("TRN2", target_bir_lowering=False)
        self.gst = ExitStack()

    def din(self, name, shape, dt=F32):
        return self.nc.dram_tensor(name, list(shape), dt, kind="ExternalInput").ap()

    def dout(self, name, shape, dt=F32):
        return self.nc.dram_tensor(name, list(shape), dt, kind="ExternalOutput").ap()

    def dscr(self, name, shape, dt):
        return self.nc.dram_tensor(name, list(shape), dt, kind="Internal").ap()

    def sb(self, st, name, shape, dt):
        self._uid = getattr(self, "_uid", 0) + 1
        return st.enter_context(self.nc.sbuf_tensor(f"{name}_{self._uid}", list(shape), dt))

    def ps(self, st, name):
        return st.enter_context(self.nc.psum_tensor(name, [128, 512], F32))

    def mm(self, out, lhsT, rhs, start, stop, R, W):
        self.S.op("pe", lambda e: e.matmul(out, lhsT=lhsT, rhs=rhs, start=start, stop=stop,
                                           skip_group_check=True), R, W)

    def tr(self, out, in_, ident, R, W):
        self.S.op("pe", lambda e: e.transpose(out=out, in_=in_, identity=ident), R, W)

    def act(self, out, in_, func, R, W, bias=None, scale=None, accum=None):
        kw = {}
        if bias is not None:
            kw["bias"] = bias
        if scale is not None:
            kw["scale"] = scale
        if accum is not None:
            kw["accum_out"] = accum
        self.S.op("act", lambda e: e.activation(out=out, in_=in_, func=func, **kw), R, W)

    def acopy(self, out, in_, R, W):
        self.S.op("act", lambda e: e.activation(out=out, in_=in_, func=AF.Identity), R, W)

    def vcopy(self, out, in_, R, W, eng="dve"):
        self.S.op(eng, lambda e: e.tensor_copy(out=out, in_=in_), R, W)

    def tt(self, out, in0, in1, op, R, W, eng="dve"):
        self.S.op(eng, lambda e: e.tensor_tensor(out=out, in0=in0, in1=in1, op=op), R, W)

    def ts(self, out, in0, s1, s2, op0, op1, R, W, eng="dve"):
        if s2 is None:
            self.S.op(eng, lambda e: e.tensor_scalar(out=out, in0=in0, scalar1=s1, scalar2=None, op0=op0), R, W)
        else:
            self.S.op(eng, lambda e: e.tensor_scalar(out=out, in0=in0, scalar1=s1, scalar2=s2, op0=op0, op1=op1), R, W)

    def stt(self, out, in0, scalar, in1, op0, op1, R, W, eng="dve"):
        self.S.op(eng, lambda e: e.scalar_tensor_tensor(out=out, in0=in0, scalar=scalar, in1=in1, op0=op0, op1=op1), R, W)

    def memset(self, ap, val, R, W, eng="pool"):
        self.S.op(eng, lambda e: e.memset(ap, val), R, W)

    def dma(self, out, in_, R, W, q="sp", slow=False):
        if slow:
            self.S.dma(q, lambda e: e.dma_start(out=out, in_=in_, allow_slow_non_contiguous=True), R, W)
        else:
            self.S.dma(q, lambda e: e.dma_start(out=out, in_=in_), R, W)

    def rstd(self, ss, n, inv_d, r_ss):
        k = self.nti % 4
        self.nti += 1
        y = self.nt_y[:, 16 * k:16 * k + n]
        t = self.nt_t[:, 16 * k:16 * k + n]
        r = self.r_nt[k]
        x = ss
        rl = list(r_ss) if isinstance(r_ss, (list, tuple)) else [r_ss]
        self.ts(x, x, inv_d, EPS, ALU.mult, ALU.add, rl, rl)
        self.ts(y.bitcast(I32), x.bitcast(I32), 1, None, ALU.arith_shift_right, None, rl, [r])
        self.ts(y.bitcast(I32), y.bitcast(I32), -1, 0x5F3759DF, ALU.mult, ALU.add, [r], [r])
        for _ in range(2):
            self.stt(t, y, -0.5, y, ALU.mult, ALU.mult, [r], [r])
            self.tt(t, t, x, ALU.mult, [r] + rl, [r])
            self.stt(y, t, 1.5, y, ALU.add, ALU.mult, [r], [r])
        return y, r

    def build(self):
        nc = self.nc
        gst = self.gst
        S = self.S = Sched(nc, gst)
        ph = self.phases

        x_p = self.din("x_prompt", [T, D])
        x_s = self.din("x_sample", [NS, D])
        c_ak = self.din("cache_a_k", [512, D])
        c_av = self.din("cache_a_v", [512, D])
        c_bc = self.din("cache_b_ckv", [PAST, 512])
        c_br = self.din("cache_b_krope", [PAST, 64])
        a_ln = self.din("a_ln", [D])
        w_a_in = self.din("w_a_in", [D, 4 * D])
        a_qn = self.din("a_q_norm", [128])
        a_kn = self.din("a_k_norm", [128])
        a_rb = self.din("a_rel_bias", [16, 257])
        w_a_out = self.din("w_a_out", [D, D])
        b_ln = self.din("b_ln", [D])
        w_b_in = self.din("w_b_in", [D, BIN])
        b_qa = self.din("b_q_a_norm", [512])
        w_b_uq = self.din("w_b_uq", [512, 3072])
        b_kva = self.din("b_kv_a_norm", [512])
        w_b_uk = self.din("w_b_uk", [512, D])
        w_b_uv = self.din("w_b_uv", [512, D])
        b_qnn = self.din("b_q_nope_norm", [128])
        b_knn = self.din("b_k_nope_norm", [128])
        b_qrn = self.din("b_q_rope_norm", [64])
        b_krn = self.din("b_k_rope_norm", [64])
        w_b_out = self.din("w_b_out", [D, D])
        rope_cs = self.din("rope_cs", [T + 128, 64])

        y_p = self.dout("y_prompt", [T, D])
        y_s = self.dout("y_sample", [NS, D])
        o_akp = self.dout("new_a_k_prompt", [512, D])
        o_avp = self.dout("new_a_v_prompt", [512, D])
        o_bcp = self.dout("new_b_ckv_prompt", [T, 512])
        o_brp = self.dout("new_b_krope_prompt", [T, 64])
        o_aks = self.dout("new_a_k_sample", [NS, D])
        o_avs = self.dout("new_a_v_sample", [NS, D])
        o_bcs = self.dout("new_b_ckv_sample", [NS, 512])
        o_brs = self.dout("new_b_krope_sample", [NS, 64])

        TT = T + 128
        wa_in_bf = self.dscr("wa_in_bf", [16, 128, 16, 512], BF16)
        wa_out_bf = self.dscr("wa_out_bf", [4, 128, 16, 512], BF16)
        wb_in_bf = self.dscr("wb_in_bf", [7, 128, 16, 512], BF16)
        wb_out_bf = self.dscr("wb_out_bf", [4, 128, 16, 512], BF16)
        y1 = self.dscr("y1", [TT, D], F32)
        self.dbgA = (ph == "A")
        sgT_d = self.dscr("sgT_d", [D, TT], BF16)
        ogT_d = self.dscr("ogT_d", [D, TT], BF16)
        ext_d = self.dscr("ext_d", [16, 384], F32)
        self.wres = {}
        self.wdone = set()

        ident_bf = self.sb(gst, "ident_bf", [128, 128], BF16)
        ident_f = self.sb(gst, "ident_f", [128, 128], F32)
        ones_bf = self.sb(gst, "ones_bf", [128, 128], BF16)
        self.nt_y = self.sb(gst, "nt_y", [128, 64], F32)
        self.nt_t = self.sb(gst, "nt_t", [128, 64], F32)
        self.r_nt = [Res() for _ in range(4)]
        self.nti = 0
        r_c = Res()
        gcolA = self.sb(gst, "gcolA", [128, 16], F32)
        gcolB = self.sb(gst, "gcolB", [128, 16], F32)
        gq_col = self.sb(gst, "gq_col", [128, 1], F32)
        gk_col = self.sb(gst, "gk_col", [128, 1], F32)
        gk_rep = self.sb(gst, "gk_rep", [128, 128], F32)
        gqn_col = self.sb(gst, "gqn_col", [128, 1], F32)
        gkn_col = self.sb(gst, "gkn_col", [128, 1], F32)
        gqa_rep = self.sb(gst, "gqa_rep", [128, 512], F32)
        gkva_rep = self.sb(gst, "gkva_rep", [128, 512], F32)
        gqr_rep = self.sb(gst, "gqr_rep", [128, 64], F32)
        gkr_rep = self.sb(gst, "gkr_rep", [128, 64], F32)
        cb = self.sb(gst, "cb", [128, 16], F32)
        self.stEB = ExitStack()
        EB = self.sb(self.stEB, "EB", [128, 16, 2, 128], BF16)

        PJ = [self.ps(gst, f"pj{i}") for i in range(2)]
        TRB = [self.ps(gst, f"trb{i}") for i in range(2)]
        STB = [self.ps(gst, f"st{i}") for i in range(2)]
        OTB = self.ps(gst, "otb")
        SUMB = self.ps(gst, "sumb")
        r_PJ = [Res(True), Res(True)]
        r_TRB = [Res(True), Res(True)]
        r_ST = [Res(True), Res(True)]
        r_OT = Res(True)
        r_SUM = Res(True)
        self.pji = 0
        self.tri = 0

        self.pj_pool = [(PJ[0], r_PJ[0]), (PJ[1], r_PJ[1])]
        pool2 = list(self.pj_pool)
        pool6 = pool2 + [(STB[0], r_ST[0]), (STB[1], r_ST[1]), (OTB, r_OT), (SUMB, r_SUM)]

        def next_pj():
            i = self.pji % len(self.pj_pool)
            self.pji += 1
            return self.pj_pool[i]

        def next_tr():
            i = self.tri
            self.tri ^= 1
            return TRB[i], r_TRB[i]

        with ExitStack() as st:
            J = self.sb(st, "J", [128, 128], F32)
            tmp16 = self.sb(st, "tmp16", [16, 128], F32)
            tmp16b = self.sb(st, "tmp16b", [16, 128], F32)
            e_sb = self.sb(st, "e_sb", [16, 384], F32)
            hk = self.sb(st, "hk", [128, 16, 2, 128], F32)
            ebf = self.sb(st, "ebf", [128, 128], F32)
            cbrow = self.sb(st, "cbrow", [16, 1], F32)
            diagc = self.sb(st, "diagc", [16, 16], F32)
            ones_f = self.sb(st, "ones_f", [16, 128], F32)
            cneg = self.sb(st, "cneg", [128, 16], F32)
            r_J, r_t16, r_esb, r_hk, r_ebf, r_cbrow, r_ext = Res(), Res(), Res(), Res(), Res(), Res(), Res()
            self.memset(ident_f[:], 0.0, [], [r_c])
            S.op("pool", lambda e: e.affine_select(out=ident_f[:], in_=ident_f[:], pattern=[[-1, 128]],
                                                   compare_op=ALU.not_equal, fill=1.0, base=0, channel_multiplier=1), [r_c], [r_c])
            self.vcopy(ident_bf[:], ident_f[:], [r_c], [r_c])
            self.memset(ones_bf[:], 1.0, [], [r_c])
            self.memset(ones_f[:], 1.0, [], [r_cbrow])
            self.memset(J[:], 0.0, [], [r_J])
            S.op("pool", lambda e: e.affine_select(out=J[:], in_=J[:], pattern=[[1, 128]],
                                                   compare_op=ALU.not_equal, fill=1.0, base=-127, channel_multiplier=1), [r_J], [r_J])
            for (src, dst) in ((a_ln, gcolA), (b_ln, gcolB)):
                tb = tmp16 if dst is gcolA else tmp16b
                self.dma(tb[:], src.rearrange("(kt p) -> kt p", p=128), [], [r_t16])
                pj, rpj = next_pj()
                self.tr(pj[:, 0:16], tb[:], ident_f[0:16, 0:16], [r_t16, r_c], [rpj])
                self.acopy(dst[:], pj[:, 0:16], [rpj], [r_c])
            for (src, dst) in ((a_qn, gq_col), (a_kn, gk_col), (b_qnn, gqn_col), (b_knn, gkn_col)):
                self.dma(dst[:], src.rearrange("(p o) -> p o", o=1), [], [r_c], slow=True)
            self.ts(gq_col[:], gq_col[:], A_SCALE, None, ALU.mult, None, [r_c], [r_c])
            for (src, dst, n) in ((a_kn, gk_rep, 128), (b_qa, gqa_rep, 512), (b_kva, gkva_rep, 512),
                                  (b_qrn, gqr_rep, 64), (b_krn, gkr_rep, 64)):
                self.dma(dst[:], src.partition_broadcast(128), [], [r_c])
            self.dma(e_sb[:, 0:257], a_rb, [], [r_esb])
            self.vcopy(e_sb[:, 257:384], e_sb[:, 256:257].to_broadcast([16, 127]), [r_esb], [r_esb])
            self.dma(ext_d, e_sb[:], [r_esb], [r_ext])
            tab_t = a_rb.tensor
            ext_t = ext_d.tensor
            for h in range(16):
                self.dma(hk[:, h, 0, :], bass.AP(ext_t, h * 384 + 129, [[1, 128], [1, 128]]), [r_ext], [r_hk])
                self.dma(hk[:, h, 1, :], bass.AP(ext_t, h * 384 + 1, [[1, 128], [1, 128]]), [r_ext], [r_hk])
            self.dma(cbrow[:], bass.AP(tab_t, 256, [[257, 16], [1, 1]]), [], [r_cbrow], slow=True)
            self.ts(diagc[:], ident_f[0:16, 0:16], cbrow[:, 0:1], None, ALU.mult, None, [r_c, r_cbrow], [r_cbrow])
            pj, rpj = next_pj()
            self.mm(pj[:, 0:16], ones_f[:, :], diagc[:, :], True, True, [r_cbrow], [rpj])
            self.acopy(cb[:], pj[:, 0:16], [rpj], [r_c])
            self.ts(cneg[:], cb[:], -1.0, None, ALU.mult, None, [r_c], [r_cbrow])
            for h in range(16):
                for t in range(2):
                    self.act(ebf[:], hk[:, h, t, :], AF.Exp, [r_hk, r_cbrow], [r_ebf], bias=cneg[:, h:h + 1])
                    pj, rpj = next_pj()
                    self.mm(pj[:, 0:128], J[:], ebf[:], True, True, [r_J, r_ebf], [rpj])
                    self.vcopy(EB[:, h, t, :], pj[:, 0:128], [rpj], [r_c])
            self.memset(EB[64:128, :, 1, 0:64], 0.0, [r_c], [r_c], eng="dve")
            S.flush()

        if ph == "setup":
            self._finish(S)
            return nc

        def wsrc(wid, j):
            if wid == "a_in":
                return w_a_in[:, 512 * j:512 * j + 512].rearrange("(kt p) c -> p kt c", p=128), 512
            if wid == "a_out":
                return w_a_out[:, 512 * j:512 * j + 512].rearrange("(kt p) c -> p kt c", p=128), 512
            if wid == "b_out":
                return w_b_out[:, 512 * j:512 * j + 512].rearrange("(kt p) c -> p kt c", p=128), 512
            if wid == "b_in":
                if j < 2:
                    return w_b_in[:, 512 * j:512 * j + 512].rearrange("(kt p) c -> p kt c", p=128), 512
                if j == 2:
                    return w_b_in[:, 1024:1088].rearrange("(kt p) c -> p kt c", p=128), 64
                c0 = 1088 + 512 * (j - 3)
                return w_b_in[:, c0:c0 + 512].rearrange("(kt p) c -> p kt c", p=128), 512
            raise ValueError(wid)

        wscr = {"a_in": wa_in_bf, "a_out": wa_out_bf, "b_in": wb_in_bf, "b_out": wb_out_bf}

        class WStream:
            def __init__(s2, bufs, rbufs):
                s2.bufs = bufs
                s2.rb = rbufs
                s2.i = 0
                s2.q = []

            def prefetch(s2, wid, j):
                k = s2.i
                s2.i ^= 1
                buf, rb = s2.bufs[k], s2.rb[k]
                key = (wid, j)
                if key not in self.wres:
                    self.wres[key] = Res()
                rw = self.wres[key]
                src, ncol = wsrc(wid, j)
                if key in self.wdone:
                    self.dma(buf[:, :, 0:ncol], wscr[wid][j, :, :, 0:ncol], [rw], [rb])
                else:
                    self.dma(buf[:, :, 0:ncol], src, [], [rb], q="pool")
                    self.dma(wscr[wid][j, :, :, 0:ncol], buf[:, :, 0:ncol], [rb], [rw], q="pool")
                    self.wdone.add(key)
                s2.q.append((buf, rb))

            def pop(s2):
                return s2.q.pop(0)

        def prep_load(st_bufs, src, nreal):
            xin, r_xin, xns, r_xns, ss1, r_ss1 = st_bufs
            if nreal < 128:
                self.memset(xin[:], 0.0, [], [r_xin], eng="dve")
            self.dma(xin[0:nreal, :], src, [], [r_xin])

        def prep_norm(st_bufs):
            xin, r_xin, xns, r_xns, ss1, r_ss1 = st_bufs
            k = self.xni % 2
            self.xni += 1
            xn, r_xn = xns[k], r_xns[k]
            self.act(xn[:], xin[:], AF.Square, [r_xin], [r_xn, r_ss1], accum=ss1[:, 0:1])
            ry_, rr_ = self.rstd(ss1[:, 0:1], 1, 1.0 / D, r_ss1)
            self.act(xn[:], xin[:], AF.Identity, [r_xin, rr_], [r_xn], scale=ry_[:, 0:1])
            return xn, r_xn

        def prep_stage1(st_bufs, src, nreal):
            prep_load(st_bufs, src, nreal)
            return prep_norm(st_bufs)

        def prep_stage2(xn, r_xn, gcol, hT, col0, rblk):
            for half in range(2):
                trb, rtr = next_tr()
                tb = trb[:].bitcast(BF16)
                for j in range(8):
                    kt = half * 8 + j
                    self.tr(tb[:, j * 128:(j + 1) * 128], xn[:, kt * 128:(kt + 1) * 128], ident_bf[:], [r_xn, r_c], [rtr])
                self.tt(hT[:, half * 8:half * 8 + 8, col0:col0 + 128],
                        tb[:, 0:1024].rearrange("p (a b) -> p a b", a=8),
                        gcol[:, half * 8:half * 8 + 8].unsqueeze(2).to_broadcast([128, 8, 128]),
                        ALU.mult, [rtr, r_c], [rblk])

        def prep_hT(st_bufs, blocks, gcol, hT, r_hT):
            for (src, nreal, col0, rblk) in blocks:
                xn, r_xn = prep_stage1(st_bufs, src, nreal)
                prep_stage2(xn, r_xn, gcol, hT, col0, rblk)

        def attention(nq, tiles, qparts, kfn, vfn, escale, ebias, post, sgT_ap, r_sg, out_ap, r_out, rs_bufs, after_epi=None):
            PTs, r_PTs, rs, r_rs, tmp, r_tmp = rs_bufs
            n = len(tiles)
            pend = None
            if self.acci % 2 == 0:
                OTB_, r_OT_, SUMB_, r_SUM_ = PJ[0], r_PJ[0], PJ[1], r_PJ[1]
            else:
                OTB_, r_OT_, SUMB_, r_SUM_ = OTB, r_OT, SUMB, r_SUM
            self.acci += 1
            for i, (tid, c0, c1) in enumerate(tiles):
                stb, rst = st_pool[self.sti % 4]
                self.sti += 1
                kparts = kfn(tid)
                for pi, ((kap, rk), (qap, rq)) in enumerate(zip(kparts, qparts)):
                    self.mm(stb[:, c0:c1], kap, qap[:, c0:c1], pi == 0, pi == len(kparts) - 1, [rk, rq], [rst])
                pt, rpt = PTs[self.pti % 3], r_PTs[self.pti % 3]
                self.pti += 1
                kw = {}
                self.act(pt[:, c0:c1], stb[:, c0:c1], AF.Exp, [rst] + ([r_c] if ebias is not None else []), [rpt],
                         bias=ebias, scale=escale)
                post(tid, pt, rpt, c0, c1)
                if pend is not None:
                    pend()
                if i == min(2, n - 1) and self.pend_epi is not None:
                    self.pend_epi()
                    self.pend_epi = None
                vap, rv = vfn(tid)

                def pv(i=i, pt=pt, rpt=rpt, c0=c0, c1=c1, vap=vap, rv=rv):
                    self.mm(OTB_[:, c0:c1], vap, pt[:, c0:c1], i == 0, i == n - 1, [rv, rpt], [r_OT_])
                    self.mm(SUMB_[:, c0:c1], ones_bf[:], pt[:, c0:c1], i == 0, i == n - 1, [r_c, rpt], [r_SUM_])
                pend = pv
            pend()

            def epi():
                S.op("dve", lambda e: e.reciprocal(out=rs[:, 0:nq], in_=SUMB_[:, 0:nq]), [r_SUM_], [r_rs])
                self.stt(rs[:, 0:nq], OTB_[:, 0:nq], 0.5, rs[:, 0:nq], ALU.mult, ALU.mult, [r_OT_, r_rs], [r_rs])
                self.tt(out_ap, rs[:, 0:nq], sgT_ap, ALU.mult, [r_rs, r_sg], [r_out])
                if after_epi is not None:
                    after_epi()
            if self.pend_epi is not None:
                self.pend_epi()
            self.pend_epi = epi

        def flush_epi():
            if self.pend_epi is not None:
                self.pend_epi()
                self.pend_epi = None

        self.pti = 0
        self.gi = 0
        self.acci = 0
        self.sti = 0
        st_pool = [(STB[0], r_ST[0]), (STB[1], r_ST[1]), (TRB[0], r_TRB[0]), (TRB[1], r_TRB[1])]
        self.pend_epi = None
        self.xni = 0

        def sigmoid_gate(pj, rpj, nq, out_ap, r_out, gbufs):
            k = self.gi % 2
            self.gi += 1
            ge, r_ge, gc, r_gc = gbufs[0][k], gbufs[1][k], gbufs[2][k], gbufs[3][k]
            self.act(ge[:, 0:nq], pj[:, 0:nq], AF.Tanh, [rpj], [r_ge], scale=0.5)
            self.act(gc[:, 0:nq], pj[:, 0:nq], AF.Identity, [rpj], [r_gc])
            self.stt(out_ap, ge[:, 0:nq], 1.0, gc[:, 0:nq], ALU.add, ALU.mult, [r_ge, r_gc], [r_out])

        def out_proj(ws, wid, actT, r_act, nb, res_src, y_dst, nreal_last, obufs, after_chunk=None):
            ytile, r_y, xre, r_x = obufs
            ws.prefetch(wid, 0)
            k = 0
            for c in range(4):
                if c + 1 < 4:
                    ws.prefetch(wid, c + 1)
                wc, rwc = ws.pop()
                for b in range(nb):
                    nr = nreal_last if b == nb - 1 else 128
                    pj, rpj = next_pj()
                    for kt in range(16):
                        self.mm(pj[:, :], actT[:, kt, 128 * b:128 * b + 128], wc[:, kt, :], kt == 0, kt == 15,
                                [r_act, rwc], [rpj])
                    yt, ry, xr, rx = ytile[k % 2], r_y[k % 2], xre[k % 2], r_x[k % 2]
                    k += 1
                    self.dma(xr[0:nr, :], res_src[128 * b:128 * b + nr, 512 * c:512 * c + 512], [], [rx])
                    self.tt(yt[0:nr, :], pj[0:nr, :], xr[0:nr, :], ALU.add, [rpj, rx], [ry])
                    self.dma(y_dst[128 * b:128 * b + nr, 512 * c:512 * c + 512], yt[0:nr, :], [ry], [], q="pool")
                if after_chunk is not None:
                    after_chunk(c)

        with ExitStack() as st:
            hT = self.sb(st, "hT", [128, 16, 512], BF16)
            r_hT = [Res() for _ in range(4)]
            Wc = [self.sb(st, f"wc{i}", [128, 16, 512], BF16) for i in range(2)]
            ws = WStream(Wc, [Res(), Res()])
            xin = self.sb(st, "xin", [128, D], F32)
            xn = [self.sb(st, f"xn{i}", [128, D], BF16) for i in range(2)]
            ss1 = self.sb(st, "ss1", [128, 1], F32)
            hbufs = (xin, Res(), xn, [Res(), Res()], ss1, Res())
            qT = [self.sb(st, f"qT{i}", [128, 4, 512], BF16) for i in range(2)]
            r_qT = [[Res() for _ in range(4)] for _ in range(2)]
            sgT = [self.sb(st, f"sgT{i}", [128, 4, 512], BF16) for i in range(2)]
            r_sgT = [[Res() for _ in range(4)] for _ in range(2)]
            kTr = self.sb(st, "kTr", [128, 16, 1024], BF16)
            r_kT = [[Res() for _ in range(2)] for _ in range(16)]
            Vr = self.sb(st, "Vr", [128, 8, D], BF16)
            r_V = [[Res() for _ in range(4)] for _ in range(8)]
            PTs = [self.sb(st, f"pt{i}", [128, 512], BF16) for i in range(3)]
            rsb = self.sb(st, "rsb", [128, 512], F32)
            abufs = (PTs, [Res(), Res(), Res()], rsb, Res(), None, None)
            gbufs = ([self.sb(st, f"ge{i}", [128, 512], F32) for i in range(2)], [Res(), Res()],
                     [self.sb(st, f"gc{i}", [128, 512], BF16) for i in range(2)], [Res(), Res()])
            ogT = self.sb(st, "ogT", [128, 16, 512], BF16)
            r_ogT = Res()
            ytile = [self.sb(st, f"ytile{i}", [128, 512], F32) for i in range(2)]
            xre = [self.sb(st, f"xre{i}", [128, 512], F32) for i in range(2)]
            obufs = (ytile, [Res(), Res()], xre, [Res(), Res()])
            qn = [self.sb(st, f"qn{i}", [128, 512], BF16) for i in range(4)]
            r_qn = [Res(), Res(), Res(), Res()]
            kf = [self.sb(st, f"kf{i}", [128, 512], F32) for i in range(2)]
            r_kf = [Res(), Res()]
            ss4 = [self.sb(st, f"ss4{i}", [128, 4], F32) for i in range(2)]
            r_ss4 = [[Res() for _ in range(4)] for _ in range(2)]
            sqj = self.sb(st, "sqj", [128, 4, 128], BF16)
            r_sqj = [Res() for _ in range(4)]
            vst = [self.sb(st, f"vst{i}", [128, 512], F32) for i in range(2)]
            r_vst = [Res(), Res()]
            self.qni = 0
            self.s4i = 0
            self.kfi = 0
            self.vsi = 0

            def a_prep(blocks, only=None):
                for b, (src, nreal) in enumerate(blocks):
                    if only is None or only == b:
                        prep_hT(hbufs, [(src, nreal, 128 * b, r_hT[b])], gcolA, hT, None)

            def a_superblock(blocks, hf, has_prev, y1_rows, kout, vout, x_rows, next_blocks=None):
                nb = len(blocks)
                ntok = 128 * nb
                pf = 1 - hf
                r_h = r_hT[0:nb]
                self.stopat("hT")
                tails = []

                def run_tails(keep=0):
                    while len(tails) > keep:
                        tails.pop(0)()

                order = []
                for G in range(4):
                    order += [("q", G), ("k", G), ("v", G), ("g", G)]
                cidx = {"q": 0, "k": 4, "v": 8, "g": 12}
                ws.prefetch("a_in", cidx[order[0][0]] + order[0][1])
                for oi, (kind, G) in enumerate(order):
                    if oi + 1 < len(order):
                        ws.prefetch("a_in", cidx[order[oi + 1][0]] + order[oi + 1][1])
                    wc, rwc = ws.pop()
                    par = G % 2
                    if oi == 1: self.stopat("q")
                    if oi == 2: self.stopat("k")
                    if oi == 3: self.stopat("v")
                    if kind in ("q", "k"):
                        for b in range(nb):
                            pj, rpj = next_pj()
                            for kt in range(16):
                                self.mm(pj[:, :], hT[:, kt, 128 * b:128 * b + 128], wc[:, kt, :], kt == 0, kt == 15,
                                        [r_h[b], rwc], [rpj])
                            s4, rs4 = ss4[self.s4i % 2], r_ss4[self.s4i % 2]
                            self.s4i += 1
                            for hh in range(4):
                                self.act(sqj[:, hh, :], pj[:, 128 * hh:128 * hh + 128], AF.Square, [rpj], [r_sqj[hh], rs4[hh]],
                                         accum=s4[:, hh:hh + 1])
                            ry_, rr_ = self.rstd(s4[:, 0:4], 4, 1.0 / 128, rs4)
                            qb, rqb = qn[self.qni % 4], r_qn[self.qni % 4]
                            self.qni += 1
                            self.tt(qb[:].rearrange("p (a b) -> p a b", a=4), pj[:].rearrange("p (a b) -> p a b", a=4),
                                    ry_[:, 0:4].unsqueeze(2).to_broadcast([128, 4, 128]), ALU.mult, [rpj, rr_], [rqb])
                            if kind == "k" and kout is not None:
                                kfb, rkf = kf[self.kfi % 2], r_kf[self.kfi % 2]
                                self.kfi += 1
                                self.tt(kfb[:].rearrange("p (a b) -> p a b", a=4), pj[:].rearrange("p (a b) -> p a b", a=4),
                                        ry_[:, 0:4].unsqueeze(2).to_broadcast([128, 4, 128]), ALU.mult, [rpj, rr_], [rkf])
                                self.tt(kfb[:].rearrange("p (a b) -> p a b", a=4), kfb[:].rearrange("p (a b) -> p a b", a=4),
                                        gk_rep[:, :].unsqueeze(1).to_broadcast([128, 4, 128]), ALU.mult, [rkf, r_c], [rkf])
                                nr = blocks[b][1]
                                self.dma(kout[128 * b:128 * b + nr, 512 * G:512 * G + 512], kfb[0:nr, :], [rkf], [], q="pool")

                            def tail(kind=kind, G=G, b=b, qb=qb, rqb=rqb, par=par):
                                trb, rtr = next_tr()
                                tb = trb[:].bitcast(BF16)
                                for hh in range(4):
                                    self.tr(tb[:, hh * 128:(hh + 1) * 128], qb[:, 128 * hh:128 * hh + 128], ident_bf[:],
                                            [rqb, r_c], [rtr])
                                src3 = tb[:, 0:512].rearrange("p (a b) -> p a b", a=4)
                                if kind == "q":
                                    dst = qT[par][:, 0:4, 128 * b:128 * b + 128]
                                    self.act(dst, src3, AF.Identity, [rtr, r_c], [r_qT[par][hh] for hh in range(4)],
                                             scale=gq_col[:, 0:1])
                                else:
                                    dst = kTr[:, 4 * G:4 * G + 4, 512 * hf + 128 * b:512 * hf + 128 * b + 128]
                                    self.act(dst, src3, AF.Identity, [rtr, r_c], [r_kT[4 * G + hh][hf] for hh in range(4)],
                                             scale=gk_col[:, 0:1])
                            run_tails(1)
                            tails.append(tail)
                    elif kind == "v":
                        for b in range(nb):
                            pj, rpj = next_pj()
                            for kt in range(16):
                                self.mm(pj[:, :], hT[:, kt, 128 * b:128 * b + 128], wc[:, kt, :], kt == 0, kt == 15,
                                        [r_h[b], rwc], [rpj])
                            self.acopy(Vr[:, 4 * hf + b, 512 * G:512 * G + 512], pj[:, :], [rpj], [r_V[4 * hf + b][G]])
                            if vout is not None:
                                vb, rvb = vst[self.vsi % 2], r_vst[self.vsi % 2]
                                self.vsi += 1
                                self.vcopy(vb[:], pj[:, :], [rpj], [rvb])
                                nr = blocks[b][1]
                                self.dma(vout[128 * b:128 * b + nr, 512 * G:512 * G + 512], vb[0:nr, :], [rvb], [], q="pool")
                            run_tails()
                    else:
                        for hh in range(4):
                            pj, rpj = next_pj()
                            for kt in range(16):
                                self.mm(pj[:, 0:ntok], wc[:, kt, 128 * hh:128 * hh + 128], hT[:, kt, 0:ntok], kt == 0, kt == 15,
                                        r_h + [rwc], [rpj])
                            sigmoid_gate(pj, rpj, ntok, sgT[par][:, hh, 0:ntok], r_sgT[par][hh], gbufs)
                            run_tails()
                        run_tails()
                        self.stopat("g")
                        for hh in range(4):
                            h = 4 * G + hh
                            tl = []
                            for t in [4, 3, 5, 2, 6, 1, 7, 0]:
                                if t < 4 and not has_prev:
                                    continue
                                if t >= 4 and t - 4 >= nb:
                                    continue
                                b0 = max(0, t - 4)
                                b1 = min(nb - 1, t)
                                tl.append((t, 128 * b0, 128 * (b1 + 1)))

                            def kfn(t, h=h):
                                half = pf if t < 4 else hf
                                c = 512 * half + 128 * (t % 4)
                                return [(kTr[:, h, c:c + 128], r_kT[h][half])]

                            def vfn(t, h=h, G=G):
                                slot = 4 * (pf if t < 4 else hf) + (t % 4)
                                return (Vr[:, slot, 128 * h:128 * h + 128], r_V[slot][G])

                            def post(t, pt, rpt, c0, c1, h=h):
                                b = t - 3
                                if 0 <= b < nb:
                                    self.tt(pt[:, 128 * b:128 * b + 128], pt[:, 128 * b:128 * b + 128], EB[:, h, 0, :],
                                            ALU.mult, [rpt, r_c], [rpt], eng=EB_ENG)
                                b = t - 4
                                if 0 <= b < nb:
                                    self.tt(pt[:, 128 * b:128 * b + 128], pt[:, 128 * b:128 * b + 128], EB[:, h, 1, :],
                                            ALU.mult, [rpt, r_c], [rpt], eng=EB_ENG)
                                b = t
                                if 0 <= b < nb:
                                    self.memset(pt[0:64, 128 * b + 64:128 * b + 128], 0.0, [rpt], [rpt])

                            attention(ntok, tl, [(qT[par][:, hh, :], r_qT[par][hh])], kfn, vfn, None, cb[:, h:h + 1], post,
                                      sgT[par][:, hh, 0:ntok], r_sgT[par][hh], ogT[:, h, 0:ntok], r_ogT, abufs)
                        flush_epi()
                self.stopat("attn")
                staged = {}

                def stage1(c):
                    if next_blocks is not None and c < len(next_blocks):
                        staged[c] = prep_stage1(hbufs, next_blocks[c][0], next_blocks[c][1])

                def after_chunk(c):
                    if c in staged:
                        xn_, rxn_ = staged.pop(c)
                        prep_stage2(xn_, rxn_, gcolA, hT, 128 * c, r_hT[c])
                    stage1(c + 1)
                stage1(0)
                out_proj(ws, "a_out", ogT, r_ogT, nb, x_rows, y1_rows, blocks[-1][1], obufs, after_chunk)

            try:
              pblocks = lambda s: [(x_p[512 * s + 128 * b:512 * s + 128 * b + 128, :], 128) for b in range(4)]
              sblocks_ = [(x_s[0:NS, :], NS)]
              a_prep(pblocks(0))
              for s in range(T // 512):
                blocks = pblocks(s)
                last = (s == T // 512 - 1)
                if s >= 1 or T // 512 == 1:
                    todo = [(w_, j_) for (w_, n_) in (("b_in", 7), ("b_out", 4)) for j_ in range(n_) if (w_, j_) not in self.wdone]
                    for (wid, j) in (todo if last else todo[:2]):
                        key = (wid, j)
                        self.wres[key] = Res()
                        src, ncol = wsrc(wid, j)
                        for q4 in range(4):
                            self.dma(wscr[wid][j, :, 4 * q4:4 * q4 + 4, 0:ncol], src[:, 4 * q4:4 * q4 + 4, :], [], [self.wres[key]], q="pool")
                        self.wdone.add(key)
                a_superblock(blocks, s % 2, s > 0, (y_p if self.dbgA else y1)[512 * s:512 * s + 512, :],
                             o_akp if last else None, o_avp if last else None, x_p[512 * s:512 * s + 512, :],
                             sblocks_ if last else pblocks(s + 1))
              self.stopat("prompt")
              kc = Wc[0][:].rearrange("p a b -> p (a b)")
              r_kc = ws.rb[0]
              for t4 in range(4):
                  self.dma(kc[:, 2048 * t4:2048 * t4 + 2048], c_ak[128 * t4:128 * t4 + 128, :], [], [r_kc], q="pool")
                  self.dma(Vr[:, t4, :], c_av[128 * t4:128 * t4 + 128, :], [], [r_V[t4][g] for g in range(4)], q="pool")
              for t4 in range(4):
                  for G in range(4):
                      trb, rtr = next_tr()
                      tb = trb[:].bitcast(BF16)
                      for hh in range(4):
                          h = 4 * G + hh
                          self.tr(tb[:, hh * 128:(hh + 1) * 128], kc[:, 2048 * t4 + 128 * h:2048 * t4 + 128 * h + 128], ident_bf[:],
                                  [r_kc, r_c], [rtr])
                      self.acopy(kTr[:, 4 * G:4 * G + 4, 128 * t4:128 * t4 + 128], tb[:, 0:512].rearrange("p (a b) -> p a b", a=4),
                                 [rtr], [r_kT[4 * G + hh][0] for hh in range(4)])
              a_superblock(sblocks_, 1, True, y_s if self.dbgA else y1[T:T + 128, :], o_aks, o_avs, x_s[0:NS, :])
            except StopBuild:
                pass
            S.flush()

        self.stEB.close()
        if ph == "A":
            self._finish(S)
            return nc
        NTP = T // 128
        with ExitStack() as stB:
            cqT_p = self.sb(stB, "cqT_p", [128, 4, T], BF16)
            ckvT_p = self.sb(stB, "ckvT_p", [128, 4, T], BF16)
            krT_p = self.sb(stB, "krT_p", [128, T], BF16)
            cqT_s = self.sb(stB, "cqT_s", [128, 4, 128], BF16)
            ckvT_s = self.sb(stB, "ckvT_s", [128, 4, 1152], BF16)
            krT_s = self.sb(stB, "krT_s", [128, 1152], BF16)
            ropet = self.sb(stB, "ropet", [128, NTP + 1, 64], F32)
            ra = self.sb(stB, "rope_a", [128, 4, 32], F32)
            rb_ = self.sb(stB, "rope_b", [128, 4, 32], F32)
            r_rope = Res()
            self.s4i = 0

            def rope(src, nh, blk, dst, r_src, r_dst, eng=None):
                cos = ropet[:, blk, 0:32].unsqueeze(1).to_broadcast([128, nh, 32])
                sin = ropet[:, blk, 32:64].unsqueeze(1).to_broadcast([128, nh, 32])
                x1 = src[:, :, 0:32]
                x2 = src[:, :, 32:64]
                a = ra[:, 0:nh, :]
                b_ = rb_[:, 0:nh, :]
                E = eng or "dve"
                self.tt(a, x1, cos, ALU.mult, [r_src, r_c], [r_rope], eng=E)
                self.tt(b_, x2, sin, ALU.mult, [r_src, r_c, r_rope], [r_rope], eng=E)
                self.tt(dst[:, :, 0:32], a, b_, ALU.subtract, [r_rope], [r_dst], eng=E)
                self.tt(a, x1, sin, ALU.mult, [r_src, r_c, r_rope], [r_rope], eng=E)
                self.tt(b_, x2, cos, ALU.mult, [r_src, r_c, r_rope], [r_rope], eng=E)
                self.tt(dst[:, :, 32:64], a, b_, ALU.add, [r_rope], [r_dst], eng=E)

            r_cqT = [Res() for _ in range(NTP // 4 + 1)]
            r_ckvT = [Res() for _ in range(NTP // 4 + 3)]
            r_krT = [Res() for _ in range(NTP // 4 + 3)]
            SP0 = NTP // 4

            with ExitStack() as st:
                hTs = [self.sb(st, f"hT{i}", [128, 16, 512], BF16) for i in range(2)]
                r_hTs = [[Res() for _ in range(4)] for _ in range(2)]
                Wc = [self.sb(st, f"wc{i}", [128, 16, 512], BF16) for i in range(2)]
                ws = WStream(Wc, [Res(), Res()])
                xin = self.sb(st, "xin", [128, D], F32)
                xn = [self.sb(st, f"xn{i}", [128, D], BF16) for i in range(2)]
                ss1 = self.sb(st, "ss1", [128, 1], F32)
                hbufs = (xin, Res(), xn, [Res(), Res()], ss1, Res())
                gbufs = ([self.sb(st, f"ge{i}", [128, 512], F32) for i in range(2)], [Res(), Res()],
                         [self.sb(st, f"gc{i}", [128, 512], BF16) for i in range(2)], [Res(), Res()])
                cstage = self.sb(st, "cstage", [128, 8, 512], BF16)
                kstage = self.sb(st, "kstage", [128, 8, 64], BF16)
                r_cst = Res()
                nb16 = [self.sb(st, f"nb16{i}", [128, 512], BF16) for i in range(3)]
                r_nb16 = [Res(), Res(), Res()]
                cf = [self.sb(st, f"cf{i}", [128, 512], F32) for i in range(2)]
                r_cf = [Res(), Res()]
                ssb = [self.sb(st, f"ssb{i}", [128, 1], F32) for i in range(2)]
                r_ssb = [Res(), Res()]
                jk = self.sb(st, "jk", [128, 512], BF16)
                r_jk = Res()
                krn = [self.sb(st, f"krn{i}", [128, 1, 64], F32) for i in range(3)]
                kro = [self.sb(st, f"kro{i}", [128, 1, 64], F32) for i in range(3)]
                krb = [self.sb(st, f"krb{i}", [128, 64], BF16) for i in range(3)]
                r_krn = [Res() for _ in range(3)]
                r_kro = [Res() for _ in range(3)]
                r_krb = [Res() for _ in range(3)]
                sgs = [self.sb(st, f"sgs{i}", [128, 512], BF16) for i in range(2)]
                r_sgs = [Res(), Res()]
                self.cnt = 0

                r_krpad = Res()
                self.memset(krT_p[64:128, :], 0.0, [], [r_krpad], eng="dve")
                self.memset(krT_s[64:128, :], 0.0, [], [r_krpad], eng="dve")
                self.dma(ropet[:], rope_cs.rearrange("(b p) c -> p b c", p=128), [], [r_c])
                for t8 in range(8):
                    self.dma(cstage[:, t8, :], c_bc[128 * t8:128 * t8 + 128, :], [], [r_cst], q="pool")
                    self.dma(kstage[:, t8, :], c_br[128 * t8:128 * t8 + 128, :], [], [r_cst], q="pool")
                for t8 in range(8):
                    trb, rtr = next_tr()
                    tb = trb[:].bitcast(BF16)
                    for kt in range(4):
                        self.tr(tb[:, kt * 128:(kt + 1) * 128], cstage[:, t8, 128 * kt:128 * kt + 128], ident_bf[:], [r_cst, r_c], [rtr])
                    self.acopy(ckvT_s[:, 0:4, 128 * t8:128 * t8 + 128], tb[:, 0:512].rearrange("p (a b) -> p a b", a=4),
                               [rtr], [r_ckvT[SP0 + t8 // 4]])
                    trb, rtr = next_tr()
                    tb = trb[:].bitcast(BF16)
                    self.tr(tb[0:64, 0:128], kstage[:, t8, :], ident_bf[:], [r_cst, r_c], [rtr])
                    self.acopy(krT_s[0:64, 128 * t8:128 * t8 + 128], tb[0:64, 0:128], [rtr], [r_krT[SP0 + t8 // 4]])

                def b1_superblock(blocks, cqT, cq_c0, r_cq, ckvT, kr_T, ck_c0, r_ck, r_kr, ckv_out, kr_out, sg_c0, blk0,
                                  hi, next_blocks):
                    nb = len(blocks)
                    ntok = 128 * nb
                    hT, r_hT = hTs[hi], r_hTs[hi]
                    hTn, r_hTn = hTs[1 - hi], r_hTs[1 - hi]
                    r_h = r_hT[0:nb]
                    staged = {}

                    def stage1_load(c):
                        if next_blocks is not None and c < len(next_blocks):
                            prep_load(hbufs, next_blocks[c][0], next_blocks[c][1])

                    def stage1_norm(c):
                        if next_blocks is not None and c < len(next_blocks):
                            staged[c] = prep_norm(hbufs)

                    def stage2(c):
                        if c in staged:
                            xn_, rxn_ = staged.pop(c)
                            prep_stage2(xn_, rxn_, gcolB, hTn, 128 * c, r_hTn[c])
                    tails = []

                    def run_tails(keep=0):
                        while len(tails) > keep:
                            tails.pop(0)()

                    ws.prefetch("b_in", 0)
                    for j in range(7):
                        if j + 1 < 7:
                            ws.prefetch("b_in", j + 1)
                        wc, rwc = ws.pop()
                        if j > 3:
                            stage2(j - 4)
                        if j >= 3:
                            stage1_load(j - 3)
                        if j < 2:
                            for b in range(nb):
                                nr = blocks[b][1]
                                pj, rpj = next_pj()
                                for kt in range(16):
                                    self.mm(pj[:, :], hT[:, kt, 128 * b:128 * b + 128], wc[:, kt, :], kt == 0, kt == 15, [r_h[b], rwc], [rpj])
                                k = self.cnt
                                self.cnt += 1
                                s1, rs1 = ssb[k % 2], r_ssb[k % 2]
                                self.act(jk[:], pj[:, :], AF.Square, [rpj], [r_jk, rs1], accum=s1[:, 0:1])
                                ry_, rr_ = self.rstd(s1[:, 0:1], 1, 1.0 / 512, rs1)
                                nbf, rnb = nb16[k % 3], r_nb16[k % 3]
                                if j == 0:
                                    self.stt(nbf[:], pj[:, :], ry_[:, 0:1], gqa_rep[:], ALU.mult, ALU.mult, [rpj, rr_, r_c], [rnb])
                                else:
                                    cfb, rcf = cf[k % 2], r_cf[k % 2]
                                    self.stt(cfb[:], pj[:, :], ry_[:, 0:1], gkva_rep[:], ALU.mult, ALU.mult, [rpj, rr_, r_c], [rcf])
                                    self.dma(ckv_out[128 * b:128 * b + nr, :], cfb[0:nr, :], [rcf], [], q="pool")

                                def tail(j=j, b=b, nbf=nbf, rnb=rnb, cfb=(cfb if j == 1 else None), rcf=(rcf if j == 1 else None)):
                                    if cfb is not None:
                                        self.acopy(nbf[:], cfb[:], [rcf], [rnb])
                                    trb, rtr = next_tr()
                                    tb = trb[:].bitcast(BF16)
                                    for kt in range(4):
                                        self.tr(tb[:, kt * 128:(kt + 1) * 128], nbf[:, 128 * kt:128 * kt + 128], ident_bf[:], [rnb, r_c], [rtr])
                                    src3 = tb[:, 0:512].rearrange("p (a b) -> p a b", a=4)
                                    if j == 0:
                                        self.acopy(cqT[:, 0:4, cq_c0 + 128 * b:cq_c0 + 128 * b + 128], src3, [rtr], [r_cq])
                                    else:
                                        self.acopy(ckvT[:, 0:4, ck_c0 + 128 * b:ck_c0 + 128 * b + 128], src3, [rtr], [r_ck])
                                run_tails()
                                tails.append(tail)
                        elif j == 2:
                            for b in range(nb):
                                nr = blocks[b][1]
                                pj, rpj = next_pj()
                                for kt in range(16):
                                    self.mm(pj[:, 0:64], hT[:, kt, 128 * b:128 * b + 128], wc[:, kt, 0:64], kt == 0, kt == 15, [r_h[b], rwc], [rpj])
                                k = self.cnt
                                self.cnt += 1
                                s1, rs1 = ssb[k % 2], r_ssb[k % 2]
                                self.act(jk[:, 0:64], pj[:, 0:64], AF.Square, [rpj], [r_jk, rs1], accum=s1[:, 0:1])
                                ry_, rr_ = self.rstd(s1[:, 0:1], 1, 1.0 / 64, rs1)
                                kn_, rkn = krn[k % 3], r_krn[k % 3]
                                ko_, rko = kro[k % 3], r_kro[k % 3]
                                kb_, rkb = krb[k % 3], r_krb[k % 3]
                                self.stt(kn_[:, 0, :], pj[:, 0:64], ry_[:, 0:1], gkr_rep[:], ALU.mult, ALU.mult, [rpj, rr_, r_c], [rkn])
                                rope(kn_, 1, blk0 + b, ko_, rkn, rko)
                                self.dma(kr_out[128 * b:128 * b + nr, :], ko_[0:nr, 0, :], [rko], [], q="pool")

                                def tail(b=b, kb_=kb_, rkb=rkb, ko_=ko_, rko=rko):
                                    self.acopy(kb_[:], ko_[:, 0, :], [rko], [rkb])
                                    trb, rtr = next_tr()
                                    tb = trb[:].bitcast(BF16)
                                    self.tr(tb[0:64, 0:128], kb_[:], ident_bf[:], [rkb, r_c], [rtr])
                                    self.acopy(kr_T[0:64, ck_c0 + 128 * b:ck_c0 + 128 * b + 128], tb[0:64, 0:128], [rtr], [r_kr])
                                run_tails(1)
                                tails.append(tail)
                        else:
                            for hh in range(4):
                                h = 4 * (j - 3) + hh
                                pj, rpj = next_pj()
                                for kt in range(16):
                                    self.mm(pj[:, 0:ntok], wc[:, kt, 128 * hh:128 * hh + 128], hT[:, kt, 0:ntok], kt == 0, kt == 15, r_h + [rwc], [rpj])
                                k = self.cnt
                                self.cnt += 1
                                sg_, rsg = sgs[k % 2], r_sgs[k % 2]
                                sigmoid_gate(pj, rpj, ntok, sg_[:, 0:ntok], rsg, gbufs)
                                self.dma(sgT_d[128 * h:128 * h + 128, sg_c0:sg_c0 + ntok], sg_[:, 0:ntok], [rsg], [], q="pool")
                                run_tails()
                            stage1_norm(j - 3)
                    run_tails()
                    stage2(3)

                try:
                    pbl = lambda s: [(y1[512 * s + 128 * b:512 * s + 128 * b + 128, :], 128) for b in range(4)]
                    sbl_ = [(y1[T:T + NS, :], NS)]
                    prep_hT(hbufs, [(src, nreal, 128 * b, r_hTs[0][b]) for b, (src, nreal) in enumerate(pbl(0))], gcolB, hTs[0], None)
                    nsb = NTP // 4
                    for s in range(nsb):
                        b1_superblock(pbl(s), cqT_p, 512 * s, r_cqT[s], ckvT_p, krT_p, 512 * s, r_ckvT[s], r_krT[s],
                                      o_bcp[512 * s:512 * s + 512, :], o_brp[512 * s:512 * s + 512, :], 512 * s, 4 * s,
                                      s % 2, pbl(s + 1) if s + 1 < nsb else sbl_)
                    self.stopat("b1prompt")
                    b1_superblock(sbl_, cqT_s, 0, r_cqT[SP0], ckvT_s, krT_s, 1024, r_ckvT[SP0 + 2], r_krT[SP0 + 2],
                                  o_bcs, o_brs, T, NTP, nsb % 2, None)
                except StopBuild:
                    pass
                S.flush()

            if ph == "B1":
                self._finish(S)
                return nc

            with ExitStack() as st:
                wuk = self.sb(st, "wuk", [128, 4, 512], BF16)
                wuv = self.sb(st, "wuv", [128, 4, 512], BF16)
                wuqn = self.sb(st, "wuqn", [128, 4, 512], BF16)
                wuqr = self.sb(st, "wuqr", [128, 4, 256], BF16)
                r_w = Res()
                kT_g = self.sb(st, "kT_g", [128, 4, max(T, 1152)], BF16)
                V_g = self.sb(st, "V_g", [128, max(NTP, 9), 512], BF16)
                r_kTg = [Res() for _ in range(max(NTP // 4, 3))]
                r_Vg = [Res() for _ in range(max(NTP // 4, 3))]
                r_qTr_pad = Res()
                r_qTr = [r_qTr_pad] * 2
                qTn = [self.sb(st, "qTn0", [128, 4, 512], BF16)] * 2
                qTr = [self.sb(st, "qTr0", [128, 4, 512], BF16)] * 2
                self.memset(qTr[0][64:128, :, :], 0.0, [], [r_qTr_pad], eng="dve")
                r_qTn = [Res()] * 2
                sgin = [self.sb(st, "sgin0", [128, 4, 512], BF16)] * 2
                r_sgin = [Res()] * 2
                PTs = [self.sb(st, f"pt{i}", [128, 512], BF16) for i in range(3)]
                rsb = self.sb(st, "rsb", [128, 512], F32)
                abufs = (PTs, [Res(), Res(), Res()], rsb, Res(), None, None)
                ogs = [self.sb(st, f"ogs{i}", [128, 512], BF16) for i in range(2)]
                r_ogs = [Res(), Res()]
                nb16 = [self.sb(st, f"nb16{i}", [128, 512], BF16) for i in range(4)]
                r_nb16 = [Res(), Res(), Res(), Res()]
                self.qri = 0
                qrn = [self.sb(st, f"qrn{i}", [128, 4, 64], F32) for i in range(2)]
                qro = [self.sb(st, f"qro{i}", [128, 4, 64], F32) for i in range(2)]
                qrb = [self.sb(st, f"qrb{i}", [128, 4, 64], BF16) for i in range(2)]
                r_qrn = [Res(), Res()]
                r_qro = [Res(), Res()]
                r_qrb = [Res(), Res()]
                self.cnt = 0
                self.par = 0

                ss12 = [self.sb(st, f"ss12_{i}", [128, 12], F32) for i in range(3)]
                r_ss12 = [[Res() for _ in range(12)] for _ in range(3)]
                sq12 = self.sb(st, "sq12", [128, 4, 128], BF16)
                r_sq12 = [Res() for _ in range(4)]
                self.s12i = 0

                def b2_seq(G, cqT, ckvT, kr_T, sblocks, og_base):
                    tails = []

                    def run_tails(keep=0):
                        while len(tails) > keep:
                            tails.pop(0)()

                    for si, (t0, nb, isq, r_cq, r_ck, r_kr, rblk0) in enumerate(sblocks):
                        ntok = 128 * nb
                        par = self.par
                        if isq:
                            self.par ^= 1
                        self.pj_pool = pool6
                        for b in range(nb):
                            c0 = t0 + 128 * b
                            q0 = 128 * b
                            qc = q0 + (t0 if cqT is cqT_p else 0)
                            s12, rs12 = ss12[self.s12i % 3], r_ss12[self.s12i % 3]
                            self.s12i += 1
                            pjk, rpjk = next_pj()
                            for kt in range(4):
                                self.mm(pjk[:, :], ckvT[:, kt, c0:c0 + 128], wuk[:, kt, :], kt == 0, kt == 3, [r_ck, r_w], [rpjk])
                            for hh in range(4):
                                self.act(sq12[:, hh, :], pjk[:, 128 * hh:128 * hh + 128], AF.Square, [rpjk], [r_sq12[hh], rs12[hh]],
                                         accum=s12[:, hh:hh + 1])
                            ncol = 4
                            if isq:
                                pjq, rpjq = next_pj()
                                for kt in range(4):
                                    self.mm(pjq[:, :], cqT[:, kt, qc:qc + 128], wuqn[:, kt, :], kt == 0, kt == 3, [r_cq, r_w], [rpjq])
                                for hh in range(4):
                                    self.act(sq12[:, hh, :], pjq[:, 128 * hh:128 * hh + 128], AF.Square, [rpjq],
                                             [r_sq12[hh], rs12[4 + hh]], accum=s12[:, 4 + hh:5 + hh])
                                pjr, rpjr = next_pj()
                                for kt in range(4):
                                    self.mm(pjr[:, 0:256], cqT[:, kt, qc:qc + 128], wuqr[:, kt, :], kt == 0, kt == 3, [r_cq, r_w], [rpjr])
                                for hh in range(4):
                                    self.act(sq12[:, hh, 0:64], pjr[:, 64 * hh:64 * hh + 64], AF.Square, [rpjr],
                                             [r_sq12[hh], rs12[8 + hh]], accum=s12[:, 8 + hh:9 + hh], scale=2.0 ** 0.5)
                                ncol = 12
                            pjv, rpjv = next_pj()
                            for kt in range(4):
                                self.mm(pjv[:, :], ckvT[:, kt, c0:c0 + 128], wuv[:, kt, :], kt == 0, kt == 3, [r_ck, r_w], [rpjv])
                            self.acopy(V_g[:, c0 // 128, :], pjv[:, :], [rpjv], [r_Vg[si]])
                            ry_, rr_ = self.rstd(s12[:, 0:ncol], ncol, 1.0 / 128, rs12[0:ncol])
                            k = self.cnt
                            self.cnt += 1
                            nbk, rnbk = nb16[k % 4], r_nb16[k % 4]
                            self.tt(nbk[:].rearrange("p (a b) -> p a b", a=4), pjk[:].rearrange("p (a b) -> p a b", a=4),
                                    ry_[:, 0:4].unsqueeze(2).to_broadcast([128, 4, 128]), ALU.mult, [rpjk, rr_], [rnbk])

                            def tailk(nbf=nbk, rnb=rnbk, c0=c0, si=si):
                                trb, rtr = next_tr()
                                tb = trb[:].bitcast(BF16)
                                for hh in range(4):
                                    self.tr(tb[:, hh * 128:(hh + 1) * 128], nbf[:, 128 * hh:128 * hh + 128], ident_bf[:], [rnb, r_c], [rtr])
                                self.act(kT_g[:, 0:4, c0:c0 + 128], tb[:, 0:512].rearrange("p (a b) -> p a b", a=4), AF.Identity,
                                         [rtr, r_c], [r_kTg[si]], scale=gkn_col[:, 0:1])
                            tails.append(tailk)
                            if isq:
                                k = self.cnt
                                self.cnt += 1
                                nbq, rnbq = nb16[k % 4], r_nb16[k % 4]
                                self.tt(nbq[:].rearrange("p (a b) -> p a b", a=4), pjq[:].rearrange("p (a b) -> p a b", a=4),
                                        ry_[:, 4:8].unsqueeze(2).to_broadcast([128, 4, 128]), ALU.mult, [rpjq, rr_], [rnbq])

                                def tailq(nbf=nbq, rnb=rnbq, q0=q0, par=par):
                                    trb, rtr = next_tr()
                                    tb = trb[:].bitcast(BF16)
                                    for hh in range(4):
                                        self.tr(tb[:, hh * 128:(hh + 1) * 128], nbf[:, 128 * hh:128 * hh + 128], ident_bf[:], [rnb, r_c], [rtr])
                                    self.act(qTn[par][:, 0:4, q0:q0 + 128], tb[:, 0:512].rearrange("p (a b) -> p a b", a=4), AF.Identity,
                                             [rtr, r_c], [r_qTn[par]], scale=gqn_col[:, 0:1])
                                tails.append(tailq)
                                kq = self.qri % 2
                                self.qri += 1
                                qn_, rqn = qrn[kq], r_qrn[kq]
                                qo_, rqo = qro[kq], r_qro[kq]
                                qb_, rqb = qrb[kq], r_qrb[kq]
                                self.tt(qn_[:], pjr[:, 0:256].rearrange("p (a b) -> p a b", a=4),
                                        ry_[:, 8:12].unsqueeze(2).to_broadcast([128, 4, 64]), ALU.mult, [rpjr, rr_], [rqn])
                                self.tt(qn_[:], qn_[:], gqr_rep[:, :].unsqueeze(1).to_broadcast([128, 4, 64]), ALU.mult, [rqn, r_c], [rqn],
                                        eng=ROPE_ENG)
                                rope(qn_, 4, rblk0 + b, qo_, rqn, rqo, eng=ROPE_ENG)

                                def tailr(qb_=qb_, rqb=rqb, q0=q0, par=par, qo_=qo_, rqo=rqo):
                                    self.acopy(qb_[:], qo_[:], [rqo], [rqb])
                                    trb, rtr = next_tr()
                                    tb = trb[:].bitcast(BF16)
                                    for hh in range(4):
                                        self.tr(tb[0:64, hh * 128:(hh + 1) * 128], qb_[:, hh, :], ident_bf[:], [rqb, r_c], [rtr])
                                    self.acopy(qTr[par][0:64, 0:4, q0:q0 + 128], tb[0:64, 0:512].rearrange("p (a b) -> p a b", a=4),
                                               [rtr], [r_qTr[par]])
                                tails.append(tailr)
                            run_tails(3 if isq else 1)
                        run_tails()
                        self.pj_pool = pool2
                        if not isq:
                            continue
                        nt0 = t0 // 128
                        og_off = og_base + t0
                        sg_, rsg = sgin[par], r_sgin[par]
                        self.dma(sg_[:, :, 0:ntok],
                                 sgT_d[512 * G:512 * G + 512, og_off:og_off + ntok].rearrange("(h p) t -> p h t", p=128), [], [rsg])
                        for hh in range(4):
                            h = 4 * G + hh
                            tl = []
                            for kt in range(nt0 + nb):
                                if kt < nt0:
                                    tl.append((kt, 0, ntok))
                                else:
                                    tl.append((kt, 128 * (kt - nt0), ntok))

                            def kfn(kt, hh=hh):
                                return [(kT_g[:, hh, 128 * kt:128 * kt + 128], r_kTg[kt // 4]),
                                        (kr_T[0:128, 128 * kt:128 * kt + 128], sblocks[kt // 4][5])]

                            def vfn(kt, hh=hh):
                                return (V_g[:, kt, 128 * hh:128 * hh + 128], r_Vg[kt // 4])

                            def post(kt, pt, rpt, c0, c1, nt0=nt0):
                                if kt >= nt0:
                                    j = kt - nt0
                                    self.memset(pt[64:128, 128 * j:128 * j + 64], 0.0, [rpt], [rpt])

                            k = self.cnt
                            self.cnt += 1
                            og_, rog = ogs[k % 2], r_ogs[k % 2]
                            def og_store(h=h, og_=og_, rog=rog, og_off=og_off, ntok=ntok):
                                self.dma(ogT_d[128 * h:128 * h + 128, og_off:og_off + ntok], og_[:, 0:ntok], [rog], [])
                            attention(ntok, tl, [(qTn[par][:, hh, :], r_qTn[par]), (qTr[par][0:128, hh, :], r_qTr[par])],
                                      kfn, vfn, B_SCALE, None, post, sg_[:, hh, 0:ntok], rsg, og_[:, 0:ntok], rog, abufs, og_store)
                        flush_epi()

                try:
                    for G in range(4):
                        self.dma(wuk[:], w_b_uk[:, 512 * G:512 * G + 512].rearrange("(kt p) c -> p kt c", p=128), [], [r_w], q="pool")
                        self.dma(wuv[:], w_b_uv[:, 512 * G:512 * G + 512].rearrange("(kt p) c -> p kt c", p=128), [], [r_w], q="pool")
                        for hh in range(4):
                            c0 = (4 * G + hh) * 192
                            self.dma(wuqn[:, :, 128 * hh:128 * hh + 128], w_b_uq[:, c0:c0 + 128].rearrange("(kt p) c -> p kt c", p=128),
                                     [], [r_w], q="pool")
                            self.dma(wuqr[:, :, 64 * hh:64 * hh + 64], w_b_uq[:, c0 + 128:c0 + 192].rearrange("(kt p) c -> p kt c", p=128),
                                     [], [r_w], q="pool")
                        sbl = [(512 * s, 4, True, r_cqT[s], r_ckvT[s], r_krT[s], 4 * s) for s in range(NTP // 4)]
                        b2_seq(G, cqT_p, ckvT_p, krT_p, sbl, 0)
                        self.stopat("b2prompt")
                        sbs = [(0, 4, False, None, r_ckvT[SP0], r_krT[SP0], 0), (512, 4, False, None, r_ckvT[SP0 + 1], r_krT[SP0 + 1], 0),
                               (1024, 1, True, r_cqT[SP0], r_ckvT[SP0 + 2], r_krT[SP0 + 2], NTP)]
                        b2_seq(G, cqT_s, ckvT_s, krT_s, sbs, T - 1024)
                        self.stopat("b2g0")
                except StopBuild:
                    pass
                S.flush()

        if ph == "B2":
            self._finish(S)
            return nc

        with ExitStack() as st:
            aT = self.sb(st, "aT", [128, 16, 1024], BF16)
            r_aT = Res()
            Wc = [self.sb(st, f"wc{i}", [128, 16, 512], BF16) for i in range(2)]
            ws = WStream(Wc, [Res(), Res()])
            ytile = [self.sb(st, f"ytile{i}", [128, 512], F32) for i in range(3)]
            xre = [self.sb(st, f"xre{i}", [128, 512], F32) for i in range(3)]
            obufs = (ytile[0:2], [Res(), Res()], xre[0:2], [Res(), Res()])
            for g0 in range(0, NTP, 8):
                nbg = min(8, NTP - g0)
                self.dma(aT[:, :, 0:128 * nbg], ogT_d[:, 128 * g0:128 * (g0 + nbg)].rearrange("(kt p) t -> p kt t", p=128), [], [r_aT])
                out_proj(ws, "b_out", aT, r_aT, nbg, y1[128 * g0:128 * (g0 + nbg), :], y_p[128 * g0:128 * (g0 + nbg), :], 128, obufs)
            self.dma(aT[:, :, 0:128], ogT_d[:, T:T + 128].rearrange("(kt p) t -> p kt t", p=128), [], [r_aT])
            out_proj(ws, "b_out", aT, r_aT, 1, y1[T:T + NS, :], y_s, NS, obufs)
            S.flush()
        self._finish(S)
        return nc


    def _finish(self, S):
        self.stats = dict(ops=S.tot_ops, waits=S.tot_waits, incs=dict(S.incc), n_dma=S.n_dma)


_CACHE = {}


def _rope_table():
    half = 32
    inv = (np.float32(10000.0) ** (-np.arange(half, dtype=np.float32) / np.float32(half))).astype(np.float32)
    pos = np.concatenate([np.arange(T), PAST + np.arange(128)]).astype(np.float32)
    ang = (pos[:, None] * inv[None, :]).astype(np.float32)
    return np.concatenate([np.cos(ang), np.sin(ang)], axis=1).astype(np.float32)


def _build(phases="all"):
    if phases not in _CACHE:
        b = Builder(phases)
        nc = b.build()
        _CACHE[phases] = (nc, b)
    return _CACHE[phases]


def kernel(**inputs):
    phases = inputs.pop("_phases", "all")
    inputs = dict(inputs)
    nc, b = _build(phases)
    f = lambda a: np.ascontiguousarray(np.asarray(a, dtype=np.float32))
    rope = _rope_table()
    shared = {
        "a_ln": f(inputs["a_ln"][0]), "w_a_in": f(inputs["w_a_in"][0]), "a_q_norm": f(inputs["a_q_norm"][0]),
        "a_k_norm": f(inputs["a_k_norm"][0]), "a_rel_bias": f(inputs["a_rel_bias"][0]), "w_a_out": f(inputs["w_a_out"][0]),
        "b_ln": f(inputs["b_ln"][0]), "w_b_in": f(inputs["w_b_in"][0]), "b_q_a_norm": f(inputs["b_q_a_norm"][0]),
        "w_b_uq": f(inputs["w_b_uq"][0]), "b_kv_a_norm": f(inputs["b_kv_a_norm"][0]), "w_b_uk": f(inputs["w_b_uk"][0]),
        "w_b_uv": f(inputs["w_b_uv"][0]), "b_q_nope_norm": f(inputs["b_q_nope_norm"][0]),
        "b_k_nope_norm": f(inputs["b_k_nope_norm"][0]), "b_q_rope_norm": f(inputs["b_q_rope_norm"][0]),
        "b_k_rope_norm": f(inputs["b_k_rope_norm"][0]), "w_b_out": f(inputs["w_b_out"][0]), "rope_cs": rope,
    }
    in_maps = []
    for c in range(N_CORES):
        m = dict(shared)
        m["x_prompt"] = f(inputs["x_prompt"][c][:T])
        m["x_sample"] = f(inputs["x_sample"][c])
        m["cache_a_k"] = f(inputs["cache_a_k"][0, c]).reshape(512, D)
        m["cache_a_v"] = f(inputs["cache_a_v"][0, c]).reshape(512, D)
        m["cache_b_ckv"] = f(inputs["cache_b_ckv"][0, c])
        m["cache_b_krope"] = f(inputs["cache_b_krope"][0, c])
        in_maps.append(m)
    ncr = int(inputs.pop("_ncores", N_CORES))
    res = run_bass_kernel_spmd(nc, in_maps[:ncr], core_ids=list(range(ncr)))
    R = list(res.results) + [res.results[0]] * (N_CORES - ncr)
    st = lambda k: np.stack([R[c][k] for c in range(N_CORES)])
    y_p = st("y_prompt")
    y_s = st("y_sample")
    akp = st("new_a_k_prompt").reshape(1, N_CORES, 512, 16, 128)
    avp = st("new_a_v_prompt").reshape(1, N_CORES, 512, 16, 128)
    bcp = st("new_b_ckv_prompt")[None]
    brp = st("new_b_krope_prompt")[None]
    aks = st("new_a_k_sample").reshape(1, N_CORES, NS, 16, 128)
    avs = st("new_a_v_sample").reshape(1, N_CORES, NS, 16, 128)
    bcs = st("new_b_ckv_sample")[None]
    brs = st("new_b_krope_sample")[None]
    return (y_p, y_s, akp, avp, bcp, brp, aks, avs, bcs, brs)
```

```python
import numpy as np
from contextlib import ExitStack
import concourse.bass as bass
import concourse.mybir as mybir
from concourse.bass_utils import run_bass_kernel_spmd

F32 = mybir.dt.float32
BF16 = mybir.dt.bfloat16
I32 = mybir.dt.int32
AF = mybir.ActivationFunctionType
ALU = mybir.AluOpType

N_CORES = 8
T = 4096
D = 2048
NS = 64
PAST = 1024
EPS = 1e-6
A_SCALE = 128 ** -0.5
B_SCALE = 192 ** -0.5
BIN = 3136

ENGS = ("pe", "act", "dve", "pool", "sp")
CH = 30000
N_DMA_SEMS = 40
import os as _os0
ROPE_ENG = _os0.environ.get("DEV_ROPE_ENG", "dve")
EB_ENG = _os0.environ.get("DEV_EB_ENG", "pool")


class Res:
    __slots__ = ("w", "r", "excl")

    def __init__(self, excl=False):
        self.w = None
        self.r = []
        self.excl = excl


class Op:
    __slots__ = ("eng", "fn", "reads", "writes", "dma", "seq", "deps", "needs_inc", "inc_idx", "dsem", "dval")

    def __init__(self, eng, fn, reads, writes, dma):
        self.eng = eng
        self.fn = fn
        self.reads = reads
        self.writes = writes
        self.dma = dma
        self.deps = []
        self.needs_inc = False


class Sched:
    def __init__(self, nc, stack):
        self.nc = nc
        self.stack = stack
        self.e = {"pe": nc.tensor, "act": nc.scalar, "dve": nc.vector, "pool": nc.gpsimd, "sp": nc.sync}
        self.ops = []
        self.seqc = {e: 0 for e in ENGS}
        self.incc = {e: 0 for e in ENGS}
        self.waited = {e: {e2: -1 for e2 in ENGS} for e in ENGS}
        self.esems = {e: [] for e in ENGS}
        self.dsems = [stack.enter_context(nc.semaphore(f"sdma{i}")) for i in range(N_DMA_SEMS)]
        self.dcount = [0] * N_DMA_SEMS
        self.n_dma = 0
        self.tot_ops = 0
        self.tot_waits = 0

    def op(self, eng, fn, reads=(), writes=()):
        self.ops.append(Op(eng, fn, tuple(reads), tuple(writes), False))

    def dma(self, eng, fn, reads=(), writes=()):
        self.ops.append(Op(eng, fn, tuple(reads), tuple(writes), True))

    def _esem(self, e, idx):
        k = idx // CH
        while len(self.esems[e]) <= k:
            self.esems[e].append(self.stack.enter_context(self.nc.semaphore(f"s{e}{len(self.esems[e])}")))
        return self.esems[e][k], idx % CH + 1

    def flush(self):
        ops = self.ops
        self.ops = []
        waited = self.waited
        waited_dma = {e: set() for e in ENGS}
        ring = []
        allres = set()
        last_on = {}
        for o in ops:
            o.seq = self.seqc[o.eng]
            self.seqc[o.eng] += 1
            cand = []
            for r in o.reads:
                allres.add(r)
                if r.w is not None:
                    cand.append(r.w)
                if r.excl:
                    for d in r.r:
                        if d.eng != o.eng:
                            cand.append(d)
            for r in o.writes:
                allres.add(r)
                if r.w is not None:
                    cand.append(r.w)
                cand.extend(r.r)
            if o.dma:
                o.dsem = self.n_dma % N_DMA_SEMS
                self.n_dma += 1
                if len(ring) >= N_DMA_SEMS:
                    cand.append(ring[len(ring) - N_DMA_SEMS])
                ring.append(o)
            else:
                last_on[o.eng] = o
            best = {}
            for d in cand:
                if d is o:
                    continue
                if d.dma:
                    if id(d) not in waited_dma[o.eng]:
                        waited_dma[o.eng].add(id(d))
                        o.deps.append(d)
                else:
                    if d.eng == "pe" and o.eng == "pe" and not o.dma:
                        continue
                    if d.seq > waited[o.eng][d.eng]:
                        if d.eng not in best or d.seq > best[d.eng].seq:
                            best[d.eng] = d
            for e2, d in best.items():
                waited[o.eng][e2] = d.seq
                d.needs_inc = True
                o.deps.append(d)
            for r in o.reads:
                r.r.append(o)
            for r in o.writes:
                r.w = o
                r.r = []
        for o in last_on.values():
            o.needs_inc = True
        for o in ops:
            if o.dma:
                self.dcount[o.dsem] += 16
                o.dval = self.dcount[o.dsem]
            elif o.needs_inc:
                o.inc_idx = self.incc[o.eng]
                self.incc[o.eng] += 1
        for o in ops:
            eng = self.e[o.eng]
            for d in o.deps:
                if d.dma:
                    eng.wait_ge(self.dsems[d.dsem], d.dval)
                else:
                    s, v = self._esem(d.eng, d.inc_idx)
                    eng.wait_ge(s, v)
                self.tot_waits += 1
            ins = o.fn(eng)
            if o.dma:
                ins.then_inc(self.dsems[o.dsem], 16)
            elif o.needs_inc:
                s, v = self._esem(o.eng, o.inc_idx)
                ins.then_inc(s, 1)
        self.tot_ops += len(ops)
        for e in ENGS:
            eng = self.e[e]
            for e2 in ENGS:
                if e2 != e and self.incc[e2] > 0:
                    s, v = self._esem(e2, self.incc[e2] - 1)
                    eng.wait_ge(s, v)
            for i in range(N_DMA_SEMS):
                if self.dcount[i] > 0:
                    eng.wait_ge(self.dsems[i], self.dcount[i])
        for e in ENGS:
            for e2 in ENGS:
                waited[e][e2] = self.seqc[e2] - 1
        for r in allres:
            r.w = None
            r.r = []


class StopBuild(Exception):
    pass


import os as _os
_STOP = _os.environ.get("DEV_STOP", "")


class Builder:
    def stopat(self, tag):
        if _STOP == tag:
            raise StopBuild()

    def __init__(self, phases="all"):
        self.phases = phases
        self.nc = bass.Bass("TRN2", target_bir_lowering=False)
        self.gst = ExitStack()

    def din(self, name, shape, dt=F32):
        return self.nc.dram_tensor(name, list(shape), dt, kind="ExternalInput").ap()

    def dout(self, name, shape, dt=F32):
        return self.nc.dram_tensor(name, list(shape), dt, kind="ExternalOutput").ap()

    def dscr(self, name, shape, dt):
        return self.nc.dram_tensor(name, list(shape), dt, kind="Internal").ap()

    def sb(self, st, name, shape, dt):
        self._uid = getattr(self, "_uid", 0) + 1
        return st.enter_context(self.nc.sbuf_tensor(f"{name}_{self._uid}", list(shape), dt))

    def ps(self, st, name):
        return st.enter_context(self.nc.psum_tensor(name, [128, 512], F32))

    def mm(self, out, lhsT, rhs, start, stop, R, W):
        self.S.op("pe", lambda e: e.matmul(out, lhsT=lhsT, rhs=rhs, start=start, stop=stop,
                                           skip_group_check=True), R, W)

    def tr(self, out, in_, ident, R, W):
        self.S.op("pe", lambda e: e.transpose(out=out, in_=in_, identity=ident), R, W)

    def act(self, out, in_, func, R, W, bias=None, scale=None, accum=None):
        kw = {}
        if bias is not None:
            kw["bias"] = bias
        if scale is not None:
            kw["scale"] = scale
        if accum is not None:
            kw["accum_out"] = accum
        self.S.op("act", lambda e: e.activation(out=out, in_=in_, func=func, **kw), R, W)

    def acopy(self, out, in_, R, W):
        self.S.op("act", lambda e: e.activation(out=out, in_=in_, func=AF.Identity), R, W)

    def vcopy(self, out, in_, R, W, eng="dve"):
        self.S.op(eng, lambda e: e.tensor_copy(out=out, in_=in_), R, W)

    def tt(self, out, in0, in1, op, R, W, eng="dve"):
        self.S.op(eng, lambda e: e.tensor_tensor(out=out, in0=in0, in1=in1, op=op), R, W)

    def ts(self, out, in0, s1, s2, op0, op1, R, W, eng="dve"):
        if s2 is None:
            self.S.op(eng, lambda e: e.tensor_scalar(out=out, in0=in0, scalar1=s1, scalar2=None, op0=op0), R, W)
        else:
            self.S.op(eng, lambda e: e.tensor_scalar(out=out, in0=in0, scalar1=s1, scalar2=s2, op0=op0, op1=op1), R, W)

    def stt(self, out, in0, scalar, in1, op0, op1, R, W, eng="dve"):
        self.S.op(eng, lambda e: e.scalar_tensor_tensor(out=out, in0=in0, scalar=scalar, in1=in1, op0=op0, op1=op1), R, W)

    def memset(self, ap, val, R, W, eng="pool"):
        self.S.op(eng, lambda e: e.memset(ap, val), R, W)

    def dma(self, out, in_, R, W, q="sp", slow=False):
        if slow:
            self.S.dma(q, lambda e: e.dma_start(out=out, in_=in_, allow_slow_non_contiguous=True), R, W)
        else:
            self.S.dma(q, lambda e: e.dma_start(out=out, in_=in_), R, W)

    def rstd(self, ss, n, inv_d, r_ss):
        k = self.nti % 4
        self.nti += 1
        y = self.nt_y[:, 16 * k:16 * k + n]
        t = self.nt_t[:, 16 * k:16 * k + n]
        r = self.r_nt[k]
        x = ss
        rl = list(r_ss) if isinstance(r_ss, (list, tuple)) else [r_ss]
        self.ts(x, x, inv_d, EPS, ALU.mult, ALU.add, rl, rl)
        self.ts(y.bitcast(I32), x.bitcast(I32), 1, None, ALU.arith_shift_right, None, rl, [r])
        self.ts(y.bitcast(I32), y.bitcast(I32), -1, 0x5F3759DF, ALU.mult, ALU.add, [r], [r])
        for _ in range(2):
            self.stt(t, y, -0.5, y, ALU.mult, ALU.mult, [r], [r])
            self.tt(t, t, x, ALU.mult, [r] + rl, [r])
            self.stt(y, t, 1.5, y, ALU.add, ALU.mult, [r], [r])
        return y, r

    def build(self):
        nc = self.nc
        gst = self.gst
        S = self.S = Sched(nc, gst)
        ph = self.phases

        x_p = self.din("x_prompt", [T, D])
        x_s = self.din("x_sample", [NS, D])
        c_ak = self.din("cache_a_k", [512, D])
        c_av = self.din("cache_a_v", [512, D])
        c_bc = self.din("cache_b_ckv", [PAST, 512])
        c_br = self.din("cache_b_krope", [PAST, 64])
        a_ln = self.din("a_ln", [D])
        w_a_in = self.din("w_a_in", [D, 4 * D])
        a_qn = self.din("a_q_norm", [128])
        a_kn = self.din("a_k_norm", [128])
        a_rb = self.din("a_rel_bias", [16, 257])
        w_a_out = self.din("w_a_out", [D, D])
        b_ln = self.din("b_ln", [D])
        w_b_in = self.din("w_b_in", [D, BIN])
        b_qa = self.din("b_q_a_norm", [512])
        w_b_uq = self.din("w_b_uq", [512, 3072])
        b_kva = self.din("b_kv_a_norm", [512])
        w_b_uk = self.din("w_b_uk", [512, D])
        w_b_uv = self.din("w_b_uv", [512, D])
        b_qnn = self.din("b_q_nope_norm", [128])
        b_knn = self.din("b_k_nope_norm", [128])
        b_qrn = self.din("b_q_rope_norm", [64])
        b_krn = self.din("b_k_rope_norm", [64])
        w_b_out = self.din("w_b_out", [D, D])
        rope_cs = self.din("rope_cs", [T + 128, 64])

        y_p = self.dout("y_prompt", [T, D])
        y_s = self.dout("y_sample", [NS, D])
        o_akp = self.dout("new_a_k_prompt", [512, D])
        o_avp = self.dout("new_a_v_prompt", [512, D])
        o_bcp = self.dout("new_b_ckv_prompt", [T, 512])
        o_brp = self.dout("new_b_krope_prompt", [T, 64])
        o_aks = self.dout("new_a_k_sample", [NS, D])
        o_avs = self.dout("new_a_v_sample", [NS, D])
        o_bcs = self.dout("new_b_ckv_sample", [NS, 512])
        o_brs = self.dout("new_b_krope_sample", [NS, 64])

        TT = T + 128
        wa_in_bf = self.dscr("wa_in_bf", [16, 128, 16, 512], BF16)
        wa_out_bf = self.dscr("wa_out_bf", [4, 128, 16, 512], BF16)
        wb_in_bf = self.dscr("wb_in_bf", [7, 128, 16, 512], BF16)
        wb_out_bf = self.dscr("wb_out_bf", [4, 128, 16, 512], BF16)
        y1 = self.dscr("y1", [TT, D], F32)
        self.dbgA = (ph == "A")
        sgT_d = self.dscr("sgT_d", [D, TT], BF16)
        ogT_d = self.dscr("ogT_d", [D, TT], BF16)
        ext_d = self.dscr("ext_d", [16, 384], F32)
        self.wres = {}
        self.wdone = set()

        ident_bf = self.sb(gst, "ident_bf", [128, 128], BF16)
        ident_f = self.sb(gst, "ident_f", [128, 128], F32)
        ones_bf = self.sb(gst, "ones_bf", [128, 128], BF16)
        self.nt_y = self.sb(gst, "nt_y", [128, 64], F32)
        self.nt_t = self.sb(gst, "nt_t", [128, 64], F32)
        self.r_nt = [Res() for _ in range(4)]
        self.nti = 0
        r_c = Res()
        gcolA = self.sb(gst, "gcolA", [128, 16], F32)
        gcolB = self.sb(gst, "gcolB", [128, 16], F32)
        gq_col = self.sb(gst, "gq_col", [128, 1], F32)
        gk_col = self.sb(gst, "gk_col", [128, 1], F32)
        gk_rep = self.sb(gst, "gk_rep", [128, 128], F32)
        gqn_col = self.sb(gst, "gqn_col", [128, 1], F32)
        gkn_col = self.sb(gst, "gkn_col", [128, 1], F32)
        gqa_rep = self.sb(gst, "gqa_rep", [128, 512], F32)
        gkva_rep = self.sb(gst, "gkva_rep", [128, 512], F32)
        gqr_rep = self.sb(gst, "gqr_rep", [128, 64], F32)
        gkr_rep = self.sb(gst, "gkr_rep", [128, 64], F32)
        cb = self.sb(gst, "cb", [128, 16], F32)
        self.stEB = ExitStack()
        EB = self.sb(self.stEB, "EB", [128, 16, 2, 128], BF16)

        PJ = [self.ps(gst, f"pj{i}") for i in range(2)]
        TRB = [self.ps(gst, f"trb{i}") for i in range(2)]
        STB = [self.ps(gst, f"st{i}") for i in range(2)]
        OTB = self.ps(gst, "otb")
        SUMB = self.ps(gst, "sumb")
        r_PJ = [Res(True), Res(True)]
        r_TRB = [Res(True), Res(True)]
        r_ST = [Res(True), Res(True)]
        r_OT = Res(True)
        r_SUM = Res(True)
        self.pji = 0
        self.tri = 0

        self.pj_pool = [(PJ[0], r_PJ[0]), (PJ[1], r_PJ[1])]
        pool2 = list(self.pj_pool)
        pool6 = pool2 + [(STB[0], r_ST[0]), (STB[1], r_ST[1]), (OTB, r_OT), (SUMB, r_SUM)]

        def next_pj():
            i = self.pji % len(self.pj_pool)
            self.pji += 1
            return self.pj_pool[i]

        def next_tr():
            i = self.tri
            self.tri ^= 1
            return TRB[i], r_TRB[i]

        with ExitStack() as st:
            J = self.sb(st, "J", [128, 128], F32)
            tmp16 = self.sb(st, "tmp16", [16, 128], F32)
            tmp16b = self.sb(st, "tmp16b", [16, 128], F32)
            e_sb = self.sb(st, "e_sb", [16, 384], F32)
            hk = self.sb(st, "hk", [128, 16, 2, 128], F32)
            ebf = self.sb(st, "ebf", [128, 128], F32)
            cbrow = self.sb(st, "cbrow", [16, 1], F32)
            diagc = self.sb(st, "diagc", [16, 16], F32)
            ones_f = self.sb(st, "ones_f", [16, 128], F32)
            cneg = self.sb(st, "cneg", [128, 16], F32)
            r_J, r_t16, r_esb, r_hk, r_ebf, r_cbrow, r_ext = Res(), Res(), Res(), Res(), Res(), Res(), Res()
            self.memset(ident_f[:], 0.0, [], [r_c])
            S.op("pool", lambda e: e.affine_select(out=ident_f[:], in_=ident_f[:], pattern=[[-1, 128]],
                                                   compare_op=ALU.not_equal, fill=1.0, base=0, channel_multiplier=1), [r_c], [r_c])
            self.vcopy(ident_bf[:], ident_f[:], [r_c], [r_c])
            self.memset(ones_bf[:], 1.0, [], [r_c])
            self.memset(ones_f[:], 1.0, [], [r_cbrow])
            self.memset(J[:], 0.0, [], [r_J])
            S.op("pool", lambda e: e.affine_select(out=J[:], in_=J[:], pattern=[[1, 128]],
                                                   compare_op=ALU.not_equal, fill=1.0, base=-127, channel_multiplier=1), [r_J], [r_J])
            for (src, dst) in ((a_ln, gcolA), (b_ln, gcolB)):
                tb = tmp16 if dst is gcolA else tmp16b
                self.dma(tb[:], src.rearrange("(kt p) -> kt p", p=128), [], [r_t16])
                pj, rpj = next_pj()
                self.tr(pj[:, 0:16], tb[:], ident_f[0:16, 0:16], [r_t16, r_c], [rpj])
                self.acopy(dst[:], pj[:, 0:16], [rpj], [r_c])
            for (src, dst) in ((a_qn, gq_col), (a_kn, gk_col), (b_qnn, gqn_col), (b_knn, gkn_col)):
                self.dma(dst[:], src.rearrange("(p o) -> p o", o=1), [], [r_c], slow=True)
            self.ts(gq_col[:], gq_col[:], A_SCALE, None, ALU.mult, None, [r_c], [r_c])
            for (src, dst, n) in ((a_kn, gk_rep, 128), (b_qa, gqa_rep, 512), (b_kva, gkva_rep, 512),
                                  (b_qrn, gqr_rep, 64), (b_krn, gkr_rep, 64)):
                self.dma(dst[:], src.partition_broadcast(128), [], [r_c])
            self.dma(e_sb[:, 0:257], a_rb, [], [r_esb])
            self.vcopy(e_sb[:, 257:384], e_sb[:, 256:257].to_broadcast([16, 127]), [r_esb], [r_esb])
            self.dma(ext_d, e_sb[:], [r_esb], [r_ext])
            tab_t = a_rb.tensor
            ext_t = ext_d.tensor
            for h in range(16):
                self.dma(hk[:, h, 0, :], bass.AP(ext_t, h * 384 + 129, [[1, 128], [1, 128]]), [r_ext], [r_hk])
                self.dma(hk[:, h, 1, :], bass.AP(ext_t, h * 384 + 1, [[1, 128], [1, 128]]), [r_ext], [r_hk])
            self.dma(cbrow[:], bass.AP(tab_t, 256, [[257, 16], [1, 1]]), [], [r_cbrow], slow=True)
            self.ts(diagc[:], ident_f[0:16, 0:16], cbrow[:, 0:1], None, ALU.mult, None, [r_c, r_cbrow], [r_cbrow])
            pj, rpj = next_pj()
            self.mm(pj[:, 0:16], ones_f[:, :], diagc[:, :], True, True, [r_cbrow], [rpj])
            self.acopy(cb[:], pj[:, 0:16], [rpj], [r_c])
            self.ts(cneg[:], cb[:], -1.0, None, ALU.mult, None, [r_c], [r_cbrow])
            for h in range(16):
                for t in range(2):
                    self.act(ebf[:], hk[:, h, t, :], AF.Exp, [r_hk, r_cbrow], [r_ebf], bias=cneg[:, h:h + 1])
                    pj, rpj = next_pj()
                    self.mm(pj[:, 0:128], J[:], ebf[:], True, True, [r_J, r_ebf], [rpj])
                    self.vcopy(EB[:, h, t, :], pj[:, 0:128], [rpj], [r_c])
            self.memset(EB[64:128, :, 1, 0:64], 0.0, [r_c], [r_c], eng="dve")
            S.flush()

        if ph == "setup":
            self._finish(S)
            return nc

        def wsrc(wid, j):
            if wid == "a_in":
                return w_a_in[:, 512 * j:512 * j + 512].rearrange("(kt p) c -> p kt c", p=128), 512
            if wid == "a_out":
                return w_a_out[:, 512 * j:512 * j + 512].rearrange("(kt p) c -> p kt c", p=128), 512
            if wid == "b_out":
                return w_b_out[:, 512 * j:512 * j + 512].rearrange("(kt p) c -> p kt c", p=128), 512
            if wid == "b_in":
                if j < 2:
                    return w_b_in[:, 512 * j:512 * j + 512].rearrange("(kt p) c -> p kt c", p=128), 512
                if j == 2:
                    return w_b_in[:, 1024:1088].rearrange("(kt p) c -> p kt c", p=128), 64
                c0 = 1088 + 512 * (j - 3)
                return w_b_in[:, c0:c0 + 512].rearrange("(kt p) c -> p kt c", p=128), 512
            raise ValueError(wid)

        wscr = {"a_in": wa_in_bf, "a_out": wa_out_bf, "b_in": wb_in_bf, "b_out": wb_out_bf}

        class WStream:
            def __init__(s2, bufs, rbufs):
                s2.bufs = bufs
                s2.rb = rbufs
                s2.i = 0
                s2.q = []

            def prefetch(s2, wid, j):
                k = s2.i
                s2.i ^= 1
                buf, rb = s2.bufs[k], s2.rb[k]
                key = (wid, j)
                if key not in self.wres:
                    self.wres[key] = Res()
                rw = self.wres[key]
                src, ncol = wsrc(wid, j)
                if key in self.wdone:
                    self.dma(buf[:, :, 0:ncol], wscr[wid][j, :, :, 0:ncol], [rw], [rb])
                else:
                    self.dma(buf[:, :, 0:ncol], src, [], [rb], q="pool")
                    self.dma(wscr[wid][j, :, :, 0:ncol], buf[:, :, 0:ncol], [rb], [rw], q="pool")
                    self.wdone.add(key)
                s2.q.append((buf, rb))

            def pop(s2):
                return s2.q.pop(0)

        def prep_load(st_bufs, src, nreal):
            xin, r_xin, xns, r_xns, ss1, r_ss1 = st_bufs
            if nreal < 128:
                self.memset(xin[:], 0.0, [], [r_xin], eng="dve")
            self.dma(xin[0:nreal, :], src, [], [r_xin])

        def prep_norm(st_bufs):
            xin, r_xin, xns, r_xns, ss1, r_ss1 = st_bufs
            k = self.xni % 2
            self.xni += 1
            xn, r_xn = xns[k], r_xns[k]
            self.act(xn[:], xin[:], AF.Square, [r_xin], [r_xn, r_ss1], accum=ss1[:, 0:1])
            ry_, rr_ = self.rstd(ss1[:, 0:1], 1, 1.0 / D, r_ss1)
            self.act(xn[:], xin[:], AF.Identity, [r_xin, rr_], [r_xn], scale=ry_[:, 0:1])
            return xn, r_xn

        def prep_stage1(st_bufs, src, nreal):
            prep_load(st_bufs, src, nreal)
            return prep_norm(st_bufs)

        def prep_stage2(xn, r_xn, gcol, hT, col0, rblk):
            for half in range(2):
                trb, rtr = next_tr()
                tb = trb[:].bitcast(BF16)
                for j in range(8):
                    kt = half * 8 + j
                    self.tr(tb[:, j * 128:(j + 1) * 128], xn[:, kt * 128:(kt + 1) * 128], ident_bf[:], [r_xn, r_c], [rtr])
                self.tt(hT[:, half * 8:half * 8 + 8, col0:col0 + 128],
                        tb[:, 0:1024].rearrange("p (a b) -> p a b", a=8),
                        gcol[:, half * 8:half * 8 + 8].unsqueeze(2).to_broadcast([128, 8, 128]),
                        ALU.mult, [rtr, r_c], [rblk])

        def prep_hT(st_bufs, blocks, gcol, hT, r_hT):
            for (src, nreal, col0, rblk) in blocks:
                xn, r_xn = prep_stage1(st_bufs, src, nreal)
                prep_stage2(xn, r_xn, gcol, hT, col0, rblk)

        def attention(nq, tiles, qparts, kfn, vfn, escale, ebias, post, sgT_ap, r_sg, out_ap, r_out, rs_bufs, after_epi=None):
            PTs, r_PTs, rs, r_rs, tmp, r_tmp = rs_bufs
            n = len(tiles)
            pend = None
            if self.acci % 2 == 0:
                OTB_, r_OT_, SUMB_, r_SUM_ = PJ[0], r_PJ[0], PJ[1], r_PJ[1]
            else:
                OTB_, r_OT_, SUMB_, r_SUM_ = OTB, r_OT, SUMB, r_SUM
            self.acci += 1
            for i, (tid, c0, c1) in enumerate(tiles):
                stb, rst = st_pool[self.sti % 4]
                self.sti += 1
                kparts = kfn(tid)
                for pi, ((kap, rk), (qap, rq)) in enumerate(zip(kparts, qparts)):
                    self.mm(stb[:, c0:c1], kap, qap[:, c0:c1], pi == 0, pi == len(kparts) - 1, [rk, rq], [rst])
                pt, rpt = PTs[self.pti % 3], r_PTs[self.pti % 3]
                self.pti += 1
                kw = {}
                self.act(pt[:, c0:c1], stb[:, c0:c1], AF.Exp, [rst] + ([r_c] if ebias is not None else []), [rpt],
                         bias=ebias, scale=escale)
                post(tid, pt, rpt, c0, c1)
                if pend is not None:
                    pend()
                if i == min(2, n - 1) and self.pend_epi is not None:
                    self.pend_epi()
                    self.pend_epi = None
                vap, rv = vfn(tid)

                def pv(i=i, pt=pt, rpt=rpt, c0=c0, c1=c1, vap=vap, rv=rv):
                    self.mm(OTB_[:, c0:c1], vap, pt[:, c0:c1], i == 0, i == n - 1, [rv, rpt], [r_OT_])
                    self.mm(SUMB_[:, c0:c1], ones_bf[:], pt[:, c0:c1], i == 0, i == n - 1, [r_c, rpt], [r_SUM_])
                pend = pv
            pend()

            def epi():
                S.op("dve", lambda e: e.reciprocal(out=rs[:, 0:nq], in_=SUMB_[:, 0:nq]), [r_SUM_], [r_rs])
                self.stt(rs[:, 0:nq], OTB_[:, 0:nq], 0.5, rs[:, 0:nq], ALU.mult, ALU.mult, [r_OT_, r_rs], [r_rs])
                self.tt(out_ap, rs[:, 0:nq], sgT_ap, ALU.mult, [r_rs, r_sg], [r_out])
                if after_epi is not None:
                    after_epi()
            if self.pend_epi is not None:
                self.pend_epi()
            self.pend_epi = epi

        def flush_epi():
            if self.pend_epi is not None:
                self.pend_epi()
                self.pend_epi = None

        self.pti = 0
        self.gi = 0
        self.acci = 0
        self.sti = 0
        st_pool = [(STB[0], r_ST[0]), (STB[1], r_ST[1]), (TRB[0], r_TRB[0]), (TRB[1], r_TRB[1])]
        self.pend_epi = None
        self.xni = 0

        def sigmoid_gate(pj, rpj, nq, out_ap, r_out, gbufs):
            k = self.gi % 2
            self.gi += 1
            ge, r_ge, gc, r_gc = gbufs[0][k], gbufs[1][k], gbufs[2][k], gbufs[3][k]
            self.act(ge[:, 0:nq], pj[:, 0:nq], AF.Tanh, [rpj], [r_ge], scale=0.5)
            self.act(gc[:, 0:nq], pj[:, 0:nq], AF.Identity, [rpj], [r_gc])
            self.stt(out_ap, ge[:, 0:nq], 1.0, gc[:, 0:nq], ALU.add, ALU.mult, [r_ge, r_gc], [r_out])

        def out_proj(ws, wid, actT, r_act, nb, res_src, y_dst, nreal_last, obufs, after_chunk=None, store_q="pool"):
            ytile, r_y, xre, r_x = obufs
            ws.prefetch(wid, 0)
            k = 0
            for c in range(4):
                if c + 1 < 4:
                    ws.prefetch(wid, c + 1)
                wc, rwc = ws.pop()
                for b in range(nb):
                    nr = nreal_last if b == nb - 1 else 128
                    pj, rpj = next_pj()
                    for kt in range(16):
                        self.mm(pj[:, :], actT[:, kt, 128 * b:128 * b + 128], wc[:, kt, :], kt == 0, kt == 15,
                                [r_act, rwc], [rpj])
                    yt, ry, xr, rx = ytile[k % 2], r_y[k % 2], xre[k % 2], r_x[k % 2]
                    k += 1
                    self.dma(xr[0:nr, :], res_src[128 * b:128 * b + nr, 512 * c:512 * c + 512], [], [rx])
                    self.tt(yt[0:nr, :], pj[0:nr, :], xr[0:nr, :], ALU.add, [rpj, rx], [ry])
                    self.dma(y_dst[128 * b:128 * b + nr, 512 * c:512 * c + 512], yt[0:nr, :], [ry], [], q=store_q)
                if after_chunk is not None:
                    after_chunk(c)

        with ExitStack() as st:
            hT = self.sb(st, "hT", [128, 16, 512], BF16)
            r_hT = [Res() for _ in range(4)]
            Wc = [self.sb(st, f"wc{i}", [128, 16, 512], BF16) for i in range(2)]
            ws = WStream(Wc, [Res(), Res()])
            xin = self.sb(st, "xin", [128, D], F32)
            xn = [self.sb(st, f"xn{i}", [128, D], BF16) for i in range(2)]
            ss1 = self.sb(st, "ss1", [128, 1], F32)
            hbufs = (xin, Res(), xn, [Res(), Res()], ss1, Res())
            qT = [self.sb(st, f"qT{i}", [128, 4, 512], BF16) for i in range(2)]
            r_qT = [[Res() for _ in range(4)] for _ in range(2)]
            sgT = [self.sb(st, f"sgT{i}", [128, 4, 512], BF16) for i in range(2)]
            r_sgT = [[Res() for _ in range(4)] for _ in range(2)]
            kTr = self.sb(st, "kTr", [128, 16, 1024], BF16)
            r_kT = [[Res() for _ in range(2)] for _ in range(16)]
            Vr = self.sb(st, "Vr", [128, 8, D], BF16)
            r_V = [[Res() for _ in range(4)] for _ in range(8)]
            PTs = [self.sb(st, f"pt{i}", [128, 512], BF16) for i in range(3)]
            rsb = self.sb(st, "rsb", [128, 512], F32)
            abufs = (PTs, [Res(), Res(), Res()], rsb, Res(), None, None)
            gbufs = ([self.sb(st, f"ge{i}", [128, 512], F32) for i in range(2)], [Res(), Res()],
                     [self.sb(st, f"gc{i}", [128, 512], BF16) for i in range(2)], [Res(), Res()])
            ogT = self.sb(st, "ogT", [128, 16, 512], BF16)
            r_ogT = Res()
            ytile = [self.sb(st, f"ytile{i}", [128, 512], F32) for i in range(2)]
            xre = [self.sb(st, f"xre{i}", [128, 512], F32) for i in range(2)]
            obufs = (ytile, [Res(), Res()], xre, [Res(), Res()])
            qn = [self.sb(st, f"qn{i}", [128, 512], BF16) for i in range(4)]
            r_qn = [Res(), Res(), Res(), Res()]
            kf = [self.sb(st, f"kf{i}", [128, 512], F32) for i in range(2)]
            r_kf = [Res(), Res()]
            ss4 = [self.sb(st, f"ss4{i}", [128, 4], F32) for i in range(2)]
            r_ss4 = [[Res() for _ in range(4)] for _ in range(2)]
            sqj = self.sb(st, "sqj", [128, 4, 128], BF16)
            r_sqj = [Res() for _ in range(4)]
            vst = [self.sb(st, f"vst{i}", [128, 512], F32) for i in range(2)]
            r_vst = [Res(), Res()]
            self.qni = 0
            self.s4i = 0
            self.kfi = 0
            self.vsi = 0

            def a_prep(blocks, only=None):
                for b, (src, nreal) in enumerate(blocks):
                    if only is None or only == b:
                        prep_hT(hbufs, [(src, nreal, 128 * b, r_hT[b])], gcolA, hT, None)

            def a_superblock(blocks, hf, has_prev, y1_rows, kout, vout, x_rows, next_blocks=None):
                nb = len(blocks)
                ntok = 128 * nb
                pf = 1 - hf
                r_h = r_hT[0:nb]
                self.stopat("hT")
                tails = []

                def run_tails(keep=0):
                    while len(tails) > keep:
                        tails.pop(0)()

                order = []
                for G in range(4):
                    order += [("q", G), ("k", G), ("v", G), ("g", G)]
                cidx = {"q": 0, "k": 4, "v": 8, "g": 12}
                ws.prefetch("a_in", cidx[order[0][0]] + order[0][1])
                for oi, (kind, G) in enumerate(order):
                    if oi + 1 < len(order):
                        ws.prefetch("a_in", cidx[order[oi + 1][0]] + order[oi + 1][1])
                    wc, rwc = ws.pop()
                    par = G % 2
                    if oi == 1: self.stopat("q")
                    if oi == 2: self.stopat("k")
                    if oi == 3: self.stopat("v")
                    if kind in ("q", "k"):
                        for b in range(nb):
                            pj, rpj = next_pj()
                            for kt in range(16):
                                self.mm(pj[:, :], hT[:, kt, 128 * b:128 * b + 128], wc[:, kt, :], kt == 0, kt == 15,
                                        [r_h[b], rwc], [rpj])
                            s4, rs4 = ss4[self.s4i % 2], r_ss4[self.s4i % 2]
                            self.s4i += 1
                            for hh in range(4):
                                self.act(sqj[:, hh, :], pj[:, 128 * hh:128 * hh + 128], AF.Square, [rpj], [r_sqj[hh], rs4[hh]],
                                         accum=s4[:, hh:hh + 1])
                            ry_, rr_ = self.rstd(s4[:, 0:4], 4, 1.0 / 128, rs4)
                            qb, rqb = qn[self.qni % 4], r_qn[self.qni % 4]
                            self.qni += 1
                            self.tt(qb[:].rearrange("p (a b) -> p a b", a=4), pj[:].rearrange("p (a b) -> p a b", a=4),
                                    ry_[:, 0:4].unsqueeze(2).to_broadcast([128, 4, 128]), ALU.mult, [rpj, rr_], [rqb])
                            if kind == "k" and kout is not None:
                                kfb, rkf = kf[self.kfi % 2], r_kf[self.kfi % 2]
                                self.kfi += 1
                                self.tt(kfb[:].rearrange("p (a b) -> p a b", a=4), pj[:].rearrange("p (a b) -> p a b", a=4),
                                        ry_[:, 0:4].unsqueeze(2).to_broadcast([128, 4, 128]), ALU.mult, [rpj, rr_], [rkf])
                                self.tt(kfb[:].rearrange("p (a b) -> p a b", a=4), kfb[:].rearrange("p (a b) -> p a b", a=4),
                                        gk_rep[:, :].unsqueeze(1).to_broadcast([128, 4, 128]), ALU.mult, [rkf, r_c], [rkf])
                                nr = blocks[b][1]
                                self.dma(kout[128 * b:128 * b + nr, 512 * G:512 * G + 512], kfb[0:nr, :], [rkf], [], q="pool")

                            def tail(kind=kind, G=G, b=b, qb=qb, rqb=rqb, par=par):
                                trb, rtr = next_tr()
                                tb = trb[:].bitcast(BF16)
                                for hh in range(4):
                                    self.tr(tb[:, hh * 128:(hh + 1) * 128], qb[:, 128 * hh:128 * hh + 128], ident_bf[:],
                                            [rqb, r_c], [rtr])
                                src3 = tb[:, 0:512].rearrange("p (a b) -> p a b", a=4)
                                if kind == "q":
                                    dst = qT[par][:, 0:4, 128 * b:128 * b + 128]
                                    self.act(dst, src3, AF.Identity, [rtr, r_c], [r_qT[par][hh] for hh in range(4)],
                                             scale=gq_col[:, 0:1])
                                else:
                                    dst = kTr[:, 4 * G:4 * G + 4, 512 * hf + 128 * b:512 * hf + 128 * b + 128]
                                    self.act(dst, src3, AF.Identity, [rtr, r_c], [r_kT[4 * G + hh][hf] for hh in range(4)],
                                             scale=gk_col[:, 0:1])
                            run_tails(1)
                            tails.append(tail)
                    elif kind == "v":
                        for b in range(nb):
                            pj, rpj = next_pj()
                            for kt in range(16):
                                self.mm(pj[:, :], hT[:, kt, 128 * b:128 * b + 128], wc[:, kt, :], kt == 0, kt == 15,
                                        [r_h[b], rwc], [rpj])
                            self.acopy(Vr[:, 4 * hf + b, 512 * G:512 * G + 512], pj[:, :], [rpj], [r_V[4 * hf + b][G]])
                            if vout is not None:
                                vb, rvb = vst[self.vsi % 2], r_vst[self.vsi % 2]
                                self.vsi += 1
                                self.vcopy(vb[:], pj[:, :], [rpj], [rvb])
                                nr = blocks[b][1]
                                self.dma(vout[128 * b:128 * b + nr, 512 * G:512 * G + 512], vb[0:nr, :], [rvb], [], q="pool")
                            run_tails()
                    else:
                        for hh in range(4):
                            pj, rpj = next_pj()
                            for kt in range(16):
                                self.mm(pj[:, 0:ntok], wc[:, kt, 128 * hh:128 * hh + 128], hT[:, kt, 0:ntok], kt == 0, kt == 15,
                                        r_h + [rwc], [rpj])
                            sigmoid_gate(pj, rpj, ntok, sgT[par][:, hh, 0:ntok], r_sgT[par][hh], gbufs)
                            run_tails()
                        run_tails()
                        self.stopat("g")
                        for hh in range(4):
                            h = 4 * G + hh
                            tl = []
                            for t in [4, 3, 5, 2, 6, 1, 7, 0]:
                                if t < 4 and not has_prev:
                                    continue
                                if t >= 4 and t - 4 >= nb:
                                    continue
                                b0 = max(0, t - 4)
                                b1 = min(nb - 1, t)
                                tl.append((t, 128 * b0, 128 * (b1 + 1)))

                            def kfn(t, h=h):
                                half = pf if t < 4 else hf
                                c = 512 * half + 128 * (t % 4)
                                return [(kTr[:, h, c:c + 128], r_kT[h][half])]

                            def vfn(t, h=h, G=G):
                                slot = 4 * (pf if t < 4 else hf) + (t % 4)
                                return (Vr[:, slot, 128 * h:128 * h + 128], r_V[slot][G])

                            def post(t, pt, rpt, c0, c1, h=h):
                                b = t - 3
                                if 0 <= b < nb:
                                    self.tt(pt[:, 128 * b:128 * b + 128], pt[:, 128 * b:128 * b + 128], EB[:, h, 0, :],
                                            ALU.mult, [rpt, r_c], [rpt], eng=EB_ENG)
                                b = t - 4
                                if 0 <= b < nb:
                                    self.tt(pt[:, 128 * b:128 * b + 128], pt[:, 128 * b:128 * b + 128], EB[:, h, 1, :],
                                            ALU.mult, [rpt, r_c], [rpt], eng=EB_ENG)
                                b = t
                                if 0 <= b < nb:
                                    self.memset(pt[0:64, 128 * b + 64:128 * b + 128], 0.0, [rpt], [rpt])

                            attention(ntok, tl, [(qT[par][:, hh, :], r_qT[par][hh])], kfn, vfn, None, cb[:, h:h + 1], post,
                                      sgT[par][:, hh, 0:ntok], r_sgT[par][hh], ogT[:, h, 0:ntok], r_ogT, abufs)
                        flush_epi()
                self.stopat("attn")
                staged = {}

                def stage1(c):
                    if next_blocks is not None and c < len(next_blocks):
                        staged[c] = prep_stage1(hbufs, next_blocks[c][0], next_blocks[c][1])

                def after_chunk(c):
                    if c in staged:
                        xn_, rxn_ = staged.pop(c)
                        prep_stage2(xn_, rxn_, gcolA, hT, 128 * c, r_hT[c])
                    stage1(c + 1)
                stage1(0)
                out_proj(ws, "a_out", ogT, r_ogT, nb, x_rows, y1_rows, blocks[-1][1], obufs, after_chunk)

            try:
              pblocks = lambda s: [(x_p[512 * s + 128 * b:512 * s + 128 * b + 128, :], 128) for b in range(4)]
              sblocks_ = [(x_s[0:NS, :], NS)]
              a_prep(pblocks(0))
              for s in range(T // 512):
                blocks = pblocks(s)
                last = (s == T // 512 - 1)
                if s >= 1 or T // 512 == 1:
                    todo = [(w_, j_) for (w_, n_) in (("b_in", 7), ("b_out", 4)) for j_ in range(n_) if (w_, j_) not in self.wdone]
                    for (wid, j) in (todo if last else todo[:2]):
                        key = (wid, j)
                        self.wres[key] = Res()
                        src, ncol = wsrc(wid, j)
                        for q4 in range(4):
                            self.dma(wscr[wid][j, :, 4 * q4:4 * q4 + 4, 0:ncol], src[:, 4 * q4:4 * q4 + 4, :], [], [self.wres[key]], q="pool")
                        self.wdone.add(key)
                a_superblock(blocks, s % 2, s > 0, (y_p if self.dbgA else y1)[512 * s:512 * s + 512, :],
                             o_akp if last else None, o_avp if last else None, x_p[512 * s:512 * s + 512, :],
                             sblocks_ if last else pblocks(s + 1))
              self.stopat("prompt")
              kc = Wc[0][:].rearrange("p a b -> p (a b)")
              r_kc = ws.rb[0]
              self.dma(kc.rearrange("p (t c) -> p t c", t=4), c_ak.rearrange("(t p) c -> p t c", p=128), [], [r_kc], q="pool")
              self.dma(Vr[:, 0:4, :], c_av.rearrange("(t p) c -> p t c", p=128), [], [r_V[t4][g] for t4 in range(4) for g in range(4)], q="pool")
              for t4 in range(4):
                  for G in range(4):
                      trb, rtr = next_tr()
                      tb = trb[:].bitcast(BF16)
                      for hh in range(4):
                          h = 4 * G + hh
                          self.tr(tb[:, hh * 128:(hh + 1) * 128], kc[:, 2048 * t4 + 128 * h:2048 * t4 + 128 * h + 128], ident_bf[:],
                                  [r_kc, r_c], [rtr])
                      self.acopy(kTr[:, 4 * G:4 * G + 4, 128 * t4:128 * t4 + 128], tb[:, 0:512].rearrange("p (a b) -> p a b", a=4),
                                 [rtr], [r_kT[4 * G + hh][0] for hh in range(4)])
              a_superblock(sblocks_, 1, True, y_s if self.dbgA else y1[T:T + 128, :], o_aks, o_avs, x_s[0:NS, :])
            except StopBuild:
                pass
            S.flush()

        self.stEB.close()
        if ph == "A":
            self._finish(S)
            return nc
        NTP = T // 128
        with ExitStack() as stB:
            cqT_p = self.sb(stB, "cqT_p", [128, 4, T], BF16)
            ckvT_p = self.sb(stB, "ckvT_p", [128, 4, T], BF16)
            krT_p = self.sb(stB, "krT_p", [128, T], BF16)
            cqT_s = self.sb(stB, "cqT_s", [128, 4, 128], BF16)
            ckvT_s = self.sb(stB, "ckvT_s", [128, 4, 1152], BF16)
            krT_s = self.sb(stB, "krT_s", [128, 1152], BF16)
            ropet = self.sb(stB, "ropet", [128, NTP + 1, 64], F32)
            ra = self.sb(stB, "rope_a", [128, 4, 32], F32)
            rb_ = self.sb(stB, "rope_b", [128, 4, 32], F32)
            r_rope = Res()
            self.s4i = 0

            def rope(src, nh, blk, dst, r_src, r_dst, eng=None):
                cos = ropet[:, blk, 0:32].unsqueeze(1).to_broadcast([128, nh, 32])
                sin = ropet[:, blk, 32:64].unsqueeze(1).to_broadcast([128, nh, 32])
                x1 = src[:, :, 0:32]
                x2 = src[:, :, 32:64]
                a = ra[:, 0:nh, :]
                b_ = rb_[:, 0:nh, :]
                E = eng or "dve"
                self.tt(a, x1, cos, ALU.mult, [r_src, r_c], [r_rope], eng=E)
                self.tt(b_, x2, sin, ALU.mult, [r_src, r_c, r_rope], [r_rope], eng=E)
                self.tt(dst[:, :, 0:32], a, b_, ALU.subtract, [r_rope], [r_dst], eng=E)
                self.tt(a, x1, sin, ALU.mult, [r_src, r_c, r_rope], [r_rope], eng=E)
                self.tt(b_, x2, cos, ALU.mult, [r_src, r_c, r_rope], [r_rope], eng=E)
                self.tt(dst[:, :, 32:64], a, b_, ALU.add, [r_rope], [r_dst], eng=E)

            r_cqT = [Res() for _ in range(NTP // 4 + 1)]
            r_ckvT = [Res() for _ in range(NTP // 4 + 3)]
            r_krT = [Res() for _ in range(NTP // 4 + 3)]
            SP0 = NTP // 4

            with ExitStack() as st:
                hTs = [self.sb(st, f"hT{i}", [128, 16, 512], BF16) for i in range(2)]
                r_hTs = [[Res() for _ in range(4)] for _ in range(2)]
                Wc = [self.sb(st, f"wc{i}", [128, 16, 512], BF16) for i in range(2)]
                ws = WStream(Wc, [Res(), Res()])
                xin = self.sb(st, "xin", [128, D], F32)
                xn = [self.sb(st, f"xn{i}", [128, D], BF16) for i in range(2)]
                ss1 = self.sb(st, "ss1", [128, 1], F32)
                hbufs = (xin, Res(), xn, [Res(), Res()], ss1, Res())
                gbufs = ([self.sb(st, f"ge{i}", [128, 512], F32) for i in range(2)], [Res(), Res()],
                         [self.sb(st, f"gc{i}", [128, 512], BF16) for i in range(2)], [Res(), Res()])
                cstage = self.sb(st, "cstage", [128, 8, 512], BF16)
                kstage = self.sb(st, "kstage", [128, 8, 64], BF16)
                r_cst = Res()
                nb16 = [self.sb(st, f"nb16{i}", [128, 512], BF16) for i in range(3)]
                r_nb16 = [Res(), Res(), Res()]
                cf = [self.sb(st, f"cf{i}", [128, 512], F32) for i in range(2)]
                r_cf = [Res(), Res()]
                ssb = [self.sb(st, f"ssb{i}", [128, 1], F32) for i in range(2)]
                r_ssb = [Res(), Res()]
                jk = self.sb(st, "jk", [128, 512], BF16)
                r_jk = Res()
                krn = [self.sb(st, f"krn{i}", [128, 1, 64], F32) for i in range(3)]
                kro = [self.sb(st, f"kro{i}", [128, 1, 64], F32) for i in range(3)]
                krb = [self.sb(st, f"krb{i}", [128, 64], BF16) for i in range(3)]
                r_krn = [Res() for _ in range(3)]
                r_kro = [Res() for _ in range(3)]
                r_krb = [Res() for _ in range(3)]
                sgs = [self.sb(st, f"sgs{i}", [128, 512], BF16) for i in range(2)]
                r_sgs = [Res(), Res()]
                self.cnt = 0

                r_krpad = Res()
                self.memset(krT_p[64:128, :], 0.0, [], [r_krpad], eng="dve")
                self.memset(krT_s[64:128, :], 0.0, [], [r_krpad], eng="dve")
                self.dma(ropet[:], rope_cs.rearrange("(b p) c -> p b c", p=128), [], [r_c])
                self.dma(cstage[:, :, :], c_bc.rearrange("(t p) c -> p t c", p=128), [], [r_cst], q="pool")
                self.dma(kstage[:, :, :], c_br.rearrange("(t p) c -> p t c", p=128), [], [r_cst], q="pool")
                for t8 in range(8):
                    trb, rtr = next_tr()
                    tb = trb[:].bitcast(BF16)
                    for kt in range(4):
                        self.tr(tb[:, kt * 128:(kt + 1) * 128], cstage[:, t8, 128 * kt:128 * kt + 128], ident_bf[:], [r_cst, r_c], [rtr])
                    self.acopy(ckvT_s[:, 0:4, 128 * t8:128 * t8 + 128], tb[:, 0:512].rearrange("p (a b) -> p a b", a=4),
                               [rtr], [r_ckvT[SP0 + t8 // 4]])
                    trb, rtr = next_tr()
                    tb = trb[:].bitcast(BF16)
                    self.tr(tb[0:64, 0:128], kstage[:, t8, :], ident_bf[:], [r_cst, r_c], [rtr])
                    self.acopy(krT_s[0:64, 128 * t8:128 * t8 + 128], tb[0:64, 0:128], [rtr], [r_krT[SP0 + t8 // 4]])

                def b1_superblock(blocks, cqT, cq_c0, r_cq, ckvT, kr_T, ck_c0, r_ck, r_kr, ckv_out, kr_out, sg_c0, blk0,
                                  hi, next_blocks):
                    nb = len(blocks)
                    ntok = 128 * nb
                    hT, r_hT = hTs[hi], r_hTs[hi]
                    hTn, r_hTn = hTs[1 - hi], r_hTs[1 - hi]
                    r_h = r_hT[0:nb]
                    staged = {}

                    def stage1_load(c):
                        if next_blocks is not None and c < len(next_blocks):
                            prep_load(hbufs, next_blocks[c][0], next_blocks[c][1])

                    def stage1_norm(c):
                        if next_blocks is not None and c < len(next_blocks):
                            staged[c] = prep_norm(hbufs)

                    def stage2(c):
                        if c in staged:
                            xn_, rxn_ = staged.pop(c)
                            prep_stage2(xn_, rxn_, gcolB, hTn, 128 * c, r_hTn[c])
                    tails = []

                    def run_tails(keep=0):
                        while len(tails) > keep:
                            tails.pop(0)()

                    ws.prefetch("b_in", 0)
                    for j in range(7):
                        if j + 1 < 7:
                            ws.prefetch("b_in", j + 1)
                        wc, rwc = ws.pop()
                        if j > 3:
                            stage2(j - 4)
                        if j >= 3:
                            stage1_load(j - 3)
                        if j < 2:
                            for b in range(nb):
                                nr = blocks[b][1]
                                pj, rpj = next_pj()
                                for kt in range(16):
                                    self.mm(pj[:, :], hT[:, kt, 128 * b:128 * b + 128], wc[:, kt, :], kt == 0, kt == 15, [r_h[b], rwc], [rpj])
                                k = self.cnt
                                self.cnt += 1
                                s1, rs1 = ssb[k % 2], r_ssb[k % 2]
                                self.act(jk[:], pj[:, :], AF.Square, [rpj], [r_jk, rs1], accum=s1[:, 0:1])
                                ry_, rr_ = self.rstd(s1[:, 0:1], 1, 1.0 / 512, rs1)
                                nbf, rnb = nb16[k % 3], r_nb16[k % 3]
                                if j == 0:
                                    self.stt(nbf[:], pj[:, :], ry_[:, 0:1], gqa_rep[:], ALU.mult, ALU.mult, [rpj, rr_, r_c], [rnb])
                                else:
                                    cfb, rcf = cf[k % 2], r_cf[k % 2]
                                    self.stt(cfb[:], pj[:, :], ry_[:, 0:1], gkva_rep[:], ALU.mult, ALU.mult, [rpj, rr_, r_c], [rcf])
                                    self.dma(ckv_out[128 * b:128 * b + nr, :], cfb[0:nr, :], [rcf], [], q="pool")

                                def tail(j=j, b=b, nbf=nbf, rnb=rnb, cfb=(cfb if j == 1 else None), rcf=(rcf if j == 1 else None)):
                                    if cfb is not None:
                                        self.acopy(nbf[:], cfb[:], [rcf], [rnb])
                                    trb, rtr = next_tr()
                                    tb = trb[:].bitcast(BF16)
                                    for kt in range(4):
                                        self.tr(tb[:, kt * 128:(kt + 1) * 128], nbf[:, 128 * kt:128 * kt + 128], ident_bf[:], [rnb, r_c], [rtr])
                                    src3 = tb[:, 0:512].rearrange("p (a b) -> p a b", a=4)
                                    if j == 0:
                                        self.acopy(cqT[:, 0:4, cq_c0 + 128 * b:cq_c0 + 128 * b + 128], src3, [rtr], [r_cq])
                                    else:
                                        self.acopy(ckvT[:, 0:4, ck_c0 + 128 * b:ck_c0 + 128 * b + 128], src3, [rtr], [r_ck])
                                run_tails()
                                tails.append(tail)
                        elif j == 2:
                            for b in range(nb):
                                nr = blocks[b][1]
                                pj, rpj = next_pj()
                                for kt in range(16):
                                    self.mm(pj[:, 0:64], hT[:, kt, 128 * b:128 * b + 128], wc[:, kt, 0:64], kt == 0, kt == 15, [r_h[b], rwc], [rpj])
                                k = self.cnt
                                self.cnt += 1
                                s1, rs1 = ssb[k % 2], r_ssb[k % 2]
                                self.act(jk[:, 0:64], pj[:, 0:64], AF.Square, [rpj], [r_jk, rs1], accum=s1[:, 0:1])
                                ry_, rr_ = self.rstd(s1[:, 0:1], 1, 1.0 / 64, rs1)
                                kn_, rkn = krn[k % 3], r_krn[k % 3]
                                ko_, rko = kro[k % 3], r_kro[k % 3]
                                kb_, rkb = krb[k % 3], r_krb[k % 3]
                                self.stt(kn_[:, 0, :], pj[:, 0:64], ry_[:, 0:1], gkr_rep[:], ALU.mult, ALU.mult, [rpj, rr_, r_c], [rkn])
                                rope(kn_, 1, blk0 + b, ko_, rkn, rko)
                                self.dma(kr_out[128 * b:128 * b + nr, :], ko_[0:nr, 0, :], [rko], [], q="pool")

                                def tail(b=b, kb_=kb_, rkb=rkb, ko_=ko_, rko=rko):
                                    self.acopy(kb_[:], ko_[:, 0, :], [rko], [rkb])
                                    trb, rtr = next_tr()
                                    tb = trb[:].bitcast(BF16)
                                    self.tr(tb[0:64, 0:128], kb_[:], ident_bf[:], [rkb, r_c], [rtr])
                                    self.acopy(kr_T[0:64, ck_c0 + 128 * b:ck_c0 + 128 * b + 128], tb[0:64, 0:128], [rtr], [r_kr])
                                run_tails(1)
                                tails.append(tail)
                        else:
                            for hh in range(4):
                                h = 4 * (j - 3) + hh
                                pj, rpj = next_pj()
                                for kt in range(16):
                                    self.mm(pj[:, 0:ntok], wc[:, kt, 128 * hh:128 * hh + 128], hT[:, kt, 0:ntok], kt == 0, kt == 15, r_h + [rwc], [rpj])
                                k = self.cnt
                                self.cnt += 1
                                sg_, rsg = sgs[k % 2], r_sgs[k % 2]
                                sigmoid_gate(pj, rpj, ntok, sg_[:, 0:ntok], rsg, gbufs)
                                self.dma(sgT_d[128 * h:128 * h + 128, sg_c0:sg_c0 + ntok], sg_[:, 0:ntok], [rsg], [], q="pool")
                                run_tails()
                            stage1_norm(j - 3)
                    run_tails()
                    stage2(3)

                try:
                    pbl = lambda s: [(y1[512 * s + 128 * b:512 * s + 128 * b + 128, :], 128) for b in range(4)]
                    sbl_ = [(y1[T:T + NS, :], NS)]
                    prep_hT(hbufs, [(src, nreal, 128 * b, r_hTs[0][b]) for b, (src, nreal) in enumerate(pbl(0))], gcolB, hTs[0], None)
                    nsb = NTP // 4
                    for s in range(nsb):
                        b1_superblock(pbl(s), cqT_p, 512 * s, r_cqT[s], ckvT_p, krT_p, 512 * s, r_ckvT[s], r_krT[s],
                                      o_bcp[512 * s:512 * s + 512, :], o_brp[512 * s:512 * s + 512, :], 512 * s, 4 * s,
                                      s % 2, pbl(s + 1) if s + 1 < nsb else sbl_)
                    self.stopat("b1prompt")
                    b1_superblock(sbl_, cqT_s, 0, r_cqT[SP0], ckvT_s, krT_s, 1024, r_ckvT[SP0 + 2], r_krT[SP0 + 2],
                                  o_bcs, o_brs, T, NTP, nsb % 2, None)
                except StopBuild:
                    pass
                S.flush()

            if ph == "B1":
                self._finish(S)
                return nc

            with ExitStack() as st:
                wuk = self.sb(st, "wuk", [128, 4, 512], BF16)
                wuv = self.sb(st, "wuv", [128, 4, 512], BF16)
                wuqn = self.sb(st, "wuqn", [128, 4, 512], BF16)
                wuqr = self.sb(st, "wuqr", [128, 4, 256], BF16)
                r_w = Res()
                kT_g = self.sb(st, "kT_g", [128, 4, max(T, 1152)], BF16)
                V_g = self.sb(st, "V_g", [128, max(NTP, 9), 512], BF16)
                r_kTg = [Res() for _ in range(max(NTP // 4, 3))]
                r_Vg = [Res() for _ in range(max(NTP // 4, 3))]
                r_qTr_pad = Res()
                r_qTr = [r_qTr_pad] * 2
                qTn = [self.sb(st, "qTn0", [128, 4, 512], BF16)] * 2
                qTr = [self.sb(st, "qTr0", [128, 4, 512], BF16)] * 2
                self.memset(qTr[0][64:128, :, :], 0.0, [], [r_qTr_pad], eng="dve")
                r_qTn = [Res()] * 2
                sgin = [self.sb(st, "sgin0", [128, 4, 512], BF16)] * 2
                r_sgin = [Res()] * 2
                PTs = [self.sb(st, f"pt{i}", [128, 512], BF16) for i in range(3)]
                rsb = self.sb(st, "rsb", [128, 512], F32)
                abufs = (PTs, [Res(), Res(), Res()], rsb, Res(), None, None)
                ogs = [self.sb(st, f"ogs{i}", [128, 512], BF16) for i in range(2)]
                r_ogs = [Res(), Res()]
                nb16 = [self.sb(st, f"nb16{i}", [128, 512], BF16) for i in range(4)]
                r_nb16 = [Res(), Res(), Res(), Res()]
                self.qri = 0
                qrn = [self.sb(st, f"qrn{i}", [128, 4, 64], F32) for i in range(2)]
                qro = [self.sb(st, f"qro{i}", [128, 4, 64], F32) for i in range(2)]
                qrb = [self.sb(st, f"qrb{i}", [128, 4, 64], BF16) for i in range(2)]
                r_qrn = [Res(), Res()]
                r_qro = [Res(), Res()]
                r_qrb = [Res(), Res()]
                self.cnt = 0
                self.par = 0

                ss12 = [self.sb(st, f"ss12_{i}", [128, 12], F32) for i in range(3)]
                r_ss12 = [[Res() for _ in range(12)] for _ in range(3)]
                sq12 = self.sb(st, "sq12", [128, 4, 128], BF16)
                r_sq12 = [Res() for _ in range(4)]
                self.s12i = 0

                def b2_seq(G, cqT, ckvT, kr_T, sblocks, og_base):
                    tails = []

                    def run_tails(keep=0):
                        while len(tails) > keep:
                            tails.pop(0)()

                    for si, (t0, nb, isq, r_cq, r_ck, r_kr, rblk0) in enumerate(sblocks):
                        ntok = 128 * nb
                        par = self.par
                        if isq:
                            self.par ^= 1
                        self.pj_pool = pool6
                        for b in range(nb):
                            c0 = t0 + 128 * b
                            q0 = 128 * b
                            qc = q0 + (t0 if cqT is cqT_p else 0)
                            s12, rs12 = ss12[self.s12i % 3], r_ss12[self.s12i % 3]
                            self.s12i += 1
                            pjk, rpjk = next_pj()
                            for kt in range(4):
                                self.mm(pjk[:, :], ckvT[:, kt, c0:c0 + 128], wuk[:, kt, :], kt == 0, kt == 3, [r_ck, r_w], [rpjk])
                            for hh in range(4):
                                self.act(sq12[:, hh, :], pjk[:, 128 * hh:128 * hh + 128], AF.Square, [rpjk], [r_sq12[hh], rs12[hh]],
                                         accum=s12[:, hh:hh + 1])
                            ncol = 4
                            if isq:
                                pjq, rpjq = next_pj()
                                for kt in range(4):
                                    self.mm(pjq[:, :], cqT[:, kt, qc:qc + 128], wuqn[:, kt, :], kt == 0, kt == 3, [r_cq, r_w], [rpjq])
                                for hh in range(4):
                                    self.act(sq12[:, hh, :], pjq[:, 128 * hh:128 * hh + 128], AF.Square, [rpjq],
                                             [r_sq12[hh], rs12[4 + hh]], accum=s12[:, 4 + hh:5 + hh])
                                pjr, rpjr = next_pj()
                                for kt in range(4):
                                    self.mm(pjr[:, 0:256], cqT[:, kt, qc:qc + 128], wuqr[:, kt, :], kt == 0, kt == 3, [r_cq, r_w], [rpjr])
                                for hh in range(4):
                                    self.act(sq12[:, hh, 0:64], pjr[:, 64 * hh:64 * hh + 64], AF.Square, [rpjr],
                                             [r_sq12[hh], rs12[8 + hh]], accum=s12[:, 8 + hh:9 + hh], scale=2.0 ** 0.5)
                                ncol = 12
                            pjv, rpjv = next_pj()
                            for kt in range(4):
                                self.mm(pjv[:, :], ckvT[:, kt, c0:c0 + 128], wuv[:, kt, :], kt == 0, kt == 3, [r_ck, r_w], [rpjv])
                            self.acopy(V_g[:, c0 // 128, :], pjv[:, :], [rpjv], [r_Vg[si]])
                            ry_, rr_ = self.rstd(s12[:, 0:ncol], ncol, 1.0 / 128, rs12[0:ncol])
                            k = self.cnt
                            self.cnt += 1
                            nbk, rnbk = nb16[k % 4], r_nb16[k % 4]
                            self.tt(nbk[:].rearrange("p (a b) -> p a b", a=4), pjk[:].rearrange("p (a b) -> p a b", a=4),
                                    ry_[:, 0:4].unsqueeze(2).to_broadcast([128, 4, 128]), ALU.mult, [rpjk, rr_], [rnbk])

                            def tailk(nbf=nbk, rnb=rnbk, c0=c0, si=si):
                                trb, rtr = next_tr()
                                tb = trb[:].bitcast(BF16)
                                for hh in range(4):
                                    self.tr(tb[:, hh * 128:(hh + 1) * 128], nbf[:, 128 * hh:128 * hh + 128], ident_bf[:], [rnb, r_c], [rtr])
                                self.act(kT_g[:, 0:4, c0:c0 + 128], tb[:, 0:512].rearrange("p (a b) -> p a b", a=4), AF.Identity,
                                         [rtr, r_c], [r_kTg[si]], scale=gkn_col[:, 0:1])
                            tails.append(tailk)
                            if isq:
                                k = self.cnt
                                self.cnt += 1
                                nbq, rnbq = nb16[k % 4], r_nb16[k % 4]
                                self.tt(nbq[:].rearrange("p (a b) -> p a b", a=4), pjq[:].rearrange("p (a b) -> p a b", a=4),
                                        ry_[:, 4:8].unsqueeze(2).to_broadcast([128, 4, 128]), ALU.mult, [rpjq, rr_], [rnbq])

                                def tailq(nbf=nbq, rnb=rnbq, q0=q0, par=par):
                                    trb, rtr = next_tr()
                                    tb = trb[:].bitcast(BF16)
                                    for hh in range(4):
                                        self.tr(tb[:, hh * 128:(hh + 1) * 128], nbf[:, 128 * hh:128 * hh + 128], ident_bf[:], [rnb, r_c], [rtr])
                                    self.act(qTn[par][:, 0:4, q0:q0 + 128], tb[:, 0:512].rearrange("p (a b) -> p a b", a=4), AF.Identity,
                                             [rtr, r_c], [r_qTn[par]], scale=gqn_col[:, 0:1])
                                tails.append(tailq)
                                kq = self.qri % 2
                                self.qri += 1
                                qn_, rqn = qrn[kq], r_qrn[kq]
                                qo_, rqo = qro[kq], r_qro[kq]
                                qb_, rqb = qrb[kq], r_qrb[kq]
                                self.tt(qn_[:], pjr[:, 0:256].rearrange("p (a b) -> p a b", a=4),
                                        ry_[:, 8:12].unsqueeze(2).to_broadcast([128, 4, 64]), ALU.mult, [rpjr, rr_], [rqn])
                                self.tt(qn_[:], qn_[:], gqr_rep[:, :].unsqueeze(1).to_broadcast([128, 4, 64]), ALU.mult, [rqn, r_c], [rqn],
                                        eng=ROPE_ENG)
                                rope(qn_, 4, rblk0 + b, qo_, rqn, rqo, eng=ROPE_ENG)

                                def tailr(qb_=qb_, rqb=rqb, q0=q0, par=par, qo_=qo_, rqo=rqo):
                                    self.acopy(qb_[:], qo_[:], [rqo], [rqb])
                                    trb, rtr = next_tr()
                                    tb = trb[:].bitcast(BF16)
                                    for hh in range(4):
                                        self.tr(tb[0:64, hh * 128:(hh + 1) * 128], qb_[:, hh, :], ident_bf[:], [rqb, r_c], [rtr])
                                    self.acopy(qTr[par][0:64, 0:4, q0:q0 + 128], tb[0:64, 0:512].rearrange("p (a b) -> p a b", a=4),
                                               [rtr], [r_qTr[par]])
                                tails.append(tailr)
                            run_tails(3 if isq else 1)
                        run_tails()
                        self.pj_pool = pool2
                        if not isq:
                            continue
                        nt0 = t0 // 128
                        og_off = og_base + t0
                        sg_, rsg = sgin[par], r_sgin[par]
                        self.dma(sg_[:, :, 0:ntok],
                                 sgT_d[512 * G:512 * G + 512, og_off:og_off + ntok].rearrange("(h p) t -> p h t", p=128), [], [rsg])
                        for hh in range(4):
                            h = 4 * G + hh
                            tl = []
                            for kt in range(nt0 + nb):
                                if kt < nt0:
                                    tl.append((kt, 0, ntok))
                                else:
                                    tl.append((kt, 128 * (kt - nt0), ntok))

                            def kfn(kt, hh=hh):
                                return [(kT_g[:, hh, 128 * kt:128 * kt + 128], r_kTg[kt // 4]),
                                        (kr_T[0:128, 128 * kt:128 * kt + 128], sblocks[kt // 4][5])]

                            def vfn(kt, hh=hh):
                                return (V_g[:, kt, 128 * hh:128 * hh + 128], r_Vg[kt // 4])

                            def post(kt, pt, rpt, c0, c1, nt0=nt0):
                                if kt >= nt0:
                                    j = kt - nt0
                                    self.memset(pt[64:128, 128 * j:128 * j + 64], 0.0, [rpt], [rpt])

                            k = self.cnt
                            self.cnt += 1
                            og_, rog = ogs[k % 2], r_ogs[k % 2]
                            def og_store(h=h, og_=og_, rog=rog, og_off=og_off, ntok=ntok):
                                self.dma(ogT_d[128 * h:128 * h + 128, og_off:og_off + ntok], og_[:, 0:ntok], [rog], [])
                            attention(ntok, tl, [(qTn[par][:, hh, :], r_qTn[par]), (qTr[par][0:128, hh, :], r_qTr[par])],
                                      kfn, vfn, B_SCALE, None, post, sg_[:, hh, 0:ntok], rsg, og_[:, 0:ntok], rog, abufs, og_store)
                        flush_epi()

                try:
                    for G in range(4):
                        self.dma(wuk[:], w_b_uk[:, 512 * G:512 * G + 512].rearrange("(kt p) c -> p kt c", p=128), [], [r_w], q="pool")
                        self.dma(wuv[:], w_b_uv[:, 512 * G:512 * G + 512].rearrange("(kt p) c -> p kt c", p=128), [], [r_w], q="pool")
                        for hh in range(4):
                            c0 = (4 * G + hh) * 192
                            self.dma(wuqn[:, :, 128 * hh:128 * hh + 128], w_b_uq[:, c0:c0 + 128].rearrange("(kt p) c -> p kt c", p=128),
                                     [], [r_w], q="pool")
                            self.dma(wuqr[:, :, 64 * hh:64 * hh + 64], w_b_uq[:, c0 + 128:c0 + 192].rearrange("(kt p) c -> p kt c", p=128),
                                     [], [r_w], q="pool")
                        sbl = [(512 * s, 4, True, r_cqT[s], r_ckvT[s], r_krT[s], 4 * s) for s in range(NTP // 4)]
                        b2_seq(G, cqT_p, ckvT_p, krT_p, sbl, 0)
                        self.stopat("b2prompt")
                        sbs = [(0, 4, False, None, r_ckvT[SP0], r_krT[SP0], 0), (512, 4, False, None, r_ckvT[SP0 + 1], r_krT[SP0 + 1], 0),
                               (1024, 1, True, r_cqT[SP0], r_ckvT[SP0 + 2], r_krT[SP0 + 2], NTP)]
                        b2_seq(G, cqT_s, ckvT_s, krT_s, sbs, T - 1024)
                        self.stopat("b2g0")
                except StopBuild:
                    pass
                S.flush()

        if ph == "B2":
            self._finish(S)
            return nc

        with ExitStack() as st:
            aT = self.sb(st, "aT", [128, 16, 1024], BF16)
            r_aT = Res()
            Wc = [self.sb(st, f"wc{i}", [128, 16, 512], BF16) for i in range(2)]
            ws = WStream(Wc, [Res(), Res()])
            ytile = [self.sb(st, f"ytile{i}", [128, 512], F32) for i in range(3)]
            xre = [self.sb(st, f"xre{i}", [128, 512], F32) for i in range(3)]
            obufs = (ytile[0:2], [Res(), Res()], xre[0:2], [Res(), Res()])
            for g0 in range(0, NTP, 8):
                nbg = min(8, NTP - g0)
                self.dma(aT[:, :, 0:128 * nbg], ogT_d[:, 128 * g0:128 * (g0 + nbg)].rearrange("(kt p) t -> p kt t", p=128), [], [r_aT])
                out_proj(ws, "b_out", aT, r_aT, nbg, y1[128 * g0:128 * (g0 + nbg), :], y_p[128 * g0:128 * (g0 + nbg), :], 128, obufs,
                         store_q="act")
            self.dma(aT[:, :, 0:128], ogT_d[:, T:T + 128].rearrange("(kt p) t -> p kt t", p=128), [], [r_aT])
            out_proj(ws, "b_out", aT, r_aT, 1, y1[T:T + NS, :], y_s, NS, obufs, store_q="act")
            S.flush()
        self._finish(S)
        return nc


    def _finish(self, S):
        self.stats = dict(ops=S.tot_ops, waits=S.tot_waits, incs=dict(S.incc), n_dma=S.n_dma)


_CACHE = {}


def _rope_table():
    half = 32
    inv = (np.float32(10000.0) ** (-np.arange(half, dtype=np.float32) / np.float32(half))).astype(np.float32)
    pos = np.concatenate([np.arange(T), PAST + np.arange(128)]).astype(np.float32)
    ang = (pos[:, None] * inv[None, :]).astype(np.float32)
    return np.concatenate([np.cos(ang), np.sin(ang)], axis=1).astype(np.float32)


def _build(phases="all"):
    if phases not in _CACHE:
        b = Builder(phases)
        nc = b.build()
        _CACHE[phases] = (nc, b)
    return _CACHE[phases]


def kernel(**inputs):
    phases = inputs.pop("_phases", "all")
    inputs = dict(inputs)
    nc, b = _build(phases)
    f = lambda a: np.ascontiguousarray(np.asarray(a, dtype=np.float32))
    rope = _rope_table()
    shared = {
        "a_ln": f(inputs["a_ln"][0]), "w_a_in": f(inputs["w_a_in"][0]), "a_q_norm": f(inputs["a_q_norm"][0]),
        "a_k_norm": f(inputs["a_k_norm"][0]), "a_rel_bias": f(inputs["a_rel_bias"][0]), "w_a_out": f(inputs["w_a_out"][0]),
        "b_ln": f(inputs["b_ln"][0]), "w_b_in": f(inputs["w_b_in"][0]), "b_q_a_norm": f(inputs["b_q_a_norm"][0]),
        "w_b_uq": f(inputs["w_b_uq"][0]), "b_kv_a_norm": f(inputs["b_kv_a_norm"][0]), "w_b_uk": f(inputs["w_b_uk"][0]),
        "w_b_uv": f(inputs["w_b_uv"][0]), "b_q_nope_norm": f(inputs["b_q_nope_norm"][0]),
        "b_k_nope_norm": f(inputs["b_k_nope_norm"][0]), "b_q_rope_norm": f(inputs["b_q_rope_norm"][0]),
        "b_k_rope_norm": f(inputs["b_k_rope_norm"][0]), "w_b_out": f(inputs["w_b_out"][0]), "rope_cs": rope,
    }
    in_maps = []
    for c in range(N_CORES):
        m = dict(shared)
        m["x_prompt"] = f(inputs["x_prompt"][c][:T])
        m["x_sample"] = f(inputs["x_sample"][c])
        m["cache_a_k"] = f(inputs["cache_a_k"][0, c]).reshape(512, D)
        m["cache_a_v"] = f(inputs["cache_a_v"][0, c]).reshape(512, D)
        m["cache_b_ckv"] = f(inputs["cache_b_ckv"][0, c])
        m["cache_b_krope"] = f(inputs["cache_b_krope"][0, c])
        in_maps.append(m)
    ncr = int(inputs.pop("_ncores", N_CORES))
    res = run_bass_kernel_spmd(nc, in_maps[:ncr], core_ids=list(range(ncr)))
    R = list(res.results) + [res.results[0]] * (N_CORES - ncr)
    st = lambda k: np.stack([R[c][k] for c in range(N_CORES)])
    y_p = st("y_prompt")
    y_s = st("y_sample")
    akp = st("new_a_k_prompt").reshape(1, N_CORES, 512, 16, 128)
    avp = st("new_a_v_prompt").reshape(1, N_CORES, 512, 16, 128)
    bcp = st("new_b_ckv_prompt")[None]
    brp = st("new_b_krope_prompt")[None]
    aks = st("new_a_k_sample").reshape(1, N_CORES, NS, 16, 128)
    avs = st("new_a_v_sample").reshape(1, N_CORES, NS, 16, 128)
    bcs = st("new_b_ckv_sample")[None]
    brs = st("new_b_krope_sample")[None]
    return (y_p, y_s, akp, avp, bcp, brp, aks, avs, bcs, brs)
```

```python
import numpy as np
from contextlib import ExitStack
import concourse.bass as bass
import concourse.mybir as mybir
from concourse.bass_utils import run_bass_kernel_spmd

F32 = mybir.dt.float32
BF16 = mybir.dt.bfloat16
I32 = mybir.dt.int32
AF = mybir.ActivationFunctionType
ALU = mybir.AluOpType

N_CORES = 8
T = 4096
D = 2048
NS = 64
PAST = 1024
EPS = 1e-6
A_SCALE = 128 ** -0.5
B_SCALE = 192 ** -0.5
BIN = 3136

ENGS = ("pe", "act", "dve", "pool", "sp")
CH = 30000
N_DMA_SEMS = 40
import os as _os0
ROPE_ENG = _os0.environ.get("DEV_ROPE_ENG", "dve")
EB_ENG = _os0.environ.get("DEV_EB_ENG", "pool")


class Res:
    __slots__ = ("w", "r", "excl")

    def __init__(self, excl=False):
        self.w = None
        self.r = []
        self.excl = excl


class Op:
    __slots__ = ("eng", "fn", "reads", "writes", "dma", "seq", "deps", "needs_inc", "inc_idx", "dsem", "dval")

    def __init__(self, eng, fn, reads, writes, dma):
        self.eng = eng
        self.fn = fn
        self.reads = reads
        self.writes = writes
        self.dma = dma
        self.deps = []
        self.needs_inc = False


class Sched:
    def __init__(self, nc, stack):
        self.nc = nc
        self.stack = stack
        self.e = {"pe": nc.tensor, "act": nc.scalar, "dve": nc.vector, "pool": nc.gpsimd, "sp": nc.sync}
        self.ops = []
        self.seqc = {e: 0 for e in ENGS}
        self.incc = {e: 0 for e in ENGS}
        self.waited = {e: {e2: -1 for e2 in ENGS} for e in ENGS}
        self.esems = {e: [] for e in ENGS}
        self.dsems = [stack.enter_context(nc.semaphore(f"sdma{i}")) for i in range(N_DMA_SEMS)]
        self.dcount = [0] * N_DMA_SEMS
        self.n_dma = 0
        self.tot_ops = 0
        self.tot_waits = 0

    def op(self, eng, fn, reads=(), writes=()):
        self.ops.append(Op(eng, fn, tuple(reads), tuple(writes), False))

    def dma(self, eng, fn, reads=(), writes=()):
        self.ops.append(Op(eng, fn, tuple(reads), tuple(writes), True))

    def _esem(self, e, idx):
        k = idx // CH
        while len(self.esems[e]) <= k:
            self.esems[e].append(self.stack.enter_context(self.nc.semaphore(f"s{e}{len(self.esems[e])}")))
        return self.esems[e][k], idx % CH + 1

    def flush(self):
        ops = self.ops
        self.ops = []
        waited = self.waited
        waited_dma = {e: set() for e in ENGS}
        ring = []
        allres = set()
        last_on = {}
        for o in ops:
            o.seq = self.seqc[o.eng]
            self.seqc[o.eng] += 1
            cand = []
            for r in o.reads:
                allres.add(r)
                if r.w is not None:
                    cand.append(r.w)
                if r.excl:
                    for d in r.r:
                        if d.eng != o.eng:
                            cand.append(d)
            for r in o.writes:
                allres.add(r)
                if r.w is not None:
                    cand.append(r.w)
                cand.extend(r.r)
            if o.dma:
                o.dsem = self.n_dma % N_DMA_SEMS
                self.n_dma += 1
                if len(ring) >= N_DMA_SEMS:
                    cand.append(ring[len(ring) - N_DMA_SEMS])
                ring.append(o)
            else:
                last_on[o.eng] = o
            best = {}
            for d in cand:
                if d is o:
                    continue
                if d.dma:
                    if id(d) not in waited_dma[o.eng]:
                        waited_dma[o.eng].add(id(d))
                        o.deps.append(d)
                else:
                    if d.eng == "pe" and o.eng == "pe" and not o.dma:
                        continue
                    if d.seq > waited[o.eng][d.eng]:
                        if d.eng not in best or d.seq > best[d.eng].seq:
                            best[d.eng] = d
            for e2, d in best.items():
                waited[o.eng][e2] = d.seq
                d.needs_inc = True
                o.deps.append(d)
            for r in o.reads:
                r.r.append(o)
            for r in o.writes:
                r.w = o
                r.r = []
        for o in last_on.values():
            o.needs_inc = True
        for o in ops:
            if o.dma:
                self.dcount[o.dsem] += 16
                o.dval = self.dcount[o.dsem]
            elif o.needs_inc:
                o.inc_idx = self.incc[o.eng]
                self.incc[o.eng] += 1
        for o in ops:
            eng = self.e[o.eng]
            for d in o.deps:
                if d.dma:
                    eng.wait_ge(self.dsems[d.dsem], d.dval)
                else:
                    s, v = self._esem(d.eng, d.inc_idx)
                    eng.wait_ge(s, v)
                self.tot_waits += 1
            ins = o.fn(eng)
            if o.dma:
                ins.then_inc(self.dsems[o.dsem], 16)
            elif o.needs_inc:
                s, v = self._esem(o.eng, o.inc_idx)
                ins.then_inc(s, 1)
        self.tot_ops += len(ops)
        for e in ENGS:
            eng = self.e[e]
            for e2 in ENGS:
                if e2 != e and self.incc[e2] > 0:
                    s, v = self._esem(e2, self.incc[e2] - 1)
                    eng.wait_ge(s, v)
            for i in range(N_DMA_SEMS):
                if self.dcount[i] > 0:
                    eng.wait_ge(self.dsems[i], self.dcount[i])
        for e in ENGS:
            for e2 in ENGS:
                waited[e][e2] = self.seqc[e2] - 1
        for r in allres:
            r.w = None
            r.r = []


class StopBuild(Exception):
    pass


import os as _os
_STOP = _os.environ.get("DEV_STOP", "")


class Builder:
    def stopat(self, tag):
        if _STOP == tag:
            raise StopBuild()

    def __init__(self, phases="all"):
        self.phases = phases
        self.nc = bass.Bass("TRN2", target_bir_lowering=False)
        self.gst = ExitStack()

    def din(self, name, shape, dt=F32):
        return self.nc.dram_tensor(name, list(shape), dt, kind="ExternalInput").ap()

    def dout(self, name, shape, dt=F32):
        return self.nc.dram_tensor(name, list(shape), dt, kind="ExternalOutput").ap()

    def dscr(self, name, shape, dt):
        return self.nc.dram_tensor(name, list(shape), dt, kind="Internal").ap()

    def sb(self, st, name, shape, dt):
        self._uid = getattr(self, "_uid", 0) + 1
        return st.enter_context(self.nc.sbuf_tensor(f"{name}_{self._uid}", list(shape), dt))

    def ps(self, st, name):
        return st.enter_context(self.nc.psum_tensor(name, [128, 512], F32))

    def mm(self, out, lhsT, rhs, start, stop, R, W):
        self.S.op("pe", lambda e: e.matmul(out, lhsT=lhsT, rhs=rhs, start=start, stop=stop,
                                           skip_group_check=True), R, W)

    def tr(self, out, in_, ident, R, W):
        self.S.op("pe", lambda e: e.transpose(out=out, in_=in_, identity=ident), R, W)

    def act(self, out, in_, func, R, W, bias=None, scale=None, accum=None):
        kw = {}
        if bias is not None:
            kw["bias"] = bias
        if scale is not None:
            kw["scale"] = scale
        if accum is not None:
            kw["accum_out"] = accum
        self.S.op("act", lambda e: e.activation(out=out, in_=in_, func=func, **kw), R, W)

    def acopy(self, out, in_, R, W):
        self.S.op("act", lambda e: e.activation(out=out, in_=in_, func=AF.Identity), R, W)

    def vcopy(self, out, in_, R, W, eng="dve"):
        self.S.op(eng, lambda e: e.tensor_copy(out=out, in_=in_), R, W)

    def tt(self, out, in0, in1, op, R, W, eng="dve"):
        self.S.op(eng, lambda e: e.tensor_tensor(out=out, in0=in0, in1=in1, op=op), R, W)

    def ts(self, out, in0, s1, s2, op0, op1, R, W, eng="dve"):
        if s2 is None:
            self.S.op(eng, lambda e: e.tensor_scalar(out=out, in0=in0, scalar1=s1, scalar2=None, op0=op0), R, W)
        else:
            self.S.op(eng, lambda e: e.tensor_scalar(out=out, in0=in0, scalar1=s1, scalar2=s2, op0=op0, op1=op1), R, W)

    def stt(self, out, in0, scalar, in1, op0, op1, R, W, eng="dve"):
        self.S.op(eng, lambda e: e.scalar_tensor_tensor(out=out, in0=in0, scalar=scalar, in1=in1, op0=op0, op1=op1), R, W)

    def memset(self, ap, val, R, W, eng="pool"):
        self.S.op(eng, lambda e: e.memset(ap, val), R, W)

    def dma(self, out, in_, R, W, q="sp", slow=False):
        if slow:
            self.S.dma(q, lambda e: e.dma_start(out=out, in_=in_, allow_slow_non_contiguous=True), R, W)
        else:
            self.S.dma(q, lambda e: e.dma_start(out=out, in_=in_), R, W)

    def rstd(self, ss, n, inv_d, r_ss):
        k = self.nti % 4
        self.nti += 1
        y = self.nt_y[:, 16 * k:16 * k + n]
        t = self.nt_t[:, 16 * k:16 * k + n]
        r = self.r_nt[k]
        x = ss
        rl = list(r_ss) if isinstance(r_ss, (list, tuple)) else [r_ss]
        self.ts(x, x, inv_d, EPS, ALU.mult, ALU.add, rl, rl)
        self.ts(y.bitcast(I32), x.bitcast(I32), 1, None, ALU.arith_shift_right, None, rl, [r])
        self.ts(y.bitcast(I32), y.bitcast(I32), -1, 0x5F3759DF, ALU.mult, ALU.add, [r], [r])
        for _ in range(2):
            self.stt(t, y, -0.5, y, ALU.mult, ALU.mult, [r], [r])
            self.tt(t, t, x, ALU.mult, [r] + rl, [r])
            self.stt(y, t, 1.5, y, ALU.add, ALU.mult, [r], [r])
        return y, r

    def build(self):
        nc = self.nc
        gst = self.gst
        S = self.S = Sched(nc, gst)
        ph = self.phases

        x_p = self.din("x_prompt", [T, D])
        x_s = self.din("x_sample", [NS, D])
        c_ak = self.din("cache_a_k", [512, D])
        c_av = self.din("cache_a_v", [512, D])
        c_bc = self.din("cache_b_ckv", [PAST, 512])
        c_br = self.din("cache_b_krope", [PAST, 64])
        a_ln = self.din("a_ln", [D])
        w_a_in = self.din("w_a_in", [D, 4 * D])
        a_qn = self.din("a_q_norm", [128])
        a_kn = self.din("a_k_norm", [128])
        a_rb = self.din("a_rel_bias", [16, 257])
        w_a_out = self.din("w_a_out", [D, D])
        b_ln = self.din("b_ln", [D])
        w_b_in = self.din("w_b_in", [D, BIN])
        b_qa = self.din("b_q_a_norm", [512])
        w_b_uq = self.din("w_b_uq", [512, 3072])
        b_kva = self.din("b_kv_a_norm", [512])
        w_b_uk = self.din("w_b_uk", [512, D])
        w_b_uv = self.din("w_b_uv", [512, D])
        b_qnn = self.din("b_q_nope_norm", [128])
        b_knn = self.din("b_k_nope_norm", [128])
        b_qrn = self.din("b_q_rope_norm", [64])
        b_krn = self.din("b_k_rope_norm", [64])
        w_b_out = self.din("w_b_out", [D, D])
        rope_cs = self.din("rope_cs", [T + 128, 64])

        y_p = self.dout("y_prompt", [T, D])
        y_s = self.dout("y_sample", [NS, D])
        o_akp = self.dout("new_a_k_prompt", [512, D])
        o_avp = self.dout("new_a_v_prompt", [512, D])
        o_bcp = self.dout("new_b_ckv_prompt", [T, 512])
        o_brp = self.dout("new_b_krope_prompt", [T, 64])
        o_aks = self.dout("new_a_k_sample", [NS, D])
        o_avs = self.dout("new_a_v_sample", [NS, D])
        o_bcs = self.dout("new_b_ckv_sample", [NS, 512])
        o_brs = self.dout("new_b_krope_sample", [NS, 64])

        TT = T + 128
        wa_in_bf = self.dscr("wa_in_bf", [16, 128, 16, 512], BF16)
        wa_out_bf = self.dscr("wa_out_bf", [4, 128, 16, 512], BF16)
        wb_in_bf = self.dscr("wb_in_bf", [7, 128, 16, 512], BF16)
        wb_out_bf = self.dscr("wb_out_bf", [4, 128, 16, 512], BF16)
        y1 = self.dscr("y1", [TT, D], F32)
        self.dbgA = (ph == "A")
        sgT_d = self.dscr("sgT_d", [D, TT], BF16)
        ogT_d = self.dscr("ogT_d", [D, TT], BF16)
        ext_d = self.dscr("ext_d", [16, 384], F32)
        self.wres = {}
        self.wdone = set()

        ident_bf = self.sb(gst, "ident_bf", [128, 128], BF16)
        ident_f = self.sb(gst, "ident_f", [128, 128], F32)
        ones_bf = self.sb(gst, "ones_bf", [128, 128], BF16)
        self.nt_y = self.sb(gst, "nt_y", [128, 64], F32)
        self.nt_t = self.sb(gst, "nt_t", [128, 64], F32)
        self.r_nt = [Res() for _ in range(4)]
        self.nti = 0
        r_c = Res()
        gcolA = self.sb(gst, "gcolA", [128, 16], F32)
        gcolB = self.sb(gst, "gcolB", [128, 16], F32)
        gq_col = self.sb(gst, "gq_col", [128, 1], F32)
        gk_col = self.sb(gst, "gk_col", [128, 1], F32)
        gk_rep = self.sb(gst, "gk_rep", [128, 128], F32)
        gqn_col = self.sb(gst, "gqn_col", [128, 1], F32)
        gkn_col = self.sb(gst, "gkn_col", [128, 1], F32)
        gqa_rep = self.sb(gst, "gqa_rep", [128, 512], F32)
        gkva_rep = self.sb(gst, "gkva_rep", [128, 512], F32)
        gqr_rep = self.sb(gst, "gqr_rep", [128, 64], F32)
        gkr_rep = self.sb(gst, "gkr_rep", [128, 64], F32)
        cb = self.sb(gst, "cb", [128, 16], F32)
        self.stEB = ExitStack()
        EB = self.sb(self.stEB, "EB", [128, 16, 2, 128], BF16)

        PJ = [self.ps(gst, f"pj{i}") for i in range(2)]
        TRB = [self.ps(gst, f"trb{i}") for i in range(2)]
        STB = [self.ps(gst, f"st{i}") for i in range(2)]
        OTB = self.ps(gst, "otb")
        SUMB = self.ps(gst, "sumb")
        r_PJ = [Res(True), Res(True)]
        r_TRB = [Res(True), Res(True)]
        r_ST = [Res(True), Res(True)]
        r_OT = Res(True)
        r_SUM = Res(True)
        self.pji = 0
        self.tri = 0

        self.pj_pool = [(PJ[0], r_PJ[0]), (PJ[1], r_PJ[1])]
        pool2 = list(self.pj_pool)
        pool6 = pool2 + [(STB[0], r_ST[0]), (STB[1], r_ST[1]), (OTB, r_OT), (SUMB, r_SUM)]

        def next_pj():
            i = self.pji % len(self.pj_pool)
            self.pji += 1
            return self.pj_pool[i]

        def next_tr():
            i = self.tri
            self.tri ^= 1
            return TRB[i], r_TRB[i]

        with ExitStack() as st:
            J = self.sb(st, "J", [128, 128], F32)
            tmp16 = self.sb(st, "tmp16", [16, 128], F32)
            tmp16b = self.sb(st, "tmp16b", [16, 128], F32)
            e_sb = self.sb(st, "e_sb", [16, 384], F32)
            hk = self.sb(st, "hk", [128, 16, 2, 128], F32)
            ebf = self.sb(st, "ebf", [128, 128], F32)
            cbrow = self.sb(st, "cbrow", [16, 1], F32)
            diagc = self.sb(st, "diagc", [16, 16], F32)
            ones_f = self.sb(st, "ones_f", [16, 128], F32)
            cneg = self.sb(st, "cneg", [128, 16], F32)
            r_J, r_t16, r_esb, r_hk, r_ebf, r_cbrow, r_ext = Res(), Res(), Res(), Res(), Res(), Res(), Res()
            self.memset(ident_f[:], 0.0, [], [r_c])
            S.op("pool", lambda e: e.affine_select(out=ident_f[:], in_=ident_f[:], pattern=[[-1, 128]],
                                                   compare_op=ALU.not_equal, fill=1.0, base=0, channel_multiplier=1), [r_c], [r_c])
            self.vcopy(ident_bf[:], ident_f[:], [r_c], [r_c])
            self.memset(ones_bf[:], 1.0, [], [r_c])
            self.memset(ones_f[:], 1.0, [], [r_cbrow])
            self.memset(J[:], 0.0, [], [r_J])
            S.op("pool", lambda e: e.affine_select(out=J[:], in_=J[:], pattern=[[1, 128]],
                                                   compare_op=ALU.not_equal, fill=1.0, base=-127, channel_multiplier=1), [r_J], [r_J])
            for (src, dst) in ((a_ln, gcolA), (b_ln, gcolB)):
                tb = tmp16 if dst is gcolA else tmp16b
                self.dma(tb[:], src.rearrange("(kt p) -> kt p", p=128), [], [r_t16])
                pj, rpj = next_pj()
                self.tr(pj[:, 0:16], tb[:], ident_f[0:16, 0:16], [r_t16, r_c], [rpj])
                self.acopy(dst[:], pj[:, 0:16], [rpj], [r_c])
            for (src, dst) in ((a_qn, gq_col), (a_kn, gk_col), (b_qnn, gqn_col), (b_knn, gkn_col)):
                self.dma(dst[:], src.rearrange("(p o) -> p o", o=1), [], [r_c], slow=True)
            self.ts(gq_col[:], gq_col[:], A_SCALE, None, ALU.mult, None, [r_c], [r_c])
            for (src, dst, n) in ((a_kn, gk_rep, 128), (b_qa, gqa_rep, 512), (b_kva, gkva_rep, 512),
                                  (b_qrn, gqr_rep, 64), (b_krn, gkr_rep, 64)):
                self.dma(dst[:], src.partition_broadcast(128), [], [r_c])
            self.dma(e_sb[:, 0:257], a_rb, [], [r_esb])
            self.vcopy(e_sb[:, 257:384], e_sb[:, 256:257].to_broadcast([16, 127]), [r_esb], [r_esb])
            self.dma(ext_d, e_sb[:], [r_esb], [r_ext])
            tab_t = a_rb.tensor
            ext_t = ext_d.tensor
            for h in range(16):
                self.dma(hk[:, h, 0, :], bass.AP(ext_t, h * 384 + 129, [[1, 128], [1, 128]]), [r_ext], [r_hk])
                self.dma(hk[:, h, 1, :], bass.AP(ext_t, h * 384 + 1, [[1, 128], [1, 128]]), [r_ext], [r_hk])
            self.dma(cbrow[:], bass.AP(tab_t, 256, [[257, 16], [1, 1]]), [], [r_cbrow], slow=True)
            self.ts(diagc[:], ident_f[0:16, 0:16], cbrow[:, 0:1], None, ALU.mult, None, [r_c, r_cbrow], [r_cbrow])
            pj, rpj = next_pj()
            self.mm(pj[:, 0:16], ones_f[:, :], diagc[:, :], True, True, [r_cbrow], [rpj])
            self.acopy(cb[:], pj[:, 0:16], [rpj], [r_c])
            self.ts(cneg[:], cb[:], -1.0, None, ALU.mult, None, [r_c], [r_cbrow])
            for h in range(16):
                for t in range(2):
                    self.act(ebf[:], hk[:, h, t, :], AF.Exp, [r_hk, r_cbrow], [r_ebf], bias=cneg[:, h:h + 1])
                    pj, rpj = next_pj()
                    self.mm(pj[:, 0:128], J[:], ebf[:], True, True, [r_J, r_ebf], [rpj])
                    self.vcopy(EB[:, h, t, :], pj[:, 0:128], [rpj], [r_c])
            self.memset(EB[64:128, :, 1, 0:64], 0.0, [r_c], [r_c], eng="dve")
            S.flush()

        if ph == "setup":
            self._finish(S)
            return nc

        def wsrc(wid, j):
            if wid == "a_in":
                return w_a_in[:, 512 * j:512 * j + 512].rearrange("(kt p) c -> p kt c", p=128), 512
            if wid == "a_out":
                return w_a_out[:, 512 * j:512 * j + 512].rearrange("(kt p) c -> p kt c", p=128), 512
            if wid == "b_out":
                return w_b_out[:, 512 * j:512 * j + 512].rearrange("(kt p) c -> p kt c", p=128), 512
            if wid == "b_in":
                if j < 2:
                    return w_b_in[:, 512 * j:512 * j + 512].rearrange("(kt p) c -> p kt c", p=128), 512
                if j == 2:
                    return w_b_in[:, 1024:1088].rearrange("(kt p) c -> p kt c", p=128), 64
                c0 = 1088 + 512 * (j - 3)
                return w_b_in[:, c0:c0 + 512].rearrange("(kt p) c -> p kt c", p=128), 512
            raise ValueError(wid)

        wscr = {"a_in": wa_in_bf, "a_out": wa_out_bf, "b_in": wb_in_bf, "b_out": wb_out_bf}

        class WStream:
            def __init__(s2, bufs, rbufs):
                s2.bufs = bufs
                s2.rb = rbufs
                s2.i = 0
                s2.q = []

            def prefetch(s2, wid, j):
                k = s2.i
                s2.i ^= 1
                buf, rb = s2.bufs[k], s2.rb[k]
                key = (wid, j)
                if key not in self.wres:
                    self.wres[key] = Res()
                rw = self.wres[key]
                src, ncol = wsrc(wid, j)
                if key in self.wdone:
                    self.dma(buf[:, :, 0:ncol], wscr[wid][j, :, :, 0:ncol], [rw], [rb])
                else:
                    self.dma(buf[:, :, 0:ncol], src, [], [rb], q="pool")
                    self.dma(wscr[wid][j, :, :, 0:ncol], buf[:, :, 0:ncol], [rb], [rw], q="pool")
                    self.wdone.add(key)
                s2.q.append((buf, rb))

            def pop(s2):
                return s2.q.pop(0)

        def prep_load(st_bufs, src, nreal):
            xin, r_xin, xns, r_xns, ss1, r_ss1 = st_bufs
            if nreal < 128:
                self.memset(xin[:], 0.0, [], [r_xin], eng="dve")
            self.dma(xin[0:nreal, :], src, [], [r_xin])

        def prep_norm(st_bufs):
            xin, r_xin, xns, r_xns, ss1, r_ss1 = st_bufs
            k = self.xni % 2
            self.xni += 1
            xn, r_xn = xns[k], r_xns[k]
            self.act(xn[:], xin[:], AF.Square, [r_xin], [r_xn, r_ss1], accum=ss1[:, 0:1])
            ry_, rr_ = self.rstd(ss1[:, 0:1], 1, 1.0 / D, r_ss1)
            self.act(xn[:], xin[:], AF.Identity, [r_xin, rr_], [r_xn], scale=ry_[:, 0:1])
            return xn, r_xn

        def prep_stage1(st_bufs, src, nreal):
            prep_load(st_bufs, src, nreal)
            return prep_norm(st_bufs)

        def prep_stage2(xn, r_xn, gcol, hT, col0, rblk):
            for half in range(2):
                trb, rtr = next_tr()
                tb = trb[:].bitcast(BF16)
                for j in range(8):
                    kt = half * 8 + j
                    self.tr(tb[:, j * 128:(j + 1) * 128], xn[:, kt * 128:(kt + 1) * 128], ident_bf[:], [r_xn, r_c], [rtr])
                self.tt(hT[:, half * 8:half * 8 + 8, col0:col0 + 128],
                        tb[:, 0:1024].rearrange("p (a b) -> p a b", a=8),
                        gcol[:, half * 8:half * 8 + 8].unsqueeze(2).to_broadcast([128, 8, 128]),
                        ALU.mult, [rtr, r_c], [rblk])

        def prep_hT(st_bufs, blocks, gcol, hT, r_hT):
            for (src, nreal, col0, rblk) in blocks:
                xn, r_xn = prep_stage1(st_bufs, src, nreal)
                prep_stage2(xn, r_xn, gcol, hT, col0, rblk)

        def attention(nq, tiles, qparts, kfn, vfn, escale, ebias, post, sgT_ap, r_sg, out_ap, r_out, rs_bufs, after_epi=None):
            PTs, r_PTs, rs, r_rs, tmp, r_tmp = rs_bufs
            n = len(tiles)
            pend = None
            if self.acci % 2 == 0:
                OTB_, r_OT_, SUMB_, r_SUM_ = PJ[0], r_PJ[0], PJ[1], r_PJ[1]
            else:
                OTB_, r_OT_, SUMB_, r_SUM_ = OTB, r_OT, SUMB, r_SUM
            self.acci += 1
            for i, (tid, c0, c1) in enumerate(tiles):
                stb, rst = st_pool[self.sti % 4]
                self.sti += 1
                kparts = kfn(tid)
                for pi, ((kap, rk), (qap, rq)) in enumerate(zip(kparts, qparts)):
                    self.mm(stb[:, c0:c1], kap, qap[:, c0:c1], pi == 0, pi == len(kparts) - 1, [rk, rq], [rst])
                pt, rpt = PTs[self.pti % 3], r_PTs[self.pti % 3]
                self.pti += 1
                kw = {}
                self.act(pt[:, c0:c1], stb[:, c0:c1], AF.Exp, [rst] + ([r_c] if ebias is not None else []), [rpt],
                         bias=ebias, scale=escale)
                post(tid, pt, rpt, c0, c1)
                if pend is not None:
                    pend()
                if i == min(2, n - 1) and self.pend_epi is not None:
                    self.pend_epi()
                    self.pend_epi = None
                vap, rv = vfn(tid)

                def pv(i=i, pt=pt, rpt=rpt, c0=c0, c1=c1, vap=vap, rv=rv):
                    self.mm(OTB_[:, c0:c1], vap, pt[:, c0:c1], i == 0, i == n - 1, [rv, rpt], [r_OT_])
                    self.mm(SUMB_[:, c0:c1], ones_bf[:], pt[:, c0:c1], i == 0, i == n - 1, [r_c, rpt], [r_SUM_])
                pend = pv
            pend()

            def epi():
                S.op("dve", lambda e: e.reciprocal(out=rs[:, 0:nq], in_=SUMB_[:, 0:nq]), [r_SUM_], [r_rs])
                self.stt(rs[:, 0:nq], OTB_[:, 0:nq], 0.5, rs[:, 0:nq], ALU.mult, ALU.mult, [r_OT_, r_rs], [r_rs])
                self.tt(out_ap, rs[:, 0:nq], sgT_ap, ALU.mult, [r_rs, r_sg], [r_out])
                if after_epi is not None:
                    after_epi()
            if self.pend_epi is not None:
                self.pend_epi()
            self.pend_epi = epi

        def flush_epi():
            if self.pend_epi is not None:
                self.pend_epi()
                self.pend_epi = None

        self.pti = 0
        self.gi = 0
        self.acci = 0
        self.sti = 0
        st_pool = [(STB[0], r_ST[0]), (STB[1], r_ST[1]), (TRB[0], r_TRB[0]), (TRB[1], r_TRB[1])]
        self.pend_epi = None
        self.xni = 0

        def sigmoid_gate(pj, rpj, nq, out_ap, r_out, gbufs):
            k = self.gi % 2
            self.gi += 1
            ge, r_ge, gc, r_gc = gbufs[0][k], gbufs[1][k], gbufs[2][k], gbufs[3][k]
            self.act(ge[:, 0:nq], pj[:, 0:nq], AF.Tanh, [rpj], [r_ge], scale=0.5)
            self.act(gc[:, 0:nq], pj[:, 0:nq], AF.Identity, [rpj], [r_gc])
            self.stt(out_ap, ge[:, 0:nq], 1.0, gc[:, 0:nq], ALU.add, ALU.mult, [r_ge, r_gc], [r_out])

        def out_proj(ws, wid, actT, r_act, nb, res_src, y_dst, nreal_last, obufs, after_chunk=None, store_q="pool"):
            ytile, r_y, xre, r_x = obufs
            ws.prefetch(wid, 0)
            k = 0
            for c in range(4):
                if c + 1 < 4:
                    ws.prefetch(wid, c + 1)
                wc, rwc = ws.pop()
                for b in range(nb):
                    nr = nreal_last if b == nb - 1 else 128
                    pj, rpj = next_pj()
                    for kt in range(16):
                        self.mm(pj[:, :], actT[:, kt, 128 * b:128 * b + 128], wc[:, kt, :], kt == 0, kt == 15,
                                [r_act, rwc], [rpj])
                    yt, ry, xr, rx = ytile[k % 2], r_y[k % 2], xre[k % 2], r_x[k % 2]
                    k += 1
                    self.dma(xr[0:nr, :], res_src[128 * b:128 * b + nr, 512 * c:512 * c + 512], [], [rx])
                    self.tt(yt[0:nr, :], pj[0:nr, :], xr[0:nr, :], ALU.add, [rpj, rx], [ry])
                    self.dma(y_dst[128 * b:128 * b + nr, 512 * c:512 * c + 512], yt[0:nr, :], [ry], [], q=store_q)
                if after_chunk is not None:
                    after_chunk(c)

        with ExitStack() as st:
            hT = self.sb(st, "hT", [128, 16, 512], BF16)
            r_hT = [Res() for _ in range(4)]
            Wc = [self.sb(st, f"wc{i}", [128, 16, 512], BF16) for i in range(2)]
            ws = WStream(Wc, [Res(), Res()])
            xin = self.sb(st, "xin", [128, D], F32)
            xn = [self.sb(st, f"xn{i}", [128, D], BF16) for i in range(2)]
            ss1 = self.sb(st, "ss1", [128, 1], F32)
            hbufs = (xin, Res(), xn, [Res(), Res()], ss1, Res())
            qT = [self.sb(st, f"qT{i}", [128, 4, 512], BF16) for i in range(2)]
            r_qT = [[Res() for _ in range(4)] for _ in range(2)]
            sgT = [self.sb(st, f"sgT{i}", [128, 4, 512], BF16) for i in range(2)]
            r_sgT = [[Res() for _ in range(4)] for _ in range(2)]
            kTr = self.sb(st, "kTr", [128, 16, 1024], BF16)
            r_kT = [[Res() for _ in range(2)] for _ in range(16)]
            Vr = self.sb(st, "Vr", [128, 8, D], BF16)
            r_V = [[Res() for _ in range(4)] for _ in range(8)]
            PTs = [self.sb(st, f"pt{i}", [128, 512], BF16) for i in range(3)]
            rsb = self.sb(st, "rsb", [128, 512], F32)
            abufs = (PTs, [Res(), Res(), Res()], rsb, Res(), None, None)
            gbufs = ([self.sb(st, f"ge{i}", [128, 512], F32) for i in range(2)], [Res(), Res()],
                     [self.sb(st, f"gc{i}", [128, 512], BF16) for i in range(2)], [Res(), Res()])
            ogT = self.sb(st, "ogT", [128, 16, 512], BF16)
            r_ogT = Res()
            ytile = [self.sb(st, f"ytile{i}", [128, 512], F32) for i in range(2)]
            xre = [self.sb(st, f"xre{i}", [128, 512], F32) for i in range(2)]
            obufs = (ytile, [Res(), Res()], xre, [Res(), Res()])
            qn = [self.sb(st, f"qn{i}", [128, 512], BF16) for i in range(4)]
            r_qn = [Res(), Res(), Res(), Res()]
            kf = [self.sb(st, f"kf{i}", [128, 512], F32) for i in range(2)]
            r_kf = [Res(), Res()]
            ss4 = [self.sb(st, f"ss4{i}", [128, 4], F32) for i in range(2)]
            r_ss4 = [[Res() for _ in range(4)] for _ in range(2)]
            sqj = self.sb(st, "sqj", [128, 4, 128], BF16)
            r_sqj = [Res() for _ in range(4)]
            vst = [self.sb(st, f"vst{i}", [128, 512], F32) for i in range(2)]
            r_vst = [Res(), Res()]
            self.qni = 0
            self.s4i = 0
            self.kfi = 0
            self.vsi = 0

            def a_prep(blocks, only=None):
                for b, (src, nreal) in enumerate(blocks):
                    if only is None or only == b:
                        prep_hT(hbufs, [(src, nreal, 128 * b, r_hT[b])], gcolA, hT, None)

            def a_superblock(blocks, hf, has_prev, y1_rows, kout, vout, x_rows, next_blocks=None):
                nb = len(blocks)
                ntok = 128 * nb
                pf = 1 - hf
                r_h = r_hT[0:nb]
                self.stopat("hT")
                tails = []

                def run_tails(keep=0):
                    while len(tails) > keep:
                        tails.pop(0)()

                order = []
                for G in range(4):
                    order += [("q", G), ("k", G), ("v", G), ("g", G)]
                cidx = {"q": 0, "k": 4, "v": 8, "g": 12}
                ws.prefetch("a_in", cidx[order[0][0]] + order[0][1])
                for oi, (kind, G) in enumerate(order):
                    if oi + 1 < len(order):
                        ws.prefetch("a_in", cidx[order[oi + 1][0]] + order[oi + 1][1])
                    wc, rwc = ws.pop()
                    par = G % 2
                    if oi == 1: self.stopat("q")
                    if oi == 2: self.stopat("k")
                    if oi == 3: self.stopat("v")
                    if kind in ("q", "k"):
                        for b in range(nb):
                            pj, rpj = next_pj()
                            for kt in range(16):
                                self.mm(pj[:, :], hT[:, kt, 128 * b:128 * b + 128], wc[:, kt, :], kt == 0, kt == 15,
                                        [r_h[b], rwc], [rpj])
                            s4, rs4 = ss4[self.s4i % 2], r_ss4[self.s4i % 2]
                            self.s4i += 1
                            for hh in range(4):
                                self.act(sqj[:, hh, :], pj[:, 128 * hh:128 * hh + 128], AF.Square, [rpj], [r_sqj[hh], rs4[hh]],
                                         accum=s4[:, hh:hh + 1])
                            ry_, rr_ = self.rstd(s4[:, 0:4], 4, 1.0 / 128, rs4)
                            qb, rqb = qn[self.qni % 4], r_qn[self.qni % 4]
                            self.qni += 1
                            self.tt(qb[:].rearrange("p (a b) -> p a b", a=4), pj[:].rearrange("p (a b) -> p a b", a=4),
                                    ry_[:, 0:4].unsqueeze(2).to_broadcast([128, 4, 128]), ALU.mult, [rpj, rr_], [rqb])
                            if kind == "k" and kout is not None:
                                kfb, rkf = kf[self.kfi % 2], r_kf[self.kfi % 2]
                                self.kfi += 1
                                self.tt(kfb[:].rearrange("p (a b) -> p a b", a=4), pj[:].rearrange("p (a b) -> p a b", a=4),
                                        ry_[:, 0:4].unsqueeze(2).to_broadcast([128, 4, 128]), ALU.mult, [rpj, rr_], [rkf])
                                self.tt(kfb[:].rearrange("p (a b) -> p a b", a=4), kfb[:].rearrange("p (a b) -> p a b", a=4),
                                        gk_rep[:, :].unsqueeze(1).to_broadcast([128, 4, 128]), ALU.mult, [rkf, r_c], [rkf])
                                nr = blocks[b][1]
                                self.dma(kout[128 * b:128 * b + nr, 512 * G:512 * G + 512], kfb[0:nr, :], [rkf], [], q="pool")

                            def tail(kind=kind, G=G, b=b, qb=qb, rqb=rqb, par=par):
                                trb, rtr = next_tr()
                                tb = trb[:].bitcast(BF16)
                                for hh in range(4):
                                    self.tr(tb[:, hh * 128:(hh + 1) * 128], qb[:, 128 * hh:128 * hh + 128], ident_bf[:],
                                            [rqb, r_c], [rtr])
                                src3 = tb[:, 0:512].rearrange("p (a b) -> p a b", a=4)
                                if kind == "q":
                                    dst = qT[par][:, 0:4, 128 * b:128 * b + 128]
                                    self.act(dst, src3, AF.Identity, [rtr, r_c], [r_qT[par][hh] for hh in range(4)],
                                             scale=gq_col[:, 0:1])
                                else:
                                    dst = kTr[:, 4 * G:4 * G + 4, 512 * hf + 128 * b:512 * hf + 128 * b + 128]
                                    self.act(dst, src3, AF.Identity, [rtr, r_c], [r_kT[4 * G + hh][hf] for hh in range(4)],
                                             scale=gk_col[:, 0:1])
                            run_tails(1)
                            tails.append(tail)
                    elif kind == "v":
                        for b in range(nb):
                            pj, rpj = next_pj()
                            for kt in range(16):
                                self.mm(pj[:, :], hT[:, kt, 128 * b:128 * b + 128], wc[:, kt, :], kt == 0, kt == 15,
                                        [r_h[b], rwc], [rpj])
                            self.acopy(Vr[:, 4 * hf + b, 512 * G:512 * G + 512], pj[:, :], [rpj], [r_V[4 * hf + b][G]])
                            if vout is not None:
                                vb, rvb = vst[self.vsi % 2], r_vst[self.vsi % 2]
                                self.vsi += 1
                                self.vcopy(vb[:], pj[:, :], [rpj], [rvb])
                                nr = blocks[b][1]
                                self.dma(vout[128 * b:128 * b + nr, 512 * G:512 * G + 512], vb[0:nr, :], [rvb], [], q="pool")
                            run_tails()
                    else:
                        for hh in range(4):
                            pj, rpj = next_pj()
                            for kt in range(16):
                                self.mm(pj[:, 0:ntok], wc[:, kt, 128 * hh:128 * hh + 128], hT[:, kt, 0:ntok], kt == 0, kt == 15,
                                        r_h + [rwc], [rpj])
                            sigmoid_gate(pj, rpj, ntok, sgT[par][:, hh, 0:ntok], r_sgT[par][hh], gbufs)
                            run_tails()
                        run_tails()
                        self.stopat("g")
                        for hh in range(4):
                            h = 4 * G + hh
                            tl = []
                            for t in [4, 3, 5, 2, 6, 1, 7, 0]:
                                if t < 4 and not has_prev:
                                    continue
                                if t >= 4 and t - 4 >= nb:
                                    continue
                                b0 = max(0, t - 4)
                                b1 = min(nb - 1, t)
                                tl.append((t, 128 * b0, 128 * (b1 + 1)))

                            def kfn(t, h=h):
                                half = pf if t < 4 else hf
                                c = 512 * half + 128 * (t % 4)
                                return [(kTr[:, h, c:c + 128], r_kT[h][half])]

                            def vfn(t, h=h, G=G):
                                slot = 4 * (pf if t < 4 else hf) + (t % 4)
                                return (Vr[:, slot, 128 * h:128 * h + 128], r_V[slot][G])

                            def post(t, pt, rpt, c0, c1, h=h):
                                b = t - 3
                                if 0 <= b < nb:
                                    self.tt(pt[:, 128 * b:128 * b + 128], pt[:, 128 * b:128 * b + 128], EB[:, h, 0, :],
                                            ALU.mult, [rpt, r_c], [rpt], eng=EB_ENG)
                                b = t - 4
                                if 0 <= b < nb:
                                    self.tt(pt[:, 128 * b:128 * b + 128], pt[:, 128 * b:128 * b + 128], EB[:, h, 1, :],
                                            ALU.mult, [rpt, r_c], [rpt], eng=EB_ENG)
                                b = t
                                if 0 <= b < nb:
                                    self.memset(pt[0:64, 128 * b + 64:128 * b + 128], 0.0, [rpt], [rpt])

                            attention(ntok, tl, [(qT[par][:, hh, :], r_qT[par][hh])], kfn, vfn, None, cb[:, h:h + 1], post,
                                      sgT[par][:, hh, 0:ntok], r_sgT[par][hh], ogT[:, h, 0:ntok], r_ogT, abufs)
                        flush_epi()
                self.stopat("attn")
                staged = {}

                def stage1(c):
                    if next_blocks is not None and c < len(next_blocks):
                        staged[c] = prep_stage1(hbufs, next_blocks[c][0], next_blocks[c][1])

                def after_chunk(c):
                    if c in staged:
                        xn_, rxn_ = staged.pop(c)
                        prep_stage2(xn_, rxn_, gcolA, hT, 128 * c, r_hT[c])
                    stage1(c + 1)
                stage1(0)
                out_proj(ws, "a_out", ogT, r_ogT, nb, x_rows, y1_rows, blocks[-1][1], obufs, after_chunk, store_q="act")

            try:
              pblocks = lambda s: [(x_p[512 * s + 128 * b:512 * s + 128 * b + 128, :], 128) for b in range(4)]
              sblocks_ = [(x_s[0:NS, :], NS)]
              a_prep(pblocks(0))
              for s in range(T // 512):
                blocks = pblocks(s)
                last = (s == T // 512 - 1)
                if s >= 1 or T // 512 == 1:
                    todo = [(w_, j_) for (w_, n_) in (("b_in", 7), ("b_out", 4)) for j_ in range(n_) if (w_, j_) not in self.wdone]
                    for (wid, j) in (todo if last else todo[:2]):
                        key = (wid, j)
                        self.wres[key] = Res()
                        src, ncol = wsrc(wid, j)
                        for q4 in range(4):
                            self.dma(wscr[wid][j, :, 4 * q4:4 * q4 + 4, 0:ncol], src[:, 4 * q4:4 * q4 + 4, :], [], [self.wres[key]], q="pool")
                        self.wdone.add(key)
                a_superblock(blocks, s % 2, s > 0, (y_p if self.dbgA else y1)[512 * s:512 * s + 512, :],
                             o_akp if last else None, o_avp if last else None, x_p[512 * s:512 * s + 512, :],
                             sblocks_ if last else pblocks(s + 1))
              self.stopat("prompt")
              kc = Wc[0][:].rearrange("p a b -> p (a b)")
              r_kc = ws.rb[0]
              self.dma(kc.rearrange("p (t c) -> p t c", t=4), c_ak.rearrange("(t p) c -> p t c", p=128), [], [r_kc], q="pool")
              self.dma(Vr[:, 0:4, :], c_av.rearrange("(t p) c -> p t c", p=128), [], [r_V[t4][g] for t4 in range(4) for g in range(4)], q="pool")
              for t4 in range(4):
                  for G in range(4):
                      trb, rtr = next_tr()
                      tb = trb[:].bitcast(BF16)
                      for hh in range(4):
                          h = 4 * G + hh
                          self.tr(tb[:, hh * 128:(hh + 1) * 128], kc[:, 2048 * t4 + 128 * h:2048 * t4 + 128 * h + 128], ident_bf[:],
                                  [r_kc, r_c], [rtr])
                      self.acopy(kTr[:, 4 * G:4 * G + 4, 128 * t4:128 * t4 + 128], tb[:, 0:512].rearrange("p (a b) -> p a b", a=4),
                                 [rtr], [r_kT[4 * G + hh][0] for hh in range(4)])
              a_superblock(sblocks_, 1, True, y_s if self.dbgA else y1[T:T + 128, :], o_aks, o_avs, x_s[0:NS, :])
            except StopBuild:
                pass
            S.flush()

        self.stEB.close()
        if ph == "A":
            self._finish(S)
            return nc
        NTP = T // 128
        with ExitStack() as stB:
            cqT_p = self.sb(stB, "cqT_p", [128, 4, T], BF16)
            ckvT_p = self.sb(stB, "ckvT_p", [128, 4, T], BF16)
            krT_p = self.sb(stB, "krT_p", [128, T], BF16)
            cqT_s = self.sb(stB, "cqT_s", [128, 4, 128], BF16)
            ckvT_s = self.sb(stB, "ckvT_s", [128, 4, 1152], BF16)
            krT_s = self.sb(stB, "krT_s", [128, 1152], BF16)
            ropet = self.sb(stB, "ropet", [128, NTP + 1, 64], F32)
            ra = self.sb(stB, "rope_a", [128, 4, 32], F32)
            rb_ = self.sb(stB, "rope_b", [128, 4, 32], F32)
            r_rope = Res()
            self.s4i = 0

            def rope(src, nh, blk, dst, r_src, r_dst, eng=None):
                cos = ropet[:, blk, 0:32].unsqueeze(1).to_broadcast([128, nh, 32])
                sin = ropet[:, blk, 32:64].unsqueeze(1).to_broadcast([128, nh, 32])
                x1 = src[:, :, 0:32]
                x2 = src[:, :, 32:64]
                a = ra[:, 0:nh, :]
                b_ = rb_[:, 0:nh, :]
                E = eng or "dve"
                self.tt(a, x1, cos, ALU.mult, [r_src, r_c], [r_rope], eng=E)
                self.tt(b_, x2, sin, ALU.mult, [r_src, r_c, r_rope], [r_rope], eng=E)
                self.tt(dst[:, :, 0:32], a, b_, ALU.subtract, [r_rope], [r_dst], eng=E)
                self.tt(a, x1, sin, ALU.mult, [r_src, r_c, r_rope], [r_rope], eng=E)
                self.tt(b_, x2, cos, ALU.mult, [r_src, r_c, r_rope], [r_rope], eng=E)
                self.tt(dst[:, :, 32:64], a, b_, ALU.add, [r_rope], [r_dst], eng=E)

            r_cqT = [Res() for _ in range(NTP // 4 + 1)]
            r_ckvT = [Res() for _ in range(NTP // 4 + 3)]
            r_krT = [Res() for _ in range(NTP // 4 + 3)]
            SP0 = NTP // 4

            with ExitStack() as st:
                hTs = [self.sb(st, f"hT{i}", [128, 16, 512], BF16) for i in range(2)]
                r_hTs = [[Res() for _ in range(4)] for _ in range(2)]
                Wc = [self.sb(st, f"wc{i}", [128, 16, 512], BF16) for i in range(2)]
                ws = WStream(Wc, [Res(), Res()])
                xin = self.sb(st, "xin", [128, D], F32)
                xn = [self.sb(st, f"xn{i}", [128, D], BF16) for i in range(2)]
                ss1 = self.sb(st, "ss1", [128, 1], F32)
                hbufs = (xin, Res(), xn, [Res(), Res()], ss1, Res())
                gbufs = ([self.sb(st, f"ge{i}", [128, 512], F32) for i in range(2)], [Res(), Res()],
                         [self.sb(st, f"gc{i}", [128, 512], BF16) for i in range(2)], [Res(), Res()])
                cstage = self.sb(st, "cstage", [128, 8, 512], BF16)
                kstage = self.sb(st, "kstage", [128, 8, 64], BF16)
                r_cst = Res()
                nb16 = [self.sb(st, f"nb16{i}", [128, 512], BF16) for i in range(3)]
                r_nb16 = [Res(), Res(), Res()]
                cf = [self.sb(st, f"cf{i}", [128, 512], F32) for i in range(2)]
                r_cf = [Res(), Res()]
                ssb = [self.sb(st, f"ssb{i}", [128, 1], F32) for i in range(2)]
                r_ssb = [Res(), Res()]
                jk = self.sb(st, "jk", [128, 512], BF16)
                r_jk = Res()
                krn = [self.sb(st, f"krn{i}", [128, 1, 64], F32) for i in range(3)]
                kro = [self.sb(st, f"kro{i}", [128, 1, 64], F32) for i in range(3)]
                krb = [self.sb(st, f"krb{i}", [128, 64], BF16) for i in range(3)]
                r_krn = [Res() for _ in range(3)]
                r_kro = [Res() for _ in range(3)]
                r_krb = [Res() for _ in range(3)]
                sgs = [self.sb(st, f"sgs{i}", [128, 512], BF16) for i in range(2)]
                r_sgs = [Res(), Res()]
                self.cnt = 0

                r_krpad = Res()
                self.memset(krT_p[64:128, :], 0.0, [], [r_krpad], eng="dve")
                self.memset(krT_s[64:128, :], 0.0, [], [r_krpad], eng="dve")
                self.dma(ropet[:], rope_cs.rearrange("(b p) c -> p b c", p=128), [], [r_c])
                self.dma(cstage[:, :, :], c_bc.rearrange("(t p) c -> p t c", p=128), [], [r_cst], q="pool")
                self.dma(kstage[:, :, :], c_br.rearrange("(t p) c -> p t c", p=128), [], [r_cst], q="pool")
                for t8 in range(8):
                    trb, rtr = next_tr()
                    tb = trb[:].bitcast(BF16)
                    for kt in range(4):
                        self.tr(tb[:, kt * 128:(kt + 1) * 128], cstage[:, t8, 128 * kt:128 * kt + 128], ident_bf[:], [r_cst, r_c], [rtr])
                    self.acopy(ckvT_s[:, 0:4, 128 * t8:128 * t8 + 128], tb[:, 0:512].rearrange("p (a b) -> p a b", a=4),
                               [rtr], [r_ckvT[SP0 + t8 // 4]])
                    trb, rtr = next_tr()
                    tb = trb[:].bitcast(BF16)
                    self.tr(tb[0:64, 0:128], kstage[:, t8, :], ident_bf[:], [r_cst, r_c], [rtr])
                    self.acopy(krT_s[0:64, 128 * t8:128 * t8 + 128], tb[0:64, 0:128], [rtr], [r_krT[SP0 + t8 // 4]])

                def b1_superblock(blocks, cqT, cq_c0, r_cq, ckvT, kr_T, ck_c0, r_ck, r_kr, ckv_out, kr_out, sg_c0, blk0,
                                  hi, next_blocks):
                    nb = len(blocks)
                    ntok = 128 * nb
                    hT, r_hT = hTs[hi], r_hTs[hi]
                    hTn, r_hTn = hTs[1 - hi], r_hTs[1 - hi]
                    r_h = r_hT[0:nb]
                    staged = {}

                    def stage1_load(c):
                        if next_blocks is not None and c < len(next_blocks):
                            prep_load(hbufs, next_blocks[c][0], next_blocks[c][1])

                    def stage1_norm(c):
                        if next_blocks is not None and c < len(next_blocks):
                            staged[c] = prep_norm(hbufs)

                    def stage2(c):
                        if c in staged:
                            xn_, rxn_ = staged.pop(c)
                            prep_stage2(xn_, rxn_, gcolB, hTn, 128 * c, r_hTn[c])
                    tails = []

                    def run_tails(keep=0):
                        while len(tails) > keep:
                            tails.pop(0)()

                    ws.prefetch("b_in", 0)
                    for j in range(7):
                        if j + 1 < 7:
                            ws.prefetch("b_in", j + 1)
                        wc, rwc = ws.pop()
                        if j > 3:
                            stage2(j - 4)
                        if j >= 3:
                            stage1_load(j - 3)
                        if j < 2:
                            for b in range(nb):
                                nr = blocks[b][1]
                                pj, rpj = next_pj()
                                for kt in range(16):
                                    self.mm(pj[:, :], hT[:, kt, 128 * b:128 * b + 128], wc[:, kt, :], kt == 0, kt == 15, [r_h[b], rwc], [rpj])
                                k = self.cnt
                                self.cnt += 1
                                s1, rs1 = ssb[k % 2], r_ssb[k % 2]
                                self.act(jk[:], pj[:, :], AF.Square, [rpj], [r_jk, rs1], accum=s1[:, 0:1])
                                ry_, rr_ = self.rstd(s1[:, 0:1], 1, 1.0 / 512, rs1)
                                nbf, rnb = nb16[k % 3], r_nb16[k % 3]
                                if j == 0:
                                    self.stt(nbf[:], pj[:, :], ry_[:, 0:1], gqa_rep[:], ALU.mult, ALU.mult, [rpj, rr_, r_c], [rnb])
                                else:
                                    cfb, rcf = cf[k % 2], r_cf[k % 2]
                                    self.stt(cfb[:], pj[:, :], ry_[:, 0:1], gkva_rep[:], ALU.mult, ALU.mult, [rpj, rr_, r_c], [rcf])
                                    self.dma(ckv_out[128 * b:128 * b + nr, :], cfb[0:nr, :], [rcf], [], q="pool")

                                def tail(j=j, b=b, nbf=nbf, rnb=rnb, cfb=(cfb if j == 1 else None), rcf=(rcf if j == 1 else None)):
                                    if cfb is not None:
                                        self.acopy(nbf[:], cfb[:], [rcf], [rnb])
                                    trb, rtr = next_tr()
                                    tb = trb[:].bitcast(BF16)
                                    for kt in range(4):
                                        self.tr(tb[:, kt * 128:(kt + 1) * 128], nbf[:, 128 * kt:128 * kt + 128], ident_bf[:], [rnb, r_c], [rtr])
                                    src3 = tb[:, 0:512].rearrange("p (a b) -> p a b", a=4)
                                    if j == 0:
                                        self.acopy(cqT[:, 0:4, cq_c0 + 128 * b:cq_c0 + 128 * b + 128], src3, [rtr], [r_cq])
                                    else:
                                        self.acopy(ckvT[:, 0:4, ck_c0 + 128 * b:ck_c0 + 128 * b + 128], src3, [rtr], [r_ck])
                                run_tails()
                                tails.append(tail)
                        elif j == 2:
                            for b in range(nb):
                                nr = blocks[b][1]
                                pj, rpj = next_pj()
                                for kt in range(16):
                                    self.mm(pj[:, 0:64], hT[:, kt, 128 * b:128 * b + 128], wc[:, kt, 0:64], kt == 0, kt == 15, [r_h[b], rwc], [rpj])
                                k = self.cnt
                                self.cnt += 1
                                s1, rs1 = ssb[k % 2], r_ssb[k % 2]
                                self.act(jk[:, 0:64], pj[:, 0:64], AF.Square, [rpj], [r_jk, rs1], accum=s1[:, 0:1])
                                ry_, rr_ = self.rstd(s1[:, 0:1], 1, 1.0 / 64, rs1)
                                kn_, rkn = krn[k % 3], r_krn[k % 3]
                                ko_, rko = kro[k % 3], r_kro[k % 3]
                                kb_, rkb = krb[k % 3], r_krb[k % 3]
                                self.stt(kn_[:, 0, :], pj[:, 0:64], ry_[:, 0:1], gkr_rep[:], ALU.mult, ALU.mult, [rpj, rr_, r_c], [rkn])
                                rope(kn_, 1, blk0 + b, ko_, rkn, rko)
                                self.dma(kr_out[128 * b:128 * b + nr, :], ko_[0:nr, 0, :], [rko], [], q="pool")

                                def tail(b=b, kb_=kb_, rkb=rkb, ko_=ko_, rko=rko):
                                    self.acopy(kb_[:], ko_[:, 0, :], [rko], [rkb])
                                    trb, rtr = next_tr()
                                    tb = trb[:].bitcast(BF16)
                                    self.tr(tb[0:64, 0:128], kb_[:], ident_bf[:], [rkb, r_c], [rtr])
                                    self.acopy(kr_T[0:64, ck_c0 + 128 * b:ck_c0 + 128 * b + 128], tb[0:64, 0:128], [rtr], [r_kr])
                                run_tails(1)
                                tails.append(tail)
                        else:
                            for hh in range(4):
                                h = 4 * (j - 3) + hh
                                pj, rpj = next_pj()
                                for kt in range(16):
                                    self.mm(pj[:, 0:ntok], wc[:, kt, 128 * hh:128 * hh + 128], hT[:, kt, 0:ntok], kt == 0, kt == 15, r_h + [rwc], [rpj])
                                k = self.cnt
                                self.cnt += 1
                                sg_, rsg = sgs[k % 2], r_sgs[k % 2]
                                sigmoid_gate(pj, rpj, ntok, sg_[:, 0:ntok], rsg, gbufs)
                                self.dma(sgT_d[128 * h:128 * h + 128, sg_c0:sg_c0 + ntok], sg_[:, 0:ntok], [rsg], [], q="pool")
                                run_tails()
                            stage1_norm(j - 3)
                    run_tails()
                    stage2(3)

                try:
                    pbl = lambda s: [(y1[512 * s + 128 * b:512 * s + 128 * b + 128, :], 128) for b in range(4)]
                    sbl_ = [(y1[T:T + NS, :], NS)]
                    prep_hT(hbufs, [(src, nreal, 128 * b, r_hTs[0][b]) for b, (src, nreal) in enumerate(pbl(0))], gcolB, hTs[0], None)
                    nsb = NTP // 4
                    for s in range(nsb):
                        b1_superblock(pbl(s), cqT_p, 512 * s, r_cqT[s], ckvT_p, krT_p, 512 * s, r_ckvT[s], r_krT[s],
                                      o_bcp[512 * s:512 * s + 512, :], o_brp[512 * s:512 * s + 512, :], 512 * s, 4 * s,
                                      s % 2, pbl(s + 1) if s + 1 < nsb else sbl_)
                    self.stopat("b1prompt")
                    b1_superblock(sbl_, cqT_s, 0, r_cqT[SP0], ckvT_s, krT_s, 1024, r_ckvT[SP0 + 2], r_krT[SP0 + 2],
                                  o_bcs, o_brs, T, NTP, nsb % 2, None)
                except StopBuild:
                    pass
                S.flush()

            if ph == "B1":
                self._finish(S)
                return nc

            with ExitStack() as st:
                wuk = self.sb(st, "wuk", [128, 4, 512], BF16)
                wuv = self.sb(st, "wuv", [128, 4, 512], BF16)
                wuqn = self.sb(st, "wuqn", [128, 4, 512], BF16)
                wuqr = self.sb(st, "wuqr", [128, 4, 256], BF16)
                r_w = Res()
                kT_g = self.sb(st, "kT_g", [128, 4, max(T, 1152)], BF16)
                V_g = self.sb(st, "V_g", [128, max(NTP, 9), 512], BF16)
                r_kTg = [Res() for _ in range(max(NTP // 4, 3))]
                r_Vg = [Res() for _ in range(max(NTP // 4, 3))]
                r_qTr_pad = Res()
                r_qTr = [r_qTr_pad] * 2
                qTn = [self.sb(st, "qTn0", [128, 4, 512], BF16)] * 2
                qTr = [self.sb(st, "qTr0", [128, 4, 512], BF16)] * 2
                self.memset(qTr[0][64:128, :, :], 0.0, [], [r_qTr_pad], eng="dve")
                r_qTn = [Res()] * 2
                sgin = [self.sb(st, "sgin0", [128, 4, 512], BF16)] * 2
                r_sgin = [Res()] * 2
                PTs = [self.sb(st, f"pt{i}", [128, 512], BF16) for i in range(3)]
                rsb = self.sb(st, "rsb", [128, 512], F32)
                abufs = (PTs, [Res(), Res(), Res()], rsb, Res(), None, None)
                ogs = [self.sb(st, f"ogs{i}", [128, 512], BF16) for i in range(2)]
                r_ogs = [Res(), Res()]
                nb16 = [self.sb(st, f"nb16{i}", [128, 512], BF16) for i in range(4)]
                r_nb16 = [Res(), Res(), Res(), Res()]
                self.qri = 0
                qrn = [self.sb(st, f"qrn{i}", [128, 4, 64], F32) for i in range(2)]
                qro = [self.sb(st, f"qro{i}", [128, 4, 64], F32) for i in range(2)]
                qrb = [self.sb(st, f"qrb{i}", [128, 4, 64], BF16) for i in range(2)]
                r_qrn = [Res(), Res()]
                r_qro = [Res(), Res()]
                r_qrb = [Res(), Res()]
                self.cnt = 0
                self.par = 0

                ss12 = [self.sb(st, f"ss12_{i}", [128, 12], F32) for i in range(3)]
                r_ss12 = [[Res() for _ in range(12)] for _ in range(3)]
                sq12 = self.sb(st, "sq12", [128, 4, 128], BF16)
                r_sq12 = [Res() for _ in range(4)]
                self.s12i = 0

                def b2_seq(G, cqT, ckvT, kr_T, sblocks, og_base):
                    tails = []

                    def run_tails(keep=0):
                        while len(tails) > keep:
                            tails.pop(0)()

                    for si, (t0, nb, isq, r_cq, r_ck, r_kr, rblk0) in enumerate(sblocks):
                        ntok = 128 * nb
                        par = self.par
                        if isq:
                            self.par ^= 1
                        self.pj_pool = pool6
                        for b in range(nb):
                            c0 = t0 + 128 * b
                            q0 = 128 * b
                            qc = q0 + (t0 if cqT is cqT_p else 0)
                            s12, rs12 = ss12[self.s12i % 3], r_ss12[self.s12i % 3]
                            self.s12i += 1
                            pjk, rpjk = next_pj()
                            for kt in range(4):
                                self.mm(pjk[:, :], ckvT[:, kt, c0:c0 + 128], wuk[:, kt, :], kt == 0, kt == 3, [r_ck, r_w], [rpjk])
                            for hh in range(4):
                                self.act(sq12[:, hh, :], pjk[:, 128 * hh:128 * hh + 128], AF.Square, [rpjk], [r_sq12[hh], rs12[hh]],
                                         accum=s12[:, hh:hh + 1])
                            ncol = 4
                            if isq:
                                pjq, rpjq = next_pj()
                                for kt in range(4):
                                    self.mm(pjq[:, :], cqT[:, kt, qc:qc + 128], wuqn[:, kt, :], kt == 0, kt == 3, [r_cq, r_w], [rpjq])
                                for hh in range(4):
                                    self.act(sq12[:, hh, :], pjq[:, 128 * hh:128 * hh + 128], AF.Square, [rpjq],
                                             [r_sq12[hh], rs12[4 + hh]], accum=s12[:, 4 + hh:5 + hh])
                                pjr, rpjr = next_pj()
                                for kt in range(4):
                                    self.mm(pjr[:, 0:256], cqT[:, kt, qc:qc + 128], wuqr[:, kt, :], kt == 0, kt == 3, [r_cq, r_w], [rpjr])
                                for hh in range(4):
                                    self.act(sq12[:, hh, 0:64], pjr[:, 64 * hh:64 * hh + 64], AF.Square, [rpjr],
                                             [r_sq12[hh], rs12[8 + hh]], accum=s12[:, 8 + hh:9 + hh], scale=2.0 ** 0.5)
                                ncol = 12
                            pjv, rpjv = next_pj()
                            for kt in range(4):
                                self.mm(pjv[:, :], ckvT[:, kt, c0:c0 + 128], wuv[:, kt, :], kt == 0, kt == 3, [r_ck, r_w], [rpjv])
                            self.acopy(V_g[:, c0 // 128, :], pjv[:, :], [rpjv], [r_Vg[si]])
                            ry_, rr_ = self.rstd(s12[:, 0:ncol], ncol, 1.0 / 128, rs12[0:ncol])
                            k = self.cnt
                            self.cnt += 1
                            nbk, rnbk = nb16[k % 4], r_nb16[k % 4]
                            self.tt(nbk[:].rearrange("p (a b) -> p a b", a=4), pjk[:].rearrange("p (a b) -> p a b", a=4),
                                    ry_[:, 0:4].unsqueeze(2).to_broadcast([128, 4, 128]), ALU.mult, [rpjk, rr_], [rnbk])

                            def tailk(nbf=nbk, rnb=rnbk, c0=c0, si=si):
                                trb, rtr = next_tr()
                                tb = trb[:].bitcast(BF16)
                                for hh in range(4):
                                    self.tr(tb[:, hh * 128:(hh + 1) * 128], nbf[:, 128 * hh:128 * hh + 128], ident_bf[:], [rnb, r_c], [rtr])
                                self.act(kT_g[:, 0:4, c0:c0 + 128], tb[:, 0:512].rearrange("p (a b) -> p a b", a=4), AF.Identity,
                                         [rtr, r_c], [r_kTg[si]], scale=gkn_col[:, 0:1])
                            tails.append(tailk)
                            if isq:
                                k = self.cnt
                                self.cnt += 1
                                nbq, rnbq = nb16[k % 4], r_nb16[k % 4]
                                self.tt(nbq[:].rearrange("p (a b) -> p a b", a=4), pjq[:].rearrange("p (a b) -> p a b", a=4),
                                        ry_[:, 4:8].unsqueeze(2).to_broadcast([128, 4, 128]), ALU.mult, [rpjq, rr_], [rnbq])

                                def tailq(nbf=nbq, rnb=rnbq, q0=q0, par=par):
                                    trb, rtr = next_tr()
                                    tb = trb[:].bitcast(BF16)
                                    for hh in range(4):
                                        self.tr(tb[:, hh * 128:(hh + 1) * 128], nbf[:, 128 * hh:128 * hh + 128], ident_bf[:], [rnb, r_c], [rtr])
                                    self.act(qTn[par][:, 0:4, q0:q0 + 128], tb[:, 0:512].rearrange("p (a b) -> p a b", a=4), AF.Identity,
                                             [rtr, r_c], [r_qTn[par]], scale=gqn_col[:, 0:1])
                                tails.append(tailq)
                                kq = self.qri % 2
                                self.qri += 1
                                qn_, rqn = qrn[kq], r_qrn[kq]
                                qo_, rqo = qro[kq], r_qro[kq]
                                qb_, rqb = qrb[kq], r_qrb[kq]
                                self.tt(qn_[:], pjr[:, 0:256].rearrange("p (a b) -> p a b", a=4),
                                        ry_[:, 8:12].unsqueeze(2).to_broadcast([128, 4, 64]), ALU.mult, [rpjr, rr_], [rqn])
                                self.tt(qn_[:], qn_[:], gqr_rep[:, :].unsqueeze(1).to_broadcast([128, 4, 64]), ALU.mult, [rqn, r_c], [rqn],
                                        eng=ROPE_ENG)
                                rope(qn_, 4, rblk0 + b, qo_, rqn, rqo, eng=ROPE_ENG)

                                def tailr(qb_=qb_, rqb=rqb, q0=q0, par=par, qo_=qo_, rqo=rqo):
                                    self.acopy(qb_[:], qo_[:], [rqo], [rqb])
                                    trb, rtr = next_tr()
                                    tb = trb[:].bitcast(BF16)
                                    for hh in range(4):
                                        self.tr(tb[0:64, hh * 128:(hh + 1) * 128], qb_[:, hh, :], ident_bf[:], [rqb, r_c], [rtr])
                                    self.acopy(qTr[par][0:64, 0:4, q0:q0 + 128], tb[0:64, 0:512].rearrange("p (a b) -> p a b", a=4),
                                               [rtr], [r_qTr[par]])
                                tails.append(tailr)
                            run_tails(3 if isq else 1)
                        run_tails()
                        self.pj_pool = pool2
                        if not isq:
                            continue
                        nt0 = t0 // 128
                        og_off = og_base + t0
                        sg_, rsg = sgin[par], r_sgin[par]
                        self.dma(sg_[:, :, 0:ntok],
                                 sgT_d[512 * G:512 * G + 512, og_off:og_off + ntok].rearrange("(h p) t -> p h t", p=128), [], [rsg])
                        for hh in range(4):
                            h = 4 * G + hh
                            tl = []
                            for kt in range(nt0 + nb):
                                if kt < nt0:
                                    tl.append((kt, 0, ntok))
                                else:
                                    tl.append((kt, 128 * (kt - nt0), ntok))

                            def kfn(kt, hh=hh):
                                return [(kT_g[:, hh, 128 * kt:128 * kt + 128], r_kTg[kt // 4]),
                                        (kr_T[0:128, 128 * kt:128 * kt + 128], sblocks[kt // 4][5])]

                            def vfn(kt, hh=hh):
                                return (V_g[:, kt, 128 * hh:128 * hh + 128], r_Vg[kt // 4])

                            def post(kt, pt, rpt, c0, c1, nt0=nt0):
                                if kt >= nt0:
                                    j = kt - nt0
                                    self.memset(pt[64:128, 128 * j:128 * j + 64], 0.0, [rpt], [rpt])

                            k = self.cnt
                            self.cnt += 1
                            og_, rog = ogs[k % 2], r_ogs[k % 2]
                            def og_store(h=h, og_=og_, rog=rog, og_off=og_off, ntok=ntok):
                                self.dma(ogT_d[128 * h:128 * h + 128, og_off:og_off + ntok], og_[:, 0:ntok], [rog], [])
                            attention(ntok, tl, [(qTn[par][:, hh, :], r_qTn[par]), (qTr[par][0:128, hh, :], r_qTr[par])],
                                      kfn, vfn, B_SCALE, None, post, sg_[:, hh, 0:ntok], rsg, og_[:, 0:ntok], rog, abufs, og_store)
                        flush_epi()

                try:
                    for G in range(4):
                        self.dma(wuk[:], w_b_uk[:, 512 * G:512 * G + 512].rearrange("(kt p) c -> p kt c", p=128), [], [r_w], q="pool")
                        self.dma(wuv[:], w_b_uv[:, 512 * G:512 * G + 512].rearrange("(kt p) c -> p kt c", p=128), [], [r_w], q="pool")
                        for hh in range(4):
                            c0 = (4 * G + hh) * 192
                            self.dma(wuqn[:, :, 128 * hh:128 * hh + 128], w_b_uq[:, c0:c0 + 128].rearrange("(kt p) c -> p kt c", p=128),
                                     [], [r_w], q="pool")
                            self.dma(wuqr[:, :, 64 * hh:64 * hh + 64], w_b_uq[:, c0 + 128:c0 + 192].rearrange("(kt p) c -> p kt c", p=128),
                                     [], [r_w], q="pool")
                        sbl = [(512 * s, 4, True, r_cqT[s], r_ckvT[s], r_krT[s], 4 * s) for s in range(NTP // 4)]
                        b2_seq(G, cqT_p, ckvT_p, krT_p, sbl, 0)
                        self.stopat("b2prompt")
                        sbs = [(0, 4, False, None, r_ckvT[SP0], r_krT[SP0], 0), (512, 4, False, None, r_ckvT[SP0 + 1], r_krT[SP0 + 1], 0),
                               (1024, 1, True, r_cqT[SP0], r_ckvT[SP0 + 2], r_krT[SP0 + 2], NTP)]
                        b2_seq(G, cqT_s, ckvT_s, krT_s, sbs, T - 1024)
                        self.stopat("b2g0")
                except StopBuild:
                    pass
                S.flush()

        if ph == "B2":
            self._finish(S)
            return nc

        with ExitStack() as st:
            aT = self.sb(st, "aT", [128, 16, 1024], BF16)
            r_aT = Res()
            Wc = [self.sb(st, f"wc{i}", [128, 16, 512], BF16) for i in range(2)]
            ws = WStream(Wc, [Res(), Res()])
            ytile = [self.sb(st, f"ytile{i}", [128, 512], F32) for i in range(3)]
            xre = [self.sb(st, f"xre{i}", [128, 512], F32) for i in range(3)]
            obufs = (ytile[0:2], [Res(), Res()], xre[0:2], [Res(), Res()])
            for g0 in range(0, NTP, 8):
                nbg = min(8, NTP - g0)
                self.dma(aT[:, :, 0:128 * nbg], ogT_d[:, 128 * g0:128 * (g0 + nbg)].rearrange("(kt p) t -> p kt t", p=128), [], [r_aT])
                out_proj(ws, "b_out", aT, r_aT, nbg, y1[128 * g0:128 * (g0 + nbg), :], y_p[128 * g0:128 * (g0 + nbg), :], 128, obufs,
                         store_q="act")
            self.dma(aT[:, :, 0:128], ogT_d[:, T:T + 128].rearrange("(kt p) t -> p kt t", p=128), [], [r_aT])
            out_proj(ws, "b_out", aT, r_aT, 1, y1[T:T + NS, :], y_s, NS, obufs, store_q="act")
            S.flush()
        self._finish(S)
        return nc


    def _finish(self, S):
        self.stats = dict(ops=S.tot_ops, waits=S.tot_waits, incs=dict(S.incc), n_dma=S.n_dma)


_CACHE = {}


def _rope_table():
    half = 32
    inv = (np.float32(10000.0) ** (-np.arange(half, dtype=np.float32) / np.float32(half))).astype(np.float32)
    pos = np.concatenate([np.arange(T), PAST + np.arange(128)]).astype(np.float32)
    ang = (pos[:, None] * inv[None, :]).astype(np.float32)
    return np.concatenate([np.cos(ang), np.sin(ang)], axis=1).astype(np.float32)


def _build(phases="all"):
    if phases not in _CACHE:
        b = Builder(phases)
        nc = b.build()
        _CACHE[phases] = (nc, b)
    return _CACHE[phases]


def kernel(**inputs):
    phases = inputs.pop("_phases", "all")
    inputs = dict(inputs)
    nc, b = _build(phases)
    f = lambda a: np.ascontiguousarray(np.asarray(a, dtype=np.float32))
    rope = _rope_table()
    shared = {
        "a_ln": f(inputs["a_ln"][0]), "w_a_in": f(inputs["w_a_in"][0]), "a_q_norm": f(inputs["a_q_norm"][0]),
        "a_k_norm": f(inputs["a_k_norm"][0]), "a_rel_bias": f(inputs["a_rel_bias"][0]), "w_a_out": f(inputs["w_a_out"][0]),
        "b_ln": f(inputs["b_ln"][0]), "w_b_in": f(inputs["w_b_in"][0]), "b_q_a_norm": f(inputs["b_q_a_norm"][0]),
        "w_b_uq": f(inputs["w_b_uq"][0]), "b_kv_a_norm": f(inputs["b_kv_a_norm"][0]), "w_b_uk": f(inputs["w_b_uk"][0]),
        "w_b_uv": f(inputs["w_b_uv"][0]), "b_q_nope_norm": f(inputs["b_q_nope_norm"][0]),
        "b_k_nope_norm": f(inputs["b_k_nope_norm"][0]), "b_q_rope_norm": f(inputs["b_q_rope_norm"][0]),
        "b_k_rope_norm": f(inputs["b_k_rope_norm"][0]), "w_b_out": f(inputs["w_b_out"][0]), "rope_cs": rope,
    }
    in_maps = []
    for c in range(N_CORES):
        m = dict(shared)
        m["x_prompt"] = f(inputs["x_prompt"][c][:T])
        m["x_sample"] = f(inputs["x_sample"][c])
        m["cache_a_k"] = f(inputs["cache_a_k"][0, c]).reshape(512, D)
        m["cache_a_v"] = f(inputs["cache_a_v"][0, c]).reshape(512, D)
        m["cache_b_ckv"] = f(inputs["cache_b_ckv"][0, c])
        m["cache_b_krope"] = f(inputs["cache_b_krope"][0, c])
        in_maps.append(m)
    ncr = int(inputs.pop("_ncores", N_CORES))
    res = run_bass_kernel_spmd(nc, in_maps[:ncr], core_ids=list(range(ncr)))
    R = list(res.results) + [res.results[0]] * (N_CORES - ncr)
    st = lambda k: np.stack([R[c][k] for c in range(N_CORES)])
    y_p = st("y_prompt")
    y_s = st("y_sample")
    akp = st("new_a_k_prompt").reshape(1, N_CORES, 512, 16, 128)
    avp = st("new_a_v_prompt").reshape(1, N_CORES, 512, 16, 128)
    bcp = st("new_b_ckv_prompt")[None]
    brp = st("new_b_krope_prompt")[None]
    aks = st("new_a_k_sample").reshape(1, N_CORES, NS, 16, 128)
    avs = st("new_a_v_sample").reshape(1, N_CORES, NS, 16, 128)
    bcs = st("new_b_ckv_sample")[None]
    brs = st("new_b_krope_sample")[None]
    return (y_p, y_s, akp, avp, bcp, brp, aks, avs, bcs, brs)
```

```python
import numpy as np
from contextlib import ExitStack
import concourse.bass as bass
import concourse.mybir as mybir
from concourse.bass_utils import run_bass_kernel_spmd

F32 = mybir.dt.float32
BF16 = mybir.dt.bfloat16
I32 = mybir.dt.int32
AF = mybir.ActivationFunctionType
ALU = mybir.AluOpType

N_CORES = 8
T = 4096
D = 2048
NS = 64
PAST = 1024
EPS = 1e-6
A_SCALE = 128 ** -0.5
B_SCALE = 192 ** -0.5
BIN = 3136

ENGS = ("pe", "act", "dve", "pool", "sp")
CH = 30000
N_DMA_SEMS = 40
import os as _os0
ROPE_ENG = _os0.environ.get("DEV_ROPE_ENG", "dve")
EB_ENG = _os0.environ.get("DEV_EB_ENG", "pool")


class Res:
    __slots__ = ("w", "r", "excl")

    def __init__(self, excl=False):
        self.w = None
        self.r = []
        self.excl = excl


class Op:
    __slots__ = ("eng", "fn", "reads", "writes", "dma", "seq", "deps", "needs_inc", "inc_idx", "dsem", "dval")

    def __init__(self, eng, fn, reads, writes, dma):
        self.eng = eng
        self.fn = fn
        self.reads = reads
        self.writes = writes
        self.dma = dma
        self.deps = []
        self.needs_inc = False


class Sched:
    def __init__(self, nc, stack):
        self.nc = nc
        self.stack = stack
        self.e = {"pe": nc.tensor, "act": nc.scalar, "dve": nc.vector, "pool": nc.gpsimd, "sp": nc.sync}
        self.ops = []
        self.seqc = {e: 0 for e in ENGS}
        self.incc = {e: 0 for e in ENGS}
        self.waited = {e: {e2: -1 for e2 in ENGS} for e in ENGS}
        self.esems = {e: [] for e in ENGS}
        self.dsems = [stack.enter_context(nc.semaphore(f"sdma{i}")) for i in range(N_DMA_SEMS)]
        self.dcount = [0] * N_DMA_SEMS
        self.n_dma = 0
        self.tot_ops = 0
        self.tot_waits = 0

    def op(self, eng, fn, reads=(), writes=()):
        self.ops.append(Op(eng, fn, tuple(reads), tuple(writes), False))

    def dma(self, eng, fn, reads=(), writes=()):
        self.ops.append(Op(eng, fn, tuple(reads), tuple(writes), True))

    def _esem(self, e, idx):
        k = idx // CH
        while len(self.esems[e]) <= k:
            self.esems[e].append(self.stack.enter_context(self.nc.semaphore(f"s{e}{len(self.esems[e])}")))
        return self.esems[e][k], idx % CH + 1

    def flush(self):
        ops = self.ops
        self.ops = []
        waited = self.waited
        waited_dma = {e: set() for e in ENGS}
        ring = []
        allres = set()
        last_on = {}
        for o in ops:
            o.seq = self.seqc[o.eng]
            self.seqc[o.eng] += 1
            cand = []
            for r in o.reads:
                allres.add(r)
                if r.w is not None:
                    cand.append(r.w)
                if r.excl:
                    for d in r.r:
                        if d.eng != o.eng:
                            cand.append(d)
            for r in o.writes:
                allres.add(r)
                if r.w is not None:
                    cand.append(r.w)
                cand.extend(r.r)
            if o.dma:
                o.dsem = self.n_dma % N_DMA_SEMS
                self.n_dma += 1
                if len(ring) >= N_DMA_SEMS:
                    cand.append(ring[len(ring) - N_DMA_SEMS])
                ring.append(o)
            else:
                last_on[o.eng] = o
            best = {}
            for d in cand:
                if d is o:
                    continue
                if d.dma:
                    if id(d) not in waited_dma[o.eng]:
                        waited_dma[o.eng].add(id(d))
                        o.deps.append(d)
                else:
                    if d.eng == "pe" and o.eng == "pe" and not o.dma:
                        continue
                    if d.seq > waited[o.eng][d.eng]:
                        if d.eng not in best or d.seq > best[d.eng].seq:
                            best[d.eng] = d
            for e2, d in best.items():
                waited[o.eng][e2] = d.seq
                d.needs_inc = True
                o.deps.append(d)
            for r in o.reads:
                r.r.append(o)
            for r in o.writes:
                r.w = o
                r.r = []
        for o in last_on.values():
            o.needs_inc = True
        for o in ops:
            if o.dma:
                self.dcount[o.dsem] += 16
                o.dval = self.dcount[o.dsem]
            elif o.needs_inc:
                o.inc_idx = self.incc[o.eng]
                self.incc[o.eng] += 1
        for o in ops:
            eng = self.e[o.eng]
            for d in o.deps:
                if d.dma:
                    eng.wait_ge(self.dsems[d.dsem], d.dval)
                else:
                    s, v = self._esem(d.eng, d.inc_idx)
                    eng.wait_ge(s, v)
                self.tot_waits += 1
            ins = o.fn(eng)
            if o.dma:
                ins.then_inc(self.dsems[o.dsem], 16)
            elif o.needs_inc:
                s, v = self._esem(o.eng, o.inc_idx)
                ins.then_inc(s, 1)
        self.tot_ops += len(ops)
        for e in ENGS:
            eng = self.e[e]
            for e2 in ENGS:
                if e2 != e and self.incc[e2] > 0:
                    s, v = self._esem(e2, self.incc[e2] - 1)
                    eng.wait_ge(s, v)
            for i in range(N_DMA_SEMS):
                if self.dcount[i] > 0:
                    eng.wait_ge(self.dsems[i], self.dcount[i])
        for e in ENGS:
            for e2 in ENGS:
                waited[e][e2] = self.seqc[e2] - 1
        for r in allres:
            r.w = None
            r.r = []


class StopBuild(Exception):
    pass


import os as _os
_STOP = _os.environ.get("DEV_STOP", "")


class Builder:
    def stopat(self, tag):
        if _STOP == tag:
            raise StopBuild()

    def __init__(self, phases="all"):
        self.phases = phases
        self.nc = bass.Bass("TRN2", target_bir_lowering=False)
        self.gst = ExitStack()

    def din(self, name, shape, dt=F32):
        return self.nc.dram_tensor(name, list(shape), dt, kind="ExternalInput").ap()

    def dout(self, name, shape, dt=F32):
        return self.nc.dram_tensor(name, list(shape), dt, kind="ExternalOutput").ap()

    def dscr(self, name, shape, dt):
        return self.nc.dram_tensor(name, list(shape), dt, kind="Internal").ap()

    def sb(self, st, name, shape, dt):
        self._uid = getattr(self, "_uid", 0) + 1
        return st.enter_context(self.nc.sbuf_tensor(f"{name}_{self._uid}", list(shape), dt))

    def ps(self, st, name):
        return st.enter_context(self.nc.psum_tensor(name, [128, 512], F32))

    def mm(self, out, lhsT, rhs, start, stop, R, W):
        self.S.op("pe", lambda e: e.matmul(out, lhsT=lhsT, rhs=rhs, start=start, stop=stop,
                                           skip_group_check=True), R, W)

    def tr(self, out, in_, ident, R, W):
        self.S.op("pe", lambda e: e.transpose(out=out, in_=in_, identity=ident), R, W)

    def act(self, out, in_, func, R, W, bias=None, scale=None, accum=None):
        kw = {}
        if bias is not None:
            kw["bias"] = bias
        if scale is not None:
            kw["scale"] = scale
        if accum is not None:
            kw["accum_out"] = accum
        self.S.op("act", lambda e: e.activation(out=out, in_=in_, func=func, **kw), R, W)

    def acopy(self, out, in_, R, W):
        self.S.op("act", lambda e: e.activation(out=out, in_=in_, func=AF.Identity), R, W)

    def vcopy(self, out, in_, R, W, eng="dve"):
        self.S.op(eng, lambda e: e.tensor_copy(out=out, in_=in_), R, W)

    def tt(self, out, in0, in1, op, R, W, eng="dve"):
        self.S.op(eng, lambda e: e.tensor_tensor(out=out, in0=in0, in1=in1, op=op), R, W)

    def ts(self, out, in0, s1, s2, op0, op1, R, W, eng="dve"):
        if s2 is None:
            self.S.op(eng, lambda e: e.tensor_scalar(out=out, in0=in0, scalar1=s1, scalar2=None, op0=op0), R, W)
        else:
            self.S.op(eng, lambda e: e.tensor_scalar(out=out, in0=in0, scalar1=s1, scalar2=s2, op0=op0, op1=op1), R, W)

    def stt(self, out, in0, scalar, in1, op0, op1, R, W, eng="dve"):
        self.S.op(eng, lambda e: e.scalar_tensor_tensor(out=out, in0=in0, scalar=scalar, in1=in1, op0=op0, op1=op1), R, W)

    def memset(self, ap, val, R, W, eng="pool"):
        self.S.op(eng, lambda e: e.memset(ap, val), R, W)

    def dma(self, out, in_, R, W, q="sp", slow=False):
        if slow:
            self.S.dma(q, lambda e: e.dma_start(out=out, in_=in_, allow_slow_non_contiguous=True), R, W)
        else:
            self.S.dma(q, lambda e: e.dma_start(out=out, in_=in_), R, W)

    def rstd(self, ss, n, inv_d, r_ss):
        k = self.nti % 4
        self.nti += 1
        y = self.nt_y[:, 16 * k:16 * k + n]
        t = self.nt_t[:, 16 * k:16 * k + n]
        r = self.r_nt[k]
        x = ss
        rl = list(r_ss) if isinstance(r_ss, (list, tuple)) else [r_ss]
        self.ts(x, x, inv_d, EPS, ALU.mult, ALU.add, rl, rl)
        self.ts(y.bitcast(I32), x.bitcast(I32), 1, None, ALU.arith_shift_right, None, rl, [r])
        self.ts(y.bitcast(I32), y.bitcast(I32), -1, 0x5F3759DF, ALU.mult, ALU.add, [r], [r])
        for _ in range(2):
            self.stt(t, y, -0.5, y, ALU.mult, ALU.mult, [r], [r])
            self.tt(t, t, x, ALU.mult, [r] + rl, [r])
            self.stt(y, t, 1.5, y, ALU.add, ALU.mult, [r], [r])
        return y, r

    def build(self):
        nc = self.nc
        gst = self.gst
        S = self.S = Sched(nc, gst)
        ph = self.phases

        x_p = self.din("x_prompt", [T, D])
        x_s = self.din("x_sample", [NS, D])
        c_ak = self.din("cache_a_k", [512, D])
        c_av = self.din("cache_a_v", [512, D])
        c_bc = self.din("cache_b_ckv", [PAST, 512])
        c_br = self.din("cache_b_krope", [PAST, 64])
        a_ln = self.din("a_ln", [D])
        w_a_in = self.din("w_a_in", [D, 4 * D])
        a_qn = self.din("a_q_norm", [128])
        a_kn = self.din("a_k_norm", [128])
        a_rb = self.din("a_rel_bias", [16, 257])
        w_a_out = self.din("w_a_out", [D, D])
        b_ln = self.din("b_ln", [D])
        w_b_in = self.din("w_b_in", [D, BIN])
        b_qa = self.din("b_q_a_norm", [512])
        w_b_uq = self.din("w_b_uq", [512, 3072])
        b_kva = self.din("b_kv_a_norm", [512])
        w_b_uk = self.din("w_b_uk", [512, D])
        w_b_uv = self.din("w_b_uv", [512, D])
        b_qnn = self.din("b_q_nope_norm", [128])
        b_knn = self.din("b_k_nope_norm", [128])
        b_qrn = self.din("b_q_rope_norm", [64])
        b_krn = self.din("b_k_rope_norm", [64])
        w_b_out = self.din("w_b_out", [D, D])
        rope_cs = self.din("rope_cs", [T + 128, 64])

        y_p = self.dout("y_prompt", [T, D])
        y_s = self.dout("y_sample", [NS, D])
        o_akp = self.dout("new_a_k_prompt", [512, D])
        o_avp = self.dout("new_a_v_prompt", [512, D])
        o_bcp = self.dout("new_b_ckv_prompt", [T, 512])
        o_brp = self.dout("new_b_krope_prompt", [T, 64])
        o_aks = self.dout("new_a_k_sample", [NS, D])
        o_avs = self.dout("new_a_v_sample", [NS, D])
        o_bcs = self.dout("new_b_ckv_sample", [NS, 512])
        o_brs = self.dout("new_b_krope_sample", [NS, 64])

        TT = T + 128
        wa_in_bf = self.dscr("wa_in_bf", [16, 128, 16, 512], BF16)
        wa_out_bf = self.dscr("wa_out_bf", [4, 128, 16, 512], BF16)
        wb_in_bf = self.dscr("wb_in_bf", [7, 128, 16, 512], BF16)
        wb_out_bf = self.dscr("wb_out_bf", [4, 128, 16, 512], BF16)
        y1 = self.dscr("y1", [TT, D], F32)
        self.dbgA = (ph == "A")
        sgT_d = self.dscr("sgT_d", [D, TT], BF16)
        ogT_d = self.dscr("ogT_d", [D, TT], BF16)
        ext_d = self.dscr("ext_d", [16, 384], F32)
        self.wres = {}
        self.wdone = set()

        ident_bf = self.sb(gst, "ident_bf", [128, 128], BF16)
        ident_f = self.sb(gst, "ident_f", [128, 128], F32)
        ones_bf = self.sb(gst, "ones_bf", [128, 128], BF16)
        self.nt_y = self.sb(gst, "nt_y", [128, 64], F32)
        self.nt_t = self.sb(gst, "nt_t", [128, 64], F32)
        self.r_nt = [Res() for _ in range(4)]
        self.nti = 0
        r_c = Res()
        gcolA = self.sb(gst, "gcolA", [128, 16], F32)
        gcolB = self.sb(gst, "gcolB", [128, 16], F32)
        gq_col = self.sb(gst, "gq_col", [128, 1], F32)
        gk_col = self.sb(gst, "gk_col", [128, 1], F32)
        gk_rep = self.sb(gst, "gk_rep", [128, 128], F32)
        gqn_col = self.sb(gst, "gqn_col", [128, 1], F32)
        gkn_col = self.sb(gst, "gkn_col", [128, 1], F32)
        gqa_rep = self.sb(gst, "gqa_rep", [128, 512], F32)
        gkva_rep = self.sb(gst, "gkva_rep", [128, 512], F32)
        gqr_rep = self.sb(gst, "gqr_rep", [128, 64], F32)
        gkr_rep = self.sb(gst, "gkr_rep", [128, 64], F32)
        cb = self.sb(gst, "cb", [128, 16], F32)
        self.stEB = ExitStack()
        EB = self.sb(self.stEB, "EB", [128, 16, 2, 128], BF16)

        PJ = [self.ps(gst, f"pj{i}") for i in range(2)]
        TRB = [self.ps(gst, f"trb{i}") for i in range(2)]
        STB = [self.ps(gst, f"st{i}") for i in range(2)]
        OTB = self.ps(gst, "otb")
        SUMB = self.ps(gst, "sumb")
        r_PJ = [Res(True), Res(True)]
        r_TRB = [Res(True), Res(True)]
        r_ST = [Res(True), Res(True)]
        r_OT = Res(True)
        r_SUM = Res(True)
        self.pji = 0
        self.tri = 0

        self.pj_pool = [(PJ[0], r_PJ[0]), (PJ[1], r_PJ[1])]
        pool2 = list(self.pj_pool)
        pool6 = pool2 + [(STB[0], r_ST[0]), (STB[1], r_ST[1]), (OTB, r_OT), (SUMB, r_SUM)]

        def next_pj():
            i = self.pji % len(self.pj_pool)
            self.pji += 1
            return self.pj_pool[i]

        def next_tr():
            i = self.tri
            self.tri ^= 1
            return TRB[i], r_TRB[i]

        with ExitStack() as st:
            J = self.sb(st, "J", [128, 128], F32)
            tmp16 = self.sb(st, "tmp16", [16, 128], F32)
            tmp16b = self.sb(st, "tmp16b", [16, 128], F32)
            e_sb = self.sb(st, "e_sb", [16, 384], F32)
            hk = self.sb(st, "hk", [128, 16, 2, 128], F32)
            ebf = self.sb(st, "ebf", [128, 128], F32)
            cbrow = self.sb(st, "cbrow", [16, 1], F32)
            diagc = self.sb(st, "diagc", [16, 16], F32)
            ones_f = self.sb(st, "ones_f", [16, 128], F32)
            cneg = self.sb(st, "cneg", [128, 16], F32)
            r_J, r_t16, r_esb, r_hk, r_ebf, r_cbrow, r_ext = Res(), Res(), Res(), Res(), Res(), Res(), Res()
            self.memset(ident_f[:], 0.0, [], [r_c])
            S.op("pool", lambda e: e.affine_select(out=ident_f[:], in_=ident_f[:], pattern=[[-1, 128]],
                                                   compare_op=ALU.not_equal, fill=1.0, base=0, channel_multiplier=1), [r_c], [r_c])
            self.vcopy(ident_bf[:], ident_f[:], [r_c], [r_c])
            self.memset(ones_bf[:], 1.0, [], [r_c])
            self.memset(ones_f[:], 1.0, [], [r_cbrow])
            self.memset(J[:], 0.0, [], [r_J])
            S.op("pool", lambda e: e.affine_select(out=J[:], in_=J[:], pattern=[[1, 128]],
                                                   compare_op=ALU.not_equal, fill=1.0, base=-127, channel_multiplier=1), [r_J], [r_J])
            for (src, dst) in ((a_ln, gcolA), (b_ln, gcolB)):
                tb = tmp16 if dst is gcolA else tmp16b
                self.dma(tb[:], src.rearrange("(kt p) -> kt p", p=128), [], [r_t16])
                pj, rpj = next_pj()
                self.tr(pj[:, 0:16], tb[:], ident_f[0:16, 0:16], [r_t16, r_c], [rpj])
                self.acopy(dst[:], pj[:, 0:16], [rpj], [r_c])
            for (src, dst) in ((a_qn, gq_col), (a_kn, gk_col), (b_qnn, gqn_col), (b_knn, gkn_col)):
                self.dma(dst[:], src.rearrange("(p o) -> p o", o=1), [], [r_c], slow=True)
            self.ts(gq_col[:], gq_col[:], A_SCALE, None, ALU.mult, None, [r_c], [r_c])
            for (src, dst, n) in ((a_kn, gk_rep, 128), (b_qa, gqa_rep, 512), (b_kva, gkva_rep, 512),
                                  (b_qrn, gqr_rep, 64), (b_krn, gkr_rep, 64)):
                self.dma(dst[:], src.partition_broadcast(128), [], [r_c])
            self.dma(e_sb[:, 0:257], a_rb, [], [r_esb])
            self.vcopy(e_sb[:, 257:384], e_sb[:, 256:257].to_broadcast([16, 127]), [r_esb], [r_esb])
            self.dma(ext_d, e_sb[:], [r_esb], [r_ext])
            tab_t = a_rb.tensor
            ext_t = ext_d.tensor
            for h in range(16):
                self.dma(hk[:, h, 0, :], bass.AP(ext_t, h * 384 + 129, [[1, 128], [1, 128]]), [r_ext], [r_hk])
                self.dma(hk[:, h, 1, :], bass.AP(ext_t, h * 384 + 1, [[1, 128], [1, 128]]), [r_ext], [r_hk])
            self.dma(cbrow[:], bass.AP(tab_t, 256, [[257, 16], [1, 1]]), [], [r_cbrow], slow=True)
            self.ts(diagc[:], ident_f[0:16, 0:16], cbrow[:, 0:1], None, ALU.mult, None, [r_c, r_cbrow], [r_cbrow])
            pj, rpj = next_pj()
            self.mm(pj[:, 0:16], ones_f[:, :], diagc[:, :], True, True, [r_cbrow], [rpj])
            self.acopy(cb[:], pj[:, 0:16], [rpj], [r_c])
            self.ts(cneg[:], cb[:], -1.0, None, ALU.mult, None, [r_c], [r_cbrow])
            for h in range(16):
                for t in range(2):
                    self.act(ebf[:], hk[:, h, t, :], AF.Exp, [r_hk, r_cbrow], [r_ebf], bias=cneg[:, h:h + 1])
                    pj, rpj = next_pj()
                    self.mm(pj[:, 0:128], J[:], ebf[:], True, True, [r_J, r_ebf], [rpj])
                    self.vcopy(EB[:, h, t, :], pj[:, 0:128], [rpj], [r_c])
            self.memset(EB[64:128, :, 1, 0:64], 0.0, [r_c], [r_c], eng="dve")
            S.flush()

        if ph == "setup":
            self._finish(S)
            return nc

        def wsrc(wid, j):
            if wid == "a_in":
                return w_a_in[:, 512 * j:512 * j + 512].rearrange("(kt p) c -> p kt c", p=128), 512
            if wid == "a_out":
                return w_a_out[:, 512 * j:512 * j + 512].rearrange("(kt p) c -> p kt c", p=128), 512
            if wid == "b_out":
                return w_b_out[:, 512 * j:512 * j + 512].rearrange("(kt p) c -> p kt c", p=128), 512
            if wid == "b_in":
                if j < 2:
                    return w_b_in[:, 512 * j:512 * j + 512].rearrange("(kt p) c -> p kt c", p=128), 512
                if j == 2:
                    return w_b_in[:, 1024:1088].rearrange("(kt p) c -> p kt c", p=128), 64
                c0 = 1088 + 512 * (j - 3)
                return w_b_in[:, c0:c0 + 512].rearrange("(kt p) c -> p kt c", p=128), 512
            raise ValueError(wid)

        wscr = {"a_in": wa_in_bf, "a_out": wa_out_bf, "b_in": wb_in_bf, "b_out": wb_out_bf}

        class WStream:
            def __init__(s2, bufs, rbufs):
                s2.bufs = bufs
                s2.rb = rbufs
                s2.i = 0
                s2.q = []

            def prefetch(s2, wid, j):
                k = s2.i
                s2.i ^= 1
                buf, rb = s2.bufs[k], s2.rb[k]
                key = (wid, j)
                if key not in self.wres:
                    self.wres[key] = Res()
                rw = self.wres[key]
                src, ncol = wsrc(wid, j)
                if key in self.wdone:
                    self.dma(buf[:, :, 0:ncol], wscr[wid][j, :, :, 0:ncol], [rw], [rb])
                else:
                    self.dma(buf[:, :, 0:ncol], src, [], [rb], q="pool")
                    self.dma(wscr[wid][j, :, :, 0:ncol], buf[:, :, 0:ncol], [rb], [rw], q="pool")
                    self.wdone.add(key)
                s2.q.append((buf, rb))

            def pop(s2):
                return s2.q.pop(0)

        def prep_load(st_bufs, src, nreal):
            xin, r_xin, xns, r_xns, ss1, r_ss1 = st_bufs
            if nreal < 128:
                self.memset(xin[:], 0.0, [], [r_xin], eng="dve")
            self.dma(xin[0:nreal, :], src, [], [r_xin])

        def prep_norm(st_bufs):
            xin, r_xin, xns, r_xns, ss1, r_ss1 = st_bufs
            k = self.xni % 2
            self.xni += 1
            xn, r_xn = xns[k], r_xns[k]
            self.act(xn[:], xin[:], AF.Square, [r_xin], [r_xn, r_ss1], accum=ss1[:, 0:1])
            ry_, rr_ = self.rstd(ss1[:, 0:1], 1, 1.0 / D, r_ss1)
            self.act(xn[:], xin[:], AF.Identity, [r_xin, rr_], [r_xn], scale=ry_[:, 0:1])
            return xn, r_xn

        def prep_stage1(st_bufs, src, nreal):
            prep_load(st_bufs, src, nreal)
            return prep_norm(st_bufs)

        def prep_stage2(xn, r_xn, gcol, hT, col0, rblk):
            for half in range(2):
                trb, rtr = next_tr()
                tb = trb[:].bitcast(BF16)
                for j in range(8):
                    kt = half * 8 + j
                    self.tr(tb[:, j * 128:(j + 1) * 128], xn[:, kt * 128:(kt + 1) * 128], ident_bf[:], [r_xn, r_c], [rtr])
                self.tt(hT[:, half * 8:half * 8 + 8, col0:col0 + 128],
                        tb[:, 0:1024].rearrange("p (a b) -> p a b", a=8),
                        gcol[:, half * 8:half * 8 + 8].unsqueeze(2).to_broadcast([128, 8, 128]),
                        ALU.mult, [rtr, r_c], [rblk])

        def prep_hT(st_bufs, blocks, gcol, hT, r_hT):
            for (src, nreal, col0, rblk) in blocks:
                xn, r_xn = prep_stage1(st_bufs, src, nreal)
                prep_stage2(xn, r_xn, gcol, hT, col0, rblk)

        def attention(nq, tiles, qparts, kfn, vfn, escale, ebias, post, sgT_ap, r_sg, out_ap, r_out, rs_bufs, after_epi=None):
            PTs, r_PTs, rs, r_rs, tmp, r_tmp = rs_bufs
            n = len(tiles)
            pend = None
            if self.acci % 2 == 0:
                OTB_, r_OT_, SUMB_, r_SUM_ = PJ[0], r_PJ[0], PJ[1], r_PJ[1]
            else:
                OTB_, r_OT_, SUMB_, r_SUM_ = OTB, r_OT, SUMB, r_SUM
            self.acci += 1
            for i, (tid, c0, c1) in enumerate(tiles):
                stb, rst = st_pool[self.sti % 4]
                self.sti += 1
                kparts = kfn(tid)
                for pi, ((kap, rk), (qap, rq)) in enumerate(zip(kparts, qparts)):
                    self.mm(stb[:, c0:c1], kap, qap[:, c0:c1], pi == 0, pi == len(kparts) - 1, [rk, rq], [rst])
                pt, rpt = PTs[self.pti % 3], r_PTs[self.pti % 3]
                self.pti += 1
                kw = {}
                self.act(pt[:, c0:c1], stb[:, c0:c1], AF.Exp, [rst] + ([r_c] if ebias is not None else []), [rpt],
                         bias=ebias, scale=escale)
                post(tid, pt, rpt, c0, c1)
                if pend is not None:
                    pend()
                if i == min(2, n - 1) and self.pend_epi is not None:
                    self.pend_epi()
                    self.pend_epi = None
                vap, rv = vfn(tid)

                def pv(i=i, pt=pt, rpt=rpt, c0=c0, c1=c1, vap=vap, rv=rv):
                    self.mm(OTB_[:, c0:c1], vap, pt[:, c0:c1], i == 0, i == n - 1, [rv, rpt], [r_OT_])
                    self.mm(SUMB_[:, c0:c1], ones_bf[:], pt[:, c0:c1], i == 0, i == n - 1, [r_c, rpt], [r_SUM_])
                pend = pv
            pend()

            def epi():
                S.op("dve", lambda e: e.reciprocal(out=rs[:, 0:nq], in_=SUMB_[:, 0:nq]), [r_SUM_], [r_rs])
                self.stt(rs[:, 0:nq], OTB_[:, 0:nq], 0.5, rs[:, 0:nq], ALU.mult, ALU.mult, [r_OT_, r_rs], [r_rs])
                self.tt(out_ap, rs[:, 0:nq], sgT_ap, ALU.mult, [r_rs, r_sg], [r_out])
                if after_epi is not None:
                    after_epi()
            if self.pend_epi is not None:
                self.pend_epi()
            self.pend_epi = epi

        def flush_epi():
            if self.pend_epi is not None:
                self.pend_epi()
                self.pend_epi = None

        self.pti = 0
        self.gi = 0
        self.acci = 0
        self.sti = 0
        st_pool = [(STB[0], r_ST[0]), (STB[1], r_ST[1]), (TRB[0], r_TRB[0]), (TRB[1], r_TRB[1])]
        self.pend_epi = None
        self.xni = 0

        def sigmoid_gate(pj, rpj, nq, out_ap, r_out, gbufs):
            k = self.gi % 2
            self.gi += 1
            ge, r_ge, gc, r_gc = gbufs[0][k], gbufs[1][k], gbufs[2][k], gbufs[3][k]
            self.act(ge[:, 0:nq], pj[:, 0:nq], AF.Tanh, [rpj], [r_ge], scale=0.5)
            self.act(gc[:, 0:nq], pj[:, 0:nq], AF.Identity, [rpj], [r_gc])
            self.stt(out_ap, ge[:, 0:nq], 1.0, gc[:, 0:nq], ALU.add, ALU.mult, [r_ge, r_gc], [r_out])

        def out_proj(ws, wid, actT, r_act, nb, res_src, y_dst, nreal_last, obufs, after_chunk=None, store_q="pool"):
            ytile, r_y, xre, r_x = obufs
            ws.prefetch(wid, 0)
            k = 0
            for c in range(4):
                if c + 1 < 4:
                    ws.prefetch(wid, c + 1)
                wc, rwc = ws.pop()
                for b in range(nb):
                    nr = nreal_last if b == nb - 1 else 128
                    pj, rpj = next_pj()
                    for kt in range(16):
                        self.mm(pj[:, :], actT[:, kt, 128 * b:128 * b + 128], wc[:, kt, :], kt == 0, kt == 15,
                                [r_act, rwc], [rpj])
                    yt, ry, xr, rx = ytile[k % 2], r_y[k % 2], xre[k % 2], r_x[k % 2]
                    k += 1
                    self.dma(xr[0:nr, :], res_src[128 * b:128 * b + nr, 512 * c:512 * c + 512], [], [rx])
                    self.tt(yt[0:nr, :], pj[0:nr, :], xr[0:nr, :], ALU.add, [rpj, rx], [ry])
                    self.dma(y_dst[128 * b:128 * b + nr, 512 * c:512 * c + 512], yt[0:nr, :], [ry], [], q=store_q)
                if after_chunk is not None:
                    after_chunk(c)

        with ExitStack() as st:
            hT = self.sb(st, "hT", [128, 16, 512], BF16)
            r_hT = [Res() for _ in range(4)]
            Wc = [self.sb(st, f"wc{i}", [128, 16, 512], BF16) for i in range(2)]
            ws = WStream(Wc, [Res(), Res()])
            xin = self.sb(st, "xin", [128, D], F32)
            xn = [self.sb(st, f"xn{i}", [128, D], BF16) for i in range(2)]
            ss1 = self.sb(st, "ss1", [128, 1], F32)
            hbufs = (xin, Res(), xn, [Res(), Res()], ss1, Res())
            qT = [self.sb(st, f"qT{i}", [128, 4, 512], BF16) for i in range(2)]
            r_qT = [[Res() for _ in range(4)] for _ in range(2)]
            sgT = [self.sb(st, f"sgT{i}", [128, 4, 512], BF16) for i in range(2)]
            r_sgT = [[Res() for _ in range(4)] for _ in range(2)]
            kTr = self.sb(st, "kTr", [128, 16, 1024], BF16)
            r_kT = [[Res() for _ in range(2)] for _ in range(16)]
            Vr = self.sb(st, "Vr", [128, 8, D], BF16)
            r_V = [[Res() for _ in range(4)] for _ in range(8)]
            PTs = [self.sb(st, f"pt{i}", [128, 512], BF16) for i in range(3)]
            rsb = self.sb(st, "rsb", [128, 512], F32)
            abufs = (PTs, [Res(), Res(), Res()], rsb, Res(), None, None)
            gbufs = ([self.sb(st, f"ge{i}", [128, 512], F32) for i in range(2)], [Res(), Res()],
                     [self.sb(st, f"gc{i}", [128, 512], BF16) for i in range(2)], [Res(), Res()])
            ogT = self.sb(st, "ogT", [128, 16, 512], BF16)
            r_ogT = Res()
            ytile = [self.sb(st, f"ytile{i}", [128, 512], F32) for i in range(2)]
            xre = [self.sb(st, f"xre{i}", [128, 512], F32) for i in range(2)]
            obufs = (ytile, [Res(), Res()], xre, [Res(), Res()])
            qn = [self.sb(st, f"qn{i}", [128, 512], BF16) for i in range(4)]
            r_qn = [Res(), Res(), Res(), Res()]
            kf = [self.sb(st, f"kf{i}", [128, 512], F32) for i in range(2)]
            r_kf = [Res(), Res()]
            ss4 = [self.sb(st, f"ss4{i}", [128, 4], F32) for i in range(2)]
            r_ss4 = [[Res() for _ in range(4)] for _ in range(2)]
            sqj = self.sb(st, "sqj", [128, 4, 128], BF16)
            r_sqj = [Res() for _ in range(4)]
            vst = [self.sb(st, f"vst{i}", [128, 512], F32) for i in range(2)]
            r_vst = [Res(), Res()]
            self.qni = 0
            self.s4i = 0
            self.kfi = 0
            self.vsi = 0

            def a_prep(blocks, only=None):
                for b, (src, nreal) in enumerate(blocks):
                    if only is None or only == b:
                        prep_hT(hbufs, [(src, nreal, 128 * b, r_hT[b])], gcolA, hT, None)

            def a_superblock(blocks, hf, has_prev, y1_rows, kout, vout, x_rows, next_blocks=None):
                nb = len(blocks)
                ntok = 128 * nb
                pf = 1 - hf
                r_h = r_hT[0:nb]
                self.stopat("hT")
                tails = []

                def run_tails(keep=0):
                    while len(tails) > keep:
                        tails.pop(0)()

                order = []
                for G in range(4):
                    order += [("q", G), ("k", G), ("v", G), ("g", G)]
                cidx = {"q": 0, "k": 4, "v": 8, "g": 12}
                ws.prefetch("a_in", cidx[order[0][0]] + order[0][1])
                for oi, (kind, G) in enumerate(order):
                    if oi + 1 < len(order):
                        ws.prefetch("a_in", cidx[order[oi + 1][0]] + order[oi + 1][1])
                    wc, rwc = ws.pop()
                    par = G % 2
                    if oi == 1: self.stopat("q")
                    if oi == 2: self.stopat("k")
                    if oi == 3: self.stopat("v")
                    if kind in ("q", "k"):
                        for b in range(nb):
                            pj, rpj = next_pj()
                            for kt in range(16):
                                self.mm(pj[:, :], hT[:, kt, 128 * b:128 * b + 128], wc[:, kt, :], kt == 0, kt == 15,
                                        [r_h[b], rwc], [rpj])
                            s4, rs4 = ss4[self.s4i % 2], r_ss4[self.s4i % 2]
                            self.s4i += 1
                            for hh in range(4):
                                self.act(sqj[:, hh, :], pj[:, 128 * hh:128 * hh + 128], AF.Square, [rpj], [r_sqj[hh], rs4[hh]],
                                         accum=s4[:, hh:hh + 1])
                            ry_, rr_ = self.rstd(s4[:, 0:4], 4, 1.0 / 128, rs4)
                            qb, rqb = qn[self.qni % 4], r_qn[self.qni % 4]
                            self.qni += 1
                            self.tt(qb[:].rearrange("p (a b) -> p a b", a=4), pj[:].rearrange("p (a b) -> p a b", a=4),
                                    ry_[:, 0:4].unsqueeze(2).to_broadcast([128, 4, 128]), ALU.mult, [rpj, rr_], [rqb])
                            if kind == "k" and kout is not None:
                                kfb, rkf = kf[self.kfi % 2], r_kf[self.kfi % 2]
                                self.kfi += 1
                                self.tt(kfb[:].rearrange("p (a b) -> p a b", a=4), pj[:].rearrange("p (a b) -> p a b", a=4),
                                        ry_[:, 0:4].unsqueeze(2).to_broadcast([128, 4, 128]), ALU.mult, [rpj, rr_], [rkf])
                                self.tt(kfb[:].rearrange("p (a b) -> p a b", a=4), kfb[:].rearrange("p (a b) -> p a b", a=4),
                                        gk_rep[:, :].unsqueeze(1).to_broadcast([128, 4, 128]), ALU.mult, [rkf, r_c], [rkf])
                                nr = blocks[b][1]
                                self.dma(kout[128 * b:128 * b + nr, 512 * G:512 * G + 512], kfb[0:nr, :], [rkf], [], q="pool")

                            def tail(kind=kind, G=G, b=b, qb=qb, rqb=rqb, par=par):
                                trb, rtr = next_tr()
                                tb = trb[:].bitcast(BF16)
                                for hh in range(4):
                                    self.tr(tb[:, hh * 128:(hh + 1) * 128], qb[:, 128 * hh:128 * hh + 128], ident_bf[:],
                                            [rqb, r_c], [rtr])
                                src3 = tb[:, 0:512].rearrange("p (a b) -> p a b", a=4)
                                if kind == "q":
                                    dst = qT[par][:, 0:4, 128 * b:128 * b + 128]
                                    self.act(dst, src3, AF.Identity, [rtr, r_c], [r_qT[par][hh] for hh in range(4)],
                                             scale=gq_col[:, 0:1])
                                else:
                                    dst = kTr[:, 4 * G:4 * G + 4, 512 * hf + 128 * b:512 * hf + 128 * b + 128]
                                    self.act(dst, src3, AF.Identity, [rtr, r_c], [r_kT[4 * G + hh][hf] for hh in range(4)],
                                             scale=gk_col[:, 0:1])
                            run_tails(1)
                            tails.append(tail)
                    elif kind == "v":
                        for b in range(nb):
                            pj, rpj = next_pj()
                            for kt in range(16):
                                self.mm(pj[:, :], hT[:, kt, 128 * b:128 * b + 128], wc[:, kt, :], kt == 0, kt == 15,
                                        [r_h[b], rwc], [rpj])
                            self.acopy(Vr[:, 4 * hf + b, 512 * G:512 * G + 512], pj[:, :], [rpj], [r_V[4 * hf + b][G]])
                            if vout is not None:
                                vb, rvb = vst[self.vsi % 2], r_vst[self.vsi % 2]
                                self.vsi += 1
                                self.vcopy(vb[:], pj[:, :], [rpj], [rvb])
                                nr = blocks[b][1]
                                self.dma(vout[128 * b:128 * b + nr, 512 * G:512 * G + 512], vb[0:nr, :], [rvb], [], q="pool")
                            run_tails()
                    else:
                        for hh in range(4):
                            pj, rpj = next_pj()
                            for kt in range(16):
                                self.mm(pj[:, 0:ntok], wc[:, kt, 128 * hh:128 * hh + 128], hT[:, kt, 0:ntok], kt == 0, kt == 15,
                                        r_h + [rwc], [rpj])
                            sigmoid_gate(pj, rpj, ntok, sgT[par][:, hh, 0:ntok], r_sgT[par][hh], gbufs)
                            run_tails()
                        run_tails()
                        self.stopat("g")
                        for hh in range(4):
                            h = 4 * G + hh
                            tl = []
                            for t in [4, 3, 5, 2, 6, 1, 7, 0]:
                                if t < 4 and not has_prev:
                                    continue
                                if t >= 4 and t - 4 >= nb:
                                    continue
                                b0 = max(0, t - 4)
                                b1 = min(nb - 1, t)
                                tl.append((t, 128 * b0, 128 * (b1 + 1)))

                            def kfn(t, h=h):
                                half = pf if t < 4 else hf
                                c = 512 * half + 128 * (t % 4)
                                return [(kTr[:, h, c:c + 128], r_kT[h][half])]

                            def vfn(t, h=h, G=G):
                                slot = 4 * (pf if t < 4 else hf) + (t % 4)
                                return (Vr[:, slot, 128 * h:128 * h + 128], r_V[slot][G])

                            def post(t, pt, rpt, c0, c1, h=h):
                                b = t - 3
                                if 0 <= b < nb:
                                    self.tt(pt[:, 128 * b:128 * b + 128], pt[:, 128 * b:128 * b + 128], EB[:, h, 0, :],
                                            ALU.mult, [rpt, r_c], [rpt], eng=EB_ENG)
                                b = t - 4
                                if 0 <= b < nb:
                                    self.tt(pt[:, 128 * b:128 * b + 128], pt[:, 128 * b:128 * b + 128], EB[:, h, 1, :],
                                            ALU.mult, [rpt, r_c], [rpt], eng=EB_ENG)
                                b = t
                                if 0 <= b < nb:
                                    self.memset(pt[0:64, 128 * b + 64:128 * b + 128], 0.0, [rpt], [rpt])

                            attention(ntok, tl, [(qT[par][:, hh, :], r_qT[par][hh])], kfn, vfn, None, cb[:, h:h + 1], post,
                                      sgT[par][:, hh, 0:ntok], r_sgT[par][hh], ogT[:, h, 0:ntok], r_ogT, abufs)
                        flush_epi()
                self.stopat("attn")
                staged = {}

                def stage1(c):
                    if next_blocks is not None and c < len(next_blocks):
                        staged[c] = prep_stage1(hbufs, next_blocks[c][0], next_blocks[c][1])

                def after_chunk(c):
                    if c in staged:
                        xn_, rxn_ = staged.pop(c)
                        prep_stage2(xn_, rxn_, gcolA, hT, 128 * c, r_hT[c])
                    stage1(c + 1)
                stage1(0)
                out_proj(ws, "a_out", ogT, r_ogT, nb, x_rows, y1_rows, blocks[-1][1], obufs, after_chunk)

            try:
              pblocks = lambda s: [(x_p[512 * s + 128 * b:512 * s + 128 * b + 128, :], 128) for b in range(4)]
              sblocks_ = [(x_s[0:NS, :], NS)]
              a_prep(pblocks(0))
              for s in range(T // 512):
                blocks = pblocks(s)
                last = (s == T // 512 - 1)
                if s >= 1 or T // 512 == 1:
                    todo = [(w_, j_) for (w_, n_) in (("b_in", 7), ("b_out", 4)) for j_ in range(n_) if (w_, j_) not in self.wdone]
                    for (wid, j) in (todo if last else todo[:2]):
                        key = (wid, j)
                        self.wres[key] = Res()
                        src, ncol = wsrc(wid, j)
                        for q2 in range(2):
                            self.dma(wscr[wid][j, :, 8 * q2:8 * q2 + 8, 0:ncol], src[:, 8 * q2:8 * q2 + 8, :], [], [self.wres[key]], q="pool")
                        self.wdone.add(key)
                a_superblock(blocks, s % 2, s > 0, (y_p if self.dbgA else y1)[512 * s:512 * s + 512, :],
                             o_akp if last else None, o_avp if last else None, x_p[512 * s:512 * s + 512, :],
                             sblocks_ if last else pblocks(s + 1))
              self.stopat("prompt")
              kc = Wc[0][:].rearrange("p a b -> p (a b)")
              r_kc = ws.rb[0]
              self.dma(kc.rearrange("p (t c) -> p t c", t=4), c_ak.rearrange("(t p) c -> p t c", p=128), [], [r_kc], q="pool")
              self.dma(Vr[:, 0:4, :], c_av.rearrange("(t p) c -> p t c", p=128), [], [r_V[t4][g] for t4 in range(4) for g in range(4)], q="pool")
              for t4 in range(4):
                  for G in range(4):
                      trb, rtr = next_tr()
                      tb = trb[:].bitcast(BF16)
                      for hh in range(4):
                          h = 4 * G + hh
                          self.tr(tb[:, hh * 128:(hh + 1) * 128], kc[:, 2048 * t4 + 128 * h:2048 * t4 + 128 * h + 128], ident_bf[:],
                                  [r_kc, r_c], [rtr])
                      self.acopy(kTr[:, 4 * G:4 * G + 4, 128 * t4:128 * t4 + 128], tb[:, 0:512].rearrange("p (a b) -> p a b", a=4),
                                 [rtr], [r_kT[4 * G + hh][0] for hh in range(4)])
              a_superblock(sblocks_, 1, True, y_s if self.dbgA else y1[T:T + 128, :], o_aks, o_avs, x_s[0:NS, :])
            except StopBuild:
                pass
            S.flush()

        self.stEB.close()
        if ph == "A":
            self._finish(S)
            return nc
        NTP = T // 128
        with ExitStack() as stB:
            cqT_p = self.sb(stB, "cqT_p", [128, 4, T], BF16)
            ckvT_p = self.sb(stB, "ckvT_p", [128, 4, T], BF16)
            krT_p = self.sb(stB, "krT_p", [128, T], BF16)
            cqT_s = self.sb(stB, "cqT_s", [128, 4, 128], BF16)
            ckvT_s = self.sb(stB, "ckvT_s", [128, 4, 1152], BF16)
            krT_s = self.sb(stB, "krT_s", [128, 1152], BF16)
            ropet = self.sb(stB, "ropet", [128, NTP + 1, 64], F32)
            ra = self.sb(stB, "rope_a", [128, 4, 32], F32)
            rb_ = self.sb(stB, "rope_b", [128, 4, 32], F32)
            r_rope = Res()
            self.s4i = 0

            def rope(src, nh, blk, dst, r_src, r_dst, eng=None):
                cos = ropet[:, blk, 0:32].unsqueeze(1).to_broadcast([128, nh, 32])
                sin = ropet[:, blk, 32:64].unsqueeze(1).to_broadcast([128, nh, 32])
                x1 = src[:, :, 0:32]
                x2 = src[:, :, 32:64]
                a = ra[:, 0:nh, :]
                b_ = rb_[:, 0:nh, :]
                E = eng or "dve"
                self.tt(a, x1, cos, ALU.mult, [r_src, r_c], [r_rope], eng=E)
                self.tt(b_, x2, sin, ALU.mult, [r_src, r_c, r_rope], [r_rope], eng=E)
                self.tt(dst[:, :, 0:32], a, b_, ALU.subtract, [r_rope], [r_dst], eng=E)
                self.tt(a, x1, sin, ALU.mult, [r_src, r_c, r_rope], [r_rope], eng=E)
                self.tt(b_, x2, cos, ALU.mult, [r_src, r_c, r_rope], [r_rope], eng=E)
                self.tt(dst[:, :, 32:64], a, b_, ALU.add, [r_rope], [r_dst], eng=E)

            r_cqT = [Res() for _ in range(NTP // 4 + 1)]
            r_ckvT = [Res() for _ in range(NTP // 4 + 3)]
            r_krT = [Res() for _ in range(NTP // 4 + 3)]
            SP0 = NTP // 4

            with ExitStack() as st:
                hTs = [self.sb(st, f"hT{i}", [128, 16, 512], BF16) for i in range(2)]
                r_hTs = [[Res() for _ in range(4)] for _ in range(2)]
                Wc = [self.sb(st, f"wc{i}", [128, 16, 512], BF16) for i in range(2)]
                ws = WStream(Wc, [Res(), Res()])
                xin = self.sb(st, "xin", [128, D], F32)
                xn = [self.sb(st, f"xn{i}", [128, D], BF16) for i in range(2)]
                ss1 = self.sb(st, "ss1", [128, 1], F32)
                hbufs = (xin, Res(), xn, [Res(), Res()], ss1, Res())
                gbufs = ([self.sb(st, f"ge{i}", [128, 512], F32) for i in range(2)], [Res(), Res()],
                         [self.sb(st, f"gc{i}", [128, 512], BF16) for i in range(2)], [Res(), Res()])
                cstage = self.sb(st, "cstage", [128, 8, 512], BF16)
                kstage = self.sb(st, "kstage", [128, 8, 64], BF16)
                r_cst = Res()
                nb16 = [self.sb(st, f"nb16{i}", [128, 512], BF16) for i in range(3)]
                r_nb16 = [Res(), Res(), Res()]
                cf = [self.sb(st, f"cf{i}", [128, 512], F32) for i in range(2)]
                r_cf = [Res(), Res()]
                ssb = [self.sb(st, f"ssb{i}", [128, 1], F32) for i in range(2)]
                r_ssb = [Res(), Res()]
                jk = self.sb(st, "jk", [128, 512], BF16)
                r_jk = Res()
                krn = [self.sb(st, f"krn{i}", [128, 1, 64], F32) for i in range(3)]
                kro = [self.sb(st, f"kro{i}", [128, 1, 64], F32) for i in range(3)]
                krb = [self.sb(st, f"krb{i}", [128, 64], BF16) for i in range(3)]
                r_krn = [Res() for _ in range(3)]
                r_kro = [Res() for _ in range(3)]
                r_krb = [Res() for _ in range(3)]
                sgs = [self.sb(st, f"sgs{i}", [128, 512], BF16) for i in range(2)]
                r_sgs = [Res(), Res()]
                self.cnt = 0

                r_krpad = Res()
                self.memset(krT_p[64:128, :], 0.0, [], [r_krpad], eng="dve")
                self.memset(krT_s[64:128, :], 0.0, [], [r_krpad], eng="dve")
                self.dma(ropet[:], rope_cs.rearrange("(b p) c -> p b c", p=128), [], [r_c])
                self.dma(cstage[:, :, :], c_bc.rearrange("(t p) c -> p t c", p=128), [], [r_cst], q="pool")
                self.dma(kstage[:, :, :], c_br.rearrange("(t p) c -> p t c", p=128), [], [r_cst], q="pool")
                for t8 in range(8):
                    trb, rtr = next_tr()
                    tb = trb[:].bitcast(BF16)
                    for kt in range(4):
                        self.tr(tb[:, kt * 128:(kt + 1) * 128], cstage[:, t8, 128 * kt:128 * kt + 128], ident_bf[:], [r_cst, r_c], [rtr])
                    self.acopy(ckvT_s[:, 0:4, 128 * t8:128 * t8 + 128], tb[:, 0:512].rearrange("p (a b) -> p a b", a=4),
                               [rtr], [r_ckvT[SP0 + t8 // 4]])
                    trb, rtr = next_tr()
                    tb = trb[:].bitcast(BF16)
                    self.tr(tb[0:64, 0:128], kstage[:, t8, :], ident_bf[:], [r_cst, r_c], [rtr])
                    self.acopy(krT_s[0:64, 128 * t8:128 * t8 + 128], tb[0:64, 0:128], [rtr], [r_krT[SP0 + t8 // 4]])

                def b1_superblock(blocks, cqT, cq_c0, r_cq, ckvT, kr_T, ck_c0, r_ck, r_kr, ckv_out, kr_out, sg_c0, blk0,
                                  hi, next_blocks):
                    nb = len(blocks)
                    ntok = 128 * nb
                    hT, r_hT = hTs[hi], r_hTs[hi]
                    hTn, r_hTn = hTs[1 - hi], r_hTs[1 - hi]
                    r_h = r_hT[0:nb]
                    staged = {}

                    def stage1_load(c):
                        if next_blocks is not None and c < len(next_blocks):
                            prep_load(hbufs, next_blocks[c][0], next_blocks[c][1])

                    def stage1_norm(c):
                        if next_blocks is not None and c < len(next_blocks):
                            staged[c] = prep_norm(hbufs)

                    def stage2(c):
                        if c in staged:
                            xn_, rxn_ = staged.pop(c)
                            prep_stage2(xn_, rxn_, gcolB, hTn, 128 * c, r_hTn[c])
                    tails = []

                    def run_tails(keep=0):
                        while len(tails) > keep:
                            tails.pop(0)()

                    ws.prefetch("b_in", 0)
                    for j in range(7):
                        if j + 1 < 7:
                            ws.prefetch("b_in", j + 1)
                        wc, rwc = ws.pop()
                        if j > 3:
                            stage2(j - 4)
                        if j >= 3:
                            stage1_load(j - 3)
                        if j < 2:
                            for b in range(nb):
                                nr = blocks[b][1]
                                pj, rpj = next_pj()
                                for kt in range(16):
                                    self.mm(pj[:, :], hT[:, kt, 128 * b:128 * b + 128], wc[:, kt, :], kt == 0, kt == 15, [r_h[b], rwc], [rpj])
                                k = self.cnt
                                self.cnt += 1
                                s1, rs1 = ssb[k % 2], r_ssb[k % 2]
                                self.act(jk[:], pj[:, :], AF.Square, [rpj], [r_jk, rs1], accum=s1[:, 0:1])
                                ry_, rr_ = self.rstd(s1[:, 0:1], 1, 1.0 / 512, rs1)
                                nbf, rnb = nb16[k % 3], r_nb16[k % 3]
                                if j == 0:
                                    self.stt(nbf[:], pj[:, :], ry_[:, 0:1], gqa_rep[:], ALU.mult, ALU.mult, [rpj, rr_, r_c], [rnb])
                                else:
                                    cfb, rcf = cf[k % 2], r_cf[k % 2]
                                    self.stt(cfb[:], pj[:, :], ry_[:, 0:1], gkva_rep[:], ALU.mult, ALU.mult, [rpj, rr_, r_c], [rcf])
                                    self.dma(ckv_out[128 * b:128 * b + nr, :], cfb[0:nr, :], [rcf], [], q="pool")

                                def tail(j=j, b=b, nbf=nbf, rnb=rnb, cfb=(cfb if j == 1 else None), rcf=(rcf if j == 1 else None)):
                                    if cfb is not None:
                                        self.acopy(nbf[:], cfb[:], [rcf], [rnb])
                                    trb, rtr = next_tr()
                                    tb = trb[:].bitcast(BF16)
                                    for kt in range(4):
                                        self.tr(tb[:, kt * 128:(kt + 1) * 128], nbf[:, 128 * kt:128 * kt + 128], ident_bf[:], [rnb, r_c], [rtr])
                                    src3 = tb[:, 0:512].rearrange("p (a b) -> p a b", a=4)
                                    if j == 0:
                                        self.acopy(cqT[:, 0:4, cq_c0 + 128 * b:cq_c0 + 128 * b + 128], src3, [rtr], [r_cq])
                                    else:
                                        self.acopy(ckvT[:, 0:4, ck_c0 + 128 * b:ck_c0 + 128 * b + 128], src3, [rtr], [r_ck])
                                run_tails()
                                tails.append(tail)
                        elif j == 2:
                            for b in range(nb):
                                nr = blocks[b][1]
                                pj, rpj = next_pj()
                                for kt in range(16):
                                    self.mm(pj[:, 0:64], hT[:, kt, 128 * b:128 * b + 128], wc[:, kt, 0:64], kt == 0, kt == 15, [r_h[b], rwc], [rpj])
                                k = self.cnt
                                self.cnt += 1
                                s1, rs1 = ssb[k % 2], r_ssb[k % 2]
                                self.act(jk[:, 0:64], pj[:, 0:64], AF.Square, [rpj], [r_jk, rs1], accum=s1[:, 0:1])
                                ry_, rr_ = self.rstd(s1[:, 0:1], 1, 1.0 / 64, rs1)
                                kn_, rkn = krn[k % 3], r_krn[k % 3]
                                ko_, rko = kro[k % 3], r_kro[k % 3]
                                kb_, rkb = krb[k % 3], r_krb[k % 3]
                                self.stt(kn_[:, 0, :], pj[:, 0:64], ry_[:, 0:1], gkr_rep[:], ALU.mult, ALU.mult, [rpj, rr_, r_c], [rkn])
                                rope(kn_, 1, blk0 + b, ko_, rkn, rko)
                                self.dma(kr_out[128 * b:128 * b + nr, :], ko_[0:nr, 0, :], [rko], [], q="pool")

                                def tail(b=b, kb_=kb_, rkb=rkb, ko_=ko_, rko=rko):
                                    self.acopy(kb_[:], ko_[:, 0, :], [rko], [rkb])
                                    trb, rtr = next_tr()
                                    tb = trb[:].bitcast(BF16)
                                    self.tr(tb[0:64, 0:128], kb_[:], ident_bf[:], [rkb, r_c], [rtr])
                                    self.acopy(kr_T[0:64, ck_c0 + 128 * b:ck_c0 + 128 * b + 128], tb[0:64, 0:128], [rtr], [r_kr])
                                run_tails(1)
                                tails.append(tail)
                        else:
                            for hh in range(4):
                                h = 4 * (j - 3) + hh
                                pj, rpj = next_pj()
                                for kt in range(16):
                                    self.mm(pj[:, 0:ntok], wc[:, kt, 128 * hh:128 * hh + 128], hT[:, kt, 0:ntok], kt == 0, kt == 15, r_h + [rwc], [rpj])
                                k = self.cnt
                                self.cnt += 1
                                sg_, rsg = sgs[k % 2], r_sgs[k % 2]
                                sigmoid_gate(pj, rpj, ntok, sg_[:, 0:ntok], rsg, gbufs)
                                self.dma(sgT_d[128 * h:128 * h + 128, sg_c0:sg_c0 + ntok], sg_[:, 0:ntok], [rsg], [], q="pool")
                                run_tails()
                            stage1_norm(j - 3)
                    run_tails()
                    stage2(3)

                try:
                    pbl = lambda s: [(y1[512 * s + 128 * b:512 * s + 128 * b + 128, :], 128) for b in range(4)]
                    sbl_ = [(y1[T:T + NS, :], NS)]
                    prep_hT(hbufs, [(src, nreal, 128 * b, r_hTs[0][b]) for b, (src, nreal) in enumerate(pbl(0))], gcolB, hTs[0], None)
                    nsb = NTP // 4
                    for s in range(nsb):
                        b1_superblock(pbl(s), cqT_p, 512 * s, r_cqT[s], ckvT_p, krT_p, 512 * s, r_ckvT[s], r_krT[s],
                                      o_bcp[512 * s:512 * s + 512, :], o_brp[512 * s:512 * s + 512, :], 512 * s, 4 * s,
                                      s % 2, pbl(s + 1) if s + 1 < nsb else sbl_)
                    self.stopat("b1prompt")
                    b1_superblock(sbl_, cqT_s, 0, r_cqT[SP0], ckvT_s, krT_s, 1024, r_ckvT[SP0 + 2], r_krT[SP0 + 2],
                                  o_bcs, o_brs, T, NTP, nsb % 2, None)
                except StopBuild:
                    pass
                S.flush()

            if ph == "B1":
                self._finish(S)
                return nc

            with ExitStack() as st:
                wuk = self.sb(st, "wuk", [128, 4, 512], BF16)
                wuv = self.sb(st, "wuv", [128, 4, 512], BF16)
                wuqn = self.sb(st, "wuqn", [128, 4, 512], BF16)
                wuqr = self.sb(st, "wuqr", [128, 4, 256], BF16)
                r_w = Res()
                kT_g = self.sb(st, "kT_g", [128, 4, max(T, 1152)], BF16)
                V_g = self.sb(st, "V_g", [128, max(NTP, 9), 512], BF16)
                r_kTg = [Res() for _ in range(max(NTP // 4, 3))]
                r_Vg = [Res() for _ in range(max(NTP // 4, 3))]
                r_qTr_pad = Res()
                r_qTr = [r_qTr_pad] * 2
                qTn = [self.sb(st, "qTn0", [128, 4, 512], BF16)] * 2
                qTr = [self.sb(st, "qTr0", [128, 4, 512], BF16)] * 2
                self.memset(qTr[0][64:128, :, :], 0.0, [], [r_qTr_pad], eng="dve")
                r_qTn = [Res()] * 2
                sgin = [self.sb(st, "sgin0", [128, 4, 512], BF16)] * 2
                r_sgin = [Res()] * 2
                PTs = [self.sb(st, f"pt{i}", [128, 512], BF16) for i in range(3)]
                rsb = self.sb(st, "rsb", [128, 512], F32)
                abufs = (PTs, [Res(), Res(), Res()], rsb, Res(), None, None)
                ogs = [self.sb(st, f"ogs{i}", [128, 512], BF16) for i in range(2)]
                r_ogs = [Res(), Res()]
                nb16 = [self.sb(st, f"nb16{i}", [128, 512], BF16) for i in range(4)]
                r_nb16 = [Res(), Res(), Res(), Res()]
                self.qri = 0
                qrn = [self.sb(st, f"qrn{i}", [128, 4, 64], F32) for i in range(2)]
                qro = [self.sb(st, f"qro{i}", [128, 4, 64], F32) for i in range(2)]
                qrb = [self.sb(st, f"qrb{i}", [128, 4, 64], BF16) for i in range(2)]
                r_qrn = [Res(), Res()]
                r_qro = [Res(), Res()]
                r_qrb = [Res(), Res()]
                self.cnt = 0
                self.par = 0

                ss12 = [self.sb(st, f"ss12_{i}", [128, 12], F32) for i in range(3)]
                r_ss12 = [[Res() for _ in range(12)] for _ in range(3)]
                sq12 = self.sb(st, "sq12", [128, 4, 128], BF16)
                r_sq12 = [Res() for _ in range(4)]
                self.s12i = 0

                def b2_seq(G, cqT, ckvT, kr_T, sblocks, og_base):
                    tails = []

                    def run_tails(keep=0):
                        while len(tails) > keep:
                            tails.pop(0)()

                    for si, (t0, nb, isq, r_cq, r_ck, r_kr, rblk0) in enumerate(sblocks):
                        ntok = 128 * nb
                        par = self.par
                        if isq:
                            self.par ^= 1
                        self.pj_pool = pool6
                        for b in range(nb):
                            c0 = t0 + 128 * b
                            q0 = 128 * b
                            qc = q0 + (t0 if cqT is cqT_p else 0)
                            s12, rs12 = ss12[self.s12i % 3], r_ss12[self.s12i % 3]
                            self.s12i += 1
                            pjk, rpjk = next_pj()
                            for kt in range(4):
                                self.mm(pjk[:, :], ckvT[:, kt, c0:c0 + 128], wuk[:, kt, :], kt == 0, kt == 3, [r_ck, r_w], [rpjk])
                            for hh in range(4):
                                self.act(sq12[:, hh, :], pjk[:, 128 * hh:128 * hh + 128], AF.Square, [rpjk], [r_sq12[hh], rs12[hh]],
                                         accum=s12[:, hh:hh + 1])
                            ncol = 4
                            if isq:
                                pjq, rpjq = next_pj()
                                for kt in range(4):
                                    self.mm(pjq[:, :], cqT[:, kt, qc:qc + 128], wuqn[:, kt, :], kt == 0, kt == 3, [r_cq, r_w], [rpjq])
                                for hh in range(4):
                                    self.act(sq12[:, hh, :], pjq[:, 128 * hh:128 * hh + 128], AF.Square, [rpjq],
                                             [r_sq12[hh], rs12[4 + hh]], accum=s12[:, 4 + hh:5 + hh])
                                pjr, rpjr = next_pj()
                                for kt in range(4):
                                    self.mm(pjr[:, 0:256], cqT[:, kt, qc:qc + 128], wuqr[:, kt, :], kt == 0, kt == 3, [r_cq, r_w], [rpjr])
                                for hh in range(4):
                                    self.act(sq12[:, hh, 0:64], pjr[:, 64 * hh:64 * hh + 64], AF.Square, [rpjr],
                                             [r_sq12[hh], rs12[8 + hh]], accum=s12[:, 8 + hh:9 + hh], scale=2.0 ** 0.5)
                                ncol = 12
                            pjv, rpjv = next_pj()
                            for kt in range(4):
                                self.mm(pjv[:, :], ckvT[:, kt, c0:c0 + 128], wuv[:, kt, :], kt == 0, kt == 3, [r_ck, r_w], [rpjv])
                            self.acopy(V_g[:, c0 // 128, :], pjv[:, :], [rpjv], [r_Vg[si]])
                            ry_, rr_ = self.rstd(s12[:, 0:ncol], ncol, 1.0 / 128, rs12[0:ncol])
                            k = self.cnt
                            self.cnt += 1
                            nbk, rnbk = nb16[k % 4], r_nb16[k % 4]
                            self.tt(nbk[:].rearrange("p (a b) -> p a b", a=4), pjk[:].rearrange("p (a b) -> p a b", a=4),
                                    ry_[:, 0:4].unsqueeze(2).to_broadcast([128, 4, 128]), ALU.mult, [rpjk, rr_], [rnbk])

                            def tailk(nbf=nbk, rnb=rnbk, c0=c0, si=si):
                                trb, rtr = next_tr()
                                tb = trb[:].bitcast(BF16)
                                for hh in range(4):
                                    self.tr(tb[:, hh * 128:(hh + 1) * 128], nbf[:, 128 * hh:128 * hh + 128], ident_bf[:], [rnb, r_c], [rtr])
                                self.act(kT_g[:, 0:4, c0:c0 + 128], tb[:, 0:512].rearrange("p (a b) -> p a b", a=4), AF.Identity,
                                         [rtr, r_c], [r_kTg[si]], scale=gkn_col[:, 0:1])
                            tails.append(tailk)
                            if isq:
                                k = self.cnt
                                self.cnt += 1
                                nbq, rnbq = nb16[k % 4], r_nb16[k % 4]
                                self.tt(nbq[:].rearrange("p (a b) -> p a b", a=4), pjq[:].rearrange("p (a b) -> p a b", a=4),
                                        ry_[:, 4:8].unsqueeze(2).to_broadcast([128, 4, 128]), ALU.mult, [rpjq, rr_], [rnbq])

                                def tailq(nbf=nbq, rnb=rnbq, q0=q0, par=par):
                                    trb, rtr = next_tr()
                                    tb = trb[:].bitcast(BF16)
                                    for hh in range(4):
                                        self.tr(tb[:, hh * 128:(hh + 1) * 128], nbf[:, 128 * hh:128 * hh + 128], ident_bf[:], [rnb, r_c], [rtr])
                                    self.act(qTn[par][:, 0:4, q0:q0 + 128], tb[:, 0:512].rearrange("p (a b) -> p a b", a=4), AF.Identity,
                                             [rtr, r_c], [r_qTn[par]], scale=gqn_col[:, 0:1])
                                tails.append(tailq)
                                kq = self.qri % 2
                                self.qri += 1
                                qn_, rqn = qrn[kq], r_qrn[kq]
                                qo_, rqo = qro[kq], r_qro[kq]
                                qb_, rqb = qrb[kq], r_qrb[kq]
                                self.tt(qn_[:], pjr[:, 0:256].rearrange("p (a b) -> p a b", a=4),
                                        ry_[:, 8:12].unsqueeze(2).to_broadcast([128, 4, 64]), ALU.mult, [rpjr, rr_], [rqn])
                                self.tt(qn_[:], qn_[:], gqr_rep[:, :].unsqueeze(1).to_broadcast([128, 4, 64]), ALU.mult, [rqn, r_c], [rqn],
                                        eng=ROPE_ENG)
                                rope(qn_, 4, rblk0 + b, qo_, rqn, rqo, eng=ROPE_ENG)

                                def tailr(qb_=qb_, rqb=rqb, q0=q0, par=par, qo_=qo_, rqo=rqo):
                                    self.acopy(qb_[:], qo_[:], [rqo], [rqb])
                                    trb, rtr = next_tr()
                                    tb = trb[:].bitcast(BF16)
                                    for hh in range(4):
                                        self.tr(tb[0:64, hh * 128:(hh + 1) * 128], qb_[:, hh, :], ident_bf[:], [rqb, r_c], [rtr])
                                    self.acopy(qTr[par][0:64, 0:4, q0:q0 + 128], tb[0:64, 0:512].rearrange("p (a b) -> p a b", a=4),
                                               [rtr], [r_qTr[par]])
                                tails.append(tailr)
                            run_tails(3 if isq else 1)
                        run_tails()
                        self.pj_pool = pool2
                        if not isq:
                            continue
                        nt0 = t0 // 128
                        og_off = og_base + t0
                        sg_, rsg = sgin[par], r_sgin[par]
                        self.dma(sg_[:, :, 0:ntok],
                                 sgT_d[512 * G:512 * G + 512, og_off:og_off + ntok].rearrange("(h p) t -> p h t", p=128), [], [rsg])
                        for hh in range(4):
                            h = 4 * G + hh
                            tl = []
                            for kt in range(nt0 + nb):
                                if kt < nt0:
                                    tl.append((kt, 0, ntok))
                                else:
                                    tl.append((kt, 128 * (kt - nt0), ntok))

                            def kfn(kt, hh=hh):
                                return [(kT_g[:, hh, 128 * kt:128 * kt + 128], r_kTg[kt // 4]),
                                        (kr_T[0:128, 128 * kt:128 * kt + 128], sblocks[kt // 4][5])]

                            def vfn(kt, hh=hh):
                                return (V_g[:, kt, 128 * hh:128 * hh + 128], r_Vg[kt // 4])

                            def post(kt, pt, rpt, c0, c1, nt0=nt0):
                                if kt >= nt0:
                                    j = kt - nt0
                                    self.memset(pt[64:128, 128 * j:128 * j + 64], 0.0, [rpt], [rpt])

                            k = self.cnt
                            self.cnt += 1
                            og_, rog = ogs[k % 2], r_ogs[k % 2]
                            def og_store(h=h, og_=og_, rog=rog, og_off=og_off, ntok=ntok):
                                self.dma(ogT_d[128 * h:128 * h + 128, og_off:og_off + ntok], og_[:, 0:ntok], [rog], [])
                            attention(ntok, tl, [(qTn[par][:, hh, :], r_qTn[par]), (qTr[par][0:128, hh, :], r_qTr[par])],
                                      kfn, vfn, B_SCALE, None, post, sg_[:, hh, 0:ntok], rsg, og_[:, 0:ntok], rog, abufs, og_store)
                        flush_epi()

                try:
                    for G in range(4):
                        self.dma(wuk[:], w_b_uk[:, 512 * G:512 * G + 512].rearrange("(kt p) c -> p kt c", p=128), [], [r_w], q="pool")
                        self.dma(wuv[:], w_b_uv[:, 512 * G:512 * G + 512].rearrange("(kt p) c -> p kt c", p=128), [], [r_w], q="pool")
                        for hh in range(4):
                            c0 = (4 * G + hh) * 192
                            self.dma(wuqn[:, :, 128 * hh:128 * hh + 128], w_b_uq[:, c0:c0 + 128].rearrange("(kt p) c -> p kt c", p=128),
                                     [], [r_w], q="pool")
                            self.dma(wuqr[:, :, 64 * hh:64 * hh + 64], w_b_uq[:, c0 + 128:c0 + 192].rearrange("(kt p) c -> p kt c", p=128),
                                     [], [r_w], q="pool")
                        sbl = [(512 * s, 4, True, r_cqT[s], r_ckvT[s], r_krT[s], 4 * s) for s in range(NTP // 4)]
                        b2_seq(G, cqT_p, ckvT_p, krT_p, sbl, 0)
                        self.stopat("b2prompt")
                        sbs = [(0, 4, False, None, r_ckvT[SP0], r_krT[SP0], 0), (512, 4, False, None, r_ckvT[SP0 + 1], r_krT[SP0 + 1], 0),
                               (1024, 1, True, r_cqT[SP0], r_ckvT[SP0 + 2], r_krT[SP0 + 2], NTP)]
                        b2_seq(G, cqT_s, ckvT_s, krT_s, sbs, T - 1024)
                        self.stopat("b2g0")
                except StopBuild:
                    pass
                S.flush()

        if ph == "B2":
            self._finish(S)
            return nc

        with ExitStack() as st:
            aT = self.sb(st, "aT", [128, 16, 1024], BF16)
            r_aT = Res()
            Wc = [self.sb(st, f"wc{i}", [128, 16, 512], BF16) for i in range(2)]
            ws = WStream(Wc, [Res(), Res()])
            ytile = [self.sb(st, f"ytile{i}", [128, 512], F32) for i in range(3)]
            xre = [self.sb(st, f"xre{i}", [128, 512], F32) for i in range(3)]
            obufs = (ytile[0:2], [Res(), Res()], xre[0:2], [Res(), Res()])
            for g0 in range(0, NTP, 8):
                nbg = min(8, NTP - g0)
                self.dma(aT[:, :, 0:128 * nbg], ogT_d[:, 128 * g0:128 * (g0 + nbg)].rearrange("(kt p) t -> p kt t", p=128), [], [r_aT])
                out_proj(ws, "b_out", aT, r_aT, nbg, y1[128 * g0:128 * (g0 + nbg), :], y_p[128 * g0:128 * (g0 + nbg), :], 128, obufs,
                         store_q="act")
            self.dma(aT[:, :, 0:128], ogT_d[:, T:T + 128].rearrange("(kt p) t -> p kt t", p=128), [], [r_aT])
            out_proj(ws, "b_out", aT, r_aT, 1, y1[T:T + NS, :], y_s, NS, obufs, store_q="act")
            S.flush()
        self._finish(S)
        return nc


    def _finish(self, S):
        self.stats = dict(ops=S.tot_ops, waits=S.tot_waits, incs=dict(S.incc), n_dma=S.n_dma)


_CACHE = {}


def _rope_table():
    half = 32
    inv = (np.float32(10000.0) ** (-np.arange(half, dtype=np.float32) / np.float32(half))).astype(np.float32)
    pos = np.concatenate([np.arange(T), PAST + np.arange(128)]).astype(np.float32)
    ang = (pos[:, None] * inv[None, :]).astype(np.float32)
    return np.concatenate([np.cos(ang), np.sin(ang)], axis=1).astype(np.float32)


def _build(phases="all"):
    if phases not in _CACHE:
        b = Builder(phases)
        nc = b.build()
        _CACHE[phases] = (nc, b)
    return _CACHE[phases]


def kernel(**inputs):
    phases = inputs.pop("_phases", "all")
    inputs = dict(inputs)
    nc, b = _build(phases)
    f = lambda a: np.ascontiguousarray(np.asarray(a, dtype=np.float32))
    rope = _rope_table()
    shared = {
        "a_ln": f(inputs["a_ln"][0]), "w_a_in": f(inputs["w_a_in"][0]), "a_q_norm": f(inputs["a_q_norm"][0]),
        "a_k_norm": f(inputs["a_k_norm"][0]), "a_rel_bias": f(inputs["a_rel_bias"][0]), "w_a_out": f(inputs["w_a_out"][0]),
        "b_ln": f(inputs["b_ln"][0]), "w_b_in": f(inputs["w_b_in"][0]), "b_q_a_norm": f(inputs["b_q_a_norm"][0]),
        "w_b_uq": f(inputs["w_b_uq"][0]), "b_kv_a_norm": f(inputs["b_kv_a_norm"][0]), "w_b_uk": f(inputs["w_b_uk"][0]),
        "w_b_uv": f(inputs["w_b_uv"][0]), "b_q_nope_norm": f(inputs["b_q_nope_norm"][0]),
        "b_k_nope_norm": f(inputs["b_k_nope_norm"][0]), "b_q_rope_norm": f(inputs["b_q_rope_norm"][0]),
        "b_k_rope_norm": f(inputs["b_k_rope_norm"][0]), "w_b_out": f(inputs["w_b_out"][0]), "rope_cs": rope,
    }
    in_maps = []
    for c in range(N_CORES):
        m = dict(shared)
        m["x_prompt"] = f(inputs["x_prompt"][c][:T])
        m["x_sample"] = f(inputs["x_sample"][c])
        m["cache_a_k"] = f(inputs["cache_a_k"][0, c]).reshape(512, D)
        m["cache_a_v"] = f(inputs["cache_a_v"][0, c]).reshape(512, D)
        m["cache_b_ckv"] = f(inputs["cache_b_ckv"][0, c])
        m["cache_b_krope"] = f(inputs["cache_b_krope"][0, c])
        in_maps.append(m)
    ncr = int(inputs.pop("_ncores", N_CORES))
    res = run_bass_kernel_spmd(nc, in_maps[:ncr], core_ids=list(range(ncr)))
    R = list(res.results) + [res.results[0]] * (N_CORES - ncr)
    st = lambda k: np.stack([R[c][k] for c in range(N_CORES)])
    y_p = st("y_prompt")
    y_s = st("y_sample")
    akp = st("new_a_k_prompt").reshape(1, N_CORES, 512, 16, 128)
    avp = st("new_a_v_prompt").reshape(1, N_CORES, 512, 16, 128)
    bcp = st("new_b_ckv_prompt")[None]
    brp = st("new_b_krope_prompt")[None]
    aks = st("new_a_k_sample").reshape(1, N_CORES, NS, 16, 128)
    avs = st("new_a_v_sample").reshape(1, N_CORES, NS, 16, 128)
    bcs = st("new_b_ckv_sample")[None]
    brs = st("new_b_krope_sample")[None]
    return (y_p, y_s, akp, avp, bcp, brp, aks, avs, bcs, brs)
```
